# Optimizing a Trainium2 kernel written in Bass

```python
import math
import jax, jax.numpy as jnp
from jax import lax
import numpy as np

D_MODEL = 1024
BATCH = 8
SEQ = 4096
DEPTH = 4
DEC_BATCH = 4
DEC_SEQ = 4096
PAST_LEN = 128

F32 = jnp.float32
EPS = 1e-6
F_MIN_GAP = 1e-6
H_A = 4
DK_A = 128
DV_A = 128
W_A = H_A * DV_A
QKV_A = 2 * H_A * DK_A + W_A
CONV_K = 5
CHUNK_A = 64
H_B = 4
DK_B = 128
DV_B = 128
W_B = H_B * DV_B
CHUNK_B = 16
H_C = 4
D_C = 64
DV_C = 2 * D_C
W_C = H_C * DV_C
QK_C = H_C * 2 * D_C
ROPE_THETA = 500000.0
ROPE_DIM = D_C // 4
Q_BLOCK = 128
N_BRANCH = 3
SPLIT_SIZES = (QKV_A, W_A, 2 * H_A, 2 * H_A,
               H_B * DK_B, W_B, 2 * H_B * DK_B, W_B,
               QK_C, QK_C, W_C, W_C,
               N_BRANCH * D_MODEL)
N_IN = sum(SPLIT_SIZES)

kernel_name = 'hybrid_gdn_hgrn2_diffattn_encoder'


def split_cols(t, sizes):
    out, start = [], 0
    for s in sizes:
        out.append(t[..., start:start + s])
        start += s
    return out


def rms_norm(x, g):
    xf = x.astype(F32)
    y = xf * lax.rsqrt(jnp.mean(xf * xf, axis=-1, keepdims=True) + EPS)
    return (y * g.astype(F32)).astype(x.dtype)


def l2_norm(x):
    xf = x.astype(F32)
    return xf * lax.rsqrt(jnp.sum(xf * xf, axis=-1, keepdims=True) + EPS)


def flip(t):
    return jnp.flip(t, axis=1)


def to_chunks(t, c):
    b, s, h, d = t.shape
    return t.reshape(b, s // c, c, h, d).transpose(1, 0, 3, 2, 4)


def from_chunks(t):
    n, b, h, c, d = t.shape
    return t.transpose(1, 0, 3, 2, 4).reshape(b, n * c, h, d)


def masked_exp(diff, mask):
    return jnp.where(mask, jnp.exp(jnp.where(mask, diff, 0.0)), 0.0)


def short_conv_silu(x, w):
    y = lax.conv_general_dilated(x, w[:, None, :].astype(x.dtype), window_strides=(1,),
                                 padding=[((CONV_K - 1) // 2, CONV_K // 2)],
                                 dimension_numbers=('NWC', 'WIO', 'NWC'),
                                 feature_group_count=x.shape[-1])
    return jax.nn.silu(y)


def gated_delta_chunked(q, k, v, beta, g):
    bsz, _, nh, dk = k.shape
    dv = v.shape[-1]
    q, k, v = (to_chunks(t.astype(F32), CHUNK_A) for t in (q, k, v))
    beta = to_chunks(beta.astype(F32)[..., None], CHUNK_A)[..., 0]
    gc = jnp.cumsum(to_chunks(g.astype(F32)[..., None], CHUNK_A)[..., 0], axis=-1)
    incl = jnp.tril(jnp.ones((CHUNK_A, CHUNK_A), bool))
    strict = jnp.tril(jnp.ones((CHUNK_A, CHUNK_A), bool), -1)
    decay = masked_exp(gc[..., :, None] - gc[..., None, :], incl)
    kb = k * beta[..., None]
    m = jnp.where(strict, jnp.einsum('nbhid,nbhjd->nbhij', kb, k) * decay, 0.0)
    lhs = m + jnp.eye(CHUNK_A, dtype=F32)
    rhs = jnp.concatenate([v * beta[..., None], kb * jnp.exp(gc)[..., None]], axis=-1)
    sol = lax.linalg.triangular_solve(lhs, rhs, left_side=True, lower=True)
    u, w = sol[..., :dv], sol[..., dv:]
    qs = q * (dk ** -0.5)
    a_qk = jnp.einsum('nbhid,nbhjd->nbhij', qs, k) * decay
    q_dec = qs * jnp.exp(gc)[..., None]
    g_last = gc[..., -1]
    k_dec = k * jnp.exp(g_last[..., None] - gc)[..., None]

    def step(s, inp):
        u_c, w_c, qd_c, a_c, kd_c, gl_c = inp
        v_new = u_c - jnp.einsum('bhck,bhkv->bhcv', w_c, s)
        o = jnp.einsum('bhck,bhkv->bhcv', qd_c, s) + jnp.einsum('bhij,bhjv->bhiv', a_c, v_new)
        s = s * jnp.exp(gl_c)[..., None, None] + jnp.einsum('bhck,bhcv->bhkv', kd_c, v_new)
        return s, o

    s0 = jnp.zeros((bsz, nh, dk, dv), F32)
    _, o = lax.scan(step, s0, (u, w, q_dec, a_qk, k_dec, g_last))
    return from_chunks(o)


def hgrn2_chunked(q, k, v, log_f):
    bsz, _, nh, dk = q.shape
    dv = v.shape[-1]
    q, k, v, log_f = (to_chunks(t.astype(F32), CHUNK_B) for t in (q, k, v, log_f))
    gc = jnp.cumsum(log_f, axis=-2)
    qs = q * (dk ** -0.5)
    q_dec = qs * jnp.exp(gc)
    g_last = gc[..., -1, :]
    k_dec = k * jnp.exp(g_last[..., None, :] - gc)
    incl = jnp.tril(jnp.ones((CHUNK_B, CHUNK_B), bool))[:, :, None]

    def step(s, inp):
        qs_c, qd_c, k_c, kd_c, v_c, gc_c, gl_c = inp
        pair = masked_exp(gc_c[:, :, :, None, :] - gc_c[:, :, None, :, :], incl)
        a = jnp.einsum('bhik,bhjk,bhijk->bhij', qs_c, k_c, pair)
        o = jnp.einsum('bhck,bhkv->bhcv', qd_c, s) + jnp.einsum('bhij,bhjv->bhiv', a, v_c)
        s = s * jnp.exp(gl_c)[..., None] + jnp.einsum('bhck,bhcv->bhkv', kd_c, v_c)
        return s, o

    s0 = jnp.zeros((bsz, nh, dk, dv), F32)
    _, o = lax.scan(step, s0, (qs, q_dec, k, k_dec, v, gc, g_last))
    return from_chunks(o)


def rope_tables(seq):
    inv = 1.0 / (ROPE_THETA ** (jnp.arange(0, ROPE_DIM, 2, dtype=F32) / ROPE_DIM))
    ang = jnp.arange(seq, dtype=F32)[:, None] * inv[None, :]
    return jnp.cos(ang), jnp.sin(ang)


def partial_rope(x, cos, sin):
    half = ROPE_DIM // 2
    c = cos[None, :, None, None, :].astype(x.dtype)
    s = sin[None, :, None, None, :].astype(x.dtype)
    x1, x2 = x[..., :half], x[..., half:ROPE_DIM]
    return jnp.concatenate([x1 * c - x2 * s, x2 * c + x1 * s, x[..., ROPE_DIM:]], axis=-1)


def diff_attention(q, k, v, lam):
    bsz, seq, nh = q.shape[:3]
    nq = seq // Q_BLOCK
    qb = q.reshape(bsz, nq, Q_BLOCK, nh, 2, D_C).transpose(1, 0, 2, 3, 4, 5)

    def block(q_blk):
        s = jnp.einsum('bqhmd,bkhmd->bhmqk', q_blk, k, preferred_element_type=F32)
        p = jax.nn.softmax(s, axis=-1)
        wgt = p[:, :, 0] - lam * p[:, :, 1]
        return jnp.einsum('bhqk,bkhe->bqhe', wgt.astype(v.dtype), v)

    o = lax.map(block, qb)
    return o.transpose(1, 0, 2, 3, 4).reshape(bsz, seq, nh, DV_C)


def hgrn_lower_bounds(lb_logits):
    p = jax.nn.softmax(lb_logits.astype(F32), axis=1)
    return jnp.cumsum(p, axis=1) - p[:, :1]


def layer(x, l, lbs, cos, sin, norm_g, w_in, conv_w, a_log, dt_bias, gdn_norm_g, hgrn_norm_g,
          q_norm_g, k_norm_g, lam_p, subln_g, w_br_a, w_br_b, w_br_c, w_out):
    dt = x.dtype
    bsz, seq, _ = x.shape
    h = rms_norm(x, norm_g)
    proj = h @ w_in
    (a_qkv, a_z, a_b, a_a, b_q, b_i, b_f, b_z, c_q, c_k, c_v, c_z, gate_logits) = split_cols(proj, SPLIT_SIZES)

    a_qkv = short_conv_silu(a_qkv, conv_w)
    aq, ak, av = split_cols(a_qkv, (H_A * DK_A, H_A * DK_A, W_A))
    aq = l2_norm(aq.reshape(bsz, seq, H_A, DK_A))
    ak = l2_norm(ak.reshape(bsz, seq, H_A, DK_A))
    av = av.reshape(bsz, seq, H_A, DV_A)
    beta = jax.nn.sigmoid(a_b.astype(F32)).reshape(bsz, seq, 2, H_A)
    ga = -jnp.exp(a_log.astype(F32)) * jax.nn.softplus(a_a.astype(F32).reshape(bsz, seq, 2, H_A) + dt_bias.astype(F32))
    o_a = (gated_delta_chunked(aq, ak, av, beta[:, :, 0], ga[:, :, 0])
           + flip(gated_delta_chunked(flip(aq), flip(ak), flip(av), flip(beta[:, :, 1]), flip(ga[:, :, 1]))))
    y_a = (rms_norm(o_a, gdn_norm_g).reshape(bsz, seq, W_A) * jax.nn.silu(a_z.astype(F32))).astype(dt)

    bq = jax.nn.silu(b_q).reshape(bsz, seq, H_B, DK_B)
    bi = b_i.reshape(bsz, seq, H_B, DV_B)
    bf = b_f.astype(F32).reshape(bsz, seq, 2, H_B, DK_B)
    lb = lbs[:, l].reshape(2, H_B, DK_B)
    kk = (1.0 - lb) * jax.nn.sigmoid(-bf)
    log_f = jnp.log1p(-jnp.minimum(kk, 1.0 - F_MIN_GAP))
    o_b = (hgrn2_chunked(bq, kk[:, :, 0], bi, log_f[:, :, 0])
           + flip(hgrn2_chunked(flip(bq), flip(kk[:, :, 1]), flip(bi), flip(log_f[:, :, 1]))))
    y_b = (rms_norm(o_b, hgrn_norm_g).reshape(bsz, seq, W_B) * jax.nn.silu(b_z.astype(F32))).astype(dt)

    lambda_init = 0.8 - 0.6 * math.exp(-0.3 * l)
    lp = lam_p.astype(F32)
    lam = jnp.exp(jnp.sum(lp[0] * lp[1])) - jnp.exp(jnp.sum(lp[2] * lp[3])) + lambda_init
    cq = partial_rope(rms_norm(c_q.reshape(bsz, seq, H_C, 2, D_C), q_norm_g), cos, sin) * (D_C ** -0.5)
    ck = partial_rope(rms_norm(c_k.reshape(bsz, seq, H_C, 2, D_C), k_norm_g), cos, sin)
    cv = c_v.reshape(bsz, seq, H_C, DV_C)
    o_c = diff_attention(cq, ck, cv, lam)
    y_c = (rms_norm(o_c, subln_g).astype(F32).reshape(bsz, seq, W_C) * (1.0 - lambda_init)
           * jax.nn.silu(c_z.astype(F32))).astype(dt)

    gates = jax.nn.sigmoid(gate_logits.reshape(bsz, seq, N_BRANCH, D_MODEL))
    merged = gates[:, :, 0] * (y_a @ w_br_a) + gates[:, :, 1] * (y_b @ w_br_b) + gates[:, :, 2] * (y_c @ w_br_c)
    return (x + merged @ w_out).astype(dt)


def trunk(x, norm_g, w_in, conv_w, a_log, dt_bias, gdn_norm_g, hgrn_lb_logits, hgrn_norm_g,
          q_norm_g, k_norm_g, diff_lambda, subln_g, w_br_a, w_br_b, w_br_c, w_out):
    cos, sin = rope_tables(x.shape[1])
    lbs = hgrn_lower_bounds(hgrn_lb_logits)
    for l in range(DEPTH):
        x = layer(x, l, lbs, cos, sin, norm_g[l], w_in[l], conv_w[l], a_log[l], dt_bias[l], gdn_norm_g[l],
                  hgrn_norm_g[l], q_norm_g[l], k_norm_g[l], diff_lambda[l], subln_g[l],
                  w_br_a[l], w_br_b[l], w_br_c[l], w_out[l])
    return x


def setup_inputs(seed: int = 0) -> dict:
    key = jax.random.key(seed)
    ks = jax.random.split(key, 20)

    def nrm(k, shape, scale):
        return scale * jax.random.normal(k, shape, F32)

    x_prompt = nrm(ks[0], (BATCH, SEQ, D_MODEL), 1.0)
    x_sample = nrm(ks[1], (DEC_BATCH, DEC_SEQ, D_MODEL), 1.0)
    norm_g = 1.0 + nrm(ks[2], (DEPTH, D_MODEL), 0.02)
    w_in = nrm(ks[3], (DEPTH, D_MODEL, N_IN), D_MODEL ** -0.5)
    conv_w = nrm(ks[4], (DEPTH, CONV_K, QKV_A), CONV_K ** -0.5)
    a_log = jnp.log(jax.random.uniform(ks[5], (DEPTH, 2, H_A), F32, 1.0, 16.0))
    dt0 = jnp.exp(jax.random.uniform(ks[6], (DEPTH, 2, H_A), F32, math.log(1e-3), math.log(1e-1)))
    dt_bias = dt0 + jnp.log(-jnp.expm1(-dt0))
    gdn_norm_g = 1.0 + nrm(ks[7], (DEPTH, DV_A), 0.02)
    hgrn_lb_logits = nrm(ks[8], (2, DEPTH, W_B), 0.1)
    hgrn_norm_g = 1.0 + nrm(ks[9], (DEPTH, DV_B), 0.02)
    q_norm_g = 1.0 + nrm(ks[10], (DEPTH, 2, D_C), 0.02)
    k_norm_g = 1.0 + nrm(ks[11], (DEPTH, 2, D_C), 0.02)
    diff_lambda = nrm(ks[12], (DEPTH, 4, D_C), 0.1)
    subln_g = 1.0 + nrm(ks[13], (DEPTH, DV_C), 0.02)
    w_br_a = nrm(ks[14], (DEPTH, W_A, D_MODEL), W_A ** -0.5)
    w_br_b = nrm(ks[15], (DEPTH, W_B, D_MODEL), W_B ** -0.5)
    w_br_c = nrm(ks[16], (DEPTH, W_C, D_MODEL), W_C ** -0.5)
    w_out = nrm(ks[17], (DEPTH, D_MODEL, D_MODEL), D_MODEL ** -0.5)
    return {'x_prompt': x_prompt, 'x_sample': x_sample, 'norm_g': norm_g, 'w_in': w_in, 'conv_w': conv_w,
            'a_log': a_log, 'dt_bias': dt_bias, 'gdn_norm_g': gdn_norm_g, 'hgrn_lb_logits': hgrn_lb_logits,
            'hgrn_norm_g': hgrn_norm_g, 'q_norm_g': q_norm_g, 'k_norm_g': k_norm_g, 'diff_lambda': diff_lambda,
            'subln_g': subln_g, 'w_br_a': w_br_a, 'w_br_b': w_br_b, 'w_br_c': w_br_c, 'w_out': w_out}


def reference(x_prompt, x_sample, norm_g, w_in, conv_w, a_log, dt_bias, gdn_norm_g, hgrn_lb_logits, hgrn_norm_g,
              q_norm_g, k_norm_g, diff_lambda, subln_g, w_br_a, w_br_b, w_br_c, w_out):
    y_prompt = trunk(x_prompt, norm_g, w_in, conv_w, a_log, dt_bias, gdn_norm_g, hgrn_lb_logits, hgrn_norm_g,
                     q_norm_g, k_norm_g, diff_lambda, subln_g, w_br_a, w_br_b, w_br_c, w_out)
    y_sample = trunk(x_sample, norm_g, w_in, conv_w, a_log, dt_bias, gdn_norm_g, hgrn_lb_logits, hgrn_norm_g,
                     q_norm_g, k_norm_g, diff_lambda, subln_g, w_br_a, w_br_b, w_br_c, w_out)
    return (y_prompt, y_sample)
```

```python
import math
from contextlib import ExitStack

import numpy as np
import concourse.bass as bass
import concourse.mybir as mybir
from concourse.bass_utils import run_bass_kernel_spmd

F32 = mybir.dt.float32
BF16 = mybir.dt.bfloat16
AF = mybir.ActivationFunctionType
ALU = mybir.AluOpType
AX = mybir.AxisListType

D = 1024
NIN = 9744
KC = 8
EPS = 1e-6
OFF = dict(a_q=0, a_k=512, a_v=1024, a_z=1536, a_b=2048, a_a=2056, b_q=2064, b_i=2576, b_f=3088,
           b_z=4112, c_q=4624, c_k=5136, c_v=5648, c_z=6160, gate=6672)
NEG = -30000.0
ROPE_THETA = 500000.0


class Res:
    __slots__ = ("w", "rd")

    def __init__(self):
        self.w = None
        self.rd = {}


class DS:
    def __init__(self, sem, name):
        self.sem = sem
        self.cnt = 0
        self.name = name


class KB:
    def __init__(self, nc, es):
        self.nc = nc
        self.es = es
        self.eng = {"pe": nc.tensor, "dve": nc.vector, "act": nc.scalar, "pool": nc.gpsimd, "sp": nc.sync}
        self.sem = {k: es.enter_context(nc.semaphore("s_" + k)) for k in ("pe", "dve", "act", "pool")}
        self.cnt = {k: 0 for k in self.sem}
        self.waited = {k: {} for k in self.eng}
        self.dss = []
        self.nins = 0

    def ds(self, name):
        d = DS(self.es.enter_context(self.nc.semaphore(name)), name)
        self.dss.append(d)
        return d

    def _need(self, e, toks):
        wd = self.waited[e]
        for (key, sem, val) in toks:
            if e == "pe" and key == "pe":
                continue
            if wd.get(key, 0) >= val:
                continue
            self.eng[e].wait_ge(sem, val)
            wd[key] = val

    @staticmethod
    def _deps(rd, wr):
        toks = []
        for r in rd:
            if r.w is not None:
                toks.append(r.w)
        for w in wr:
            if w.w is not None:
                toks.append(w.w)
            toks.extend(w.rd.values())
        return toks

    @staticmethod
    def _mark(tok, rd, wr):
        for r in rd:
            r.rd[tok[0]] = tok
        for w in wr:
            w.w = tok
            w.rd = {}

    def op(self, e, fn, rd=(), wr=()):
        self._need(e, self._deps(rd, wr))
        ins = fn(self.eng[e])
        self.cnt[e] += 1
        ins.then_inc(self.sem[e], 1)
        self._mark((e, self.sem[e], self.cnt[e]), rd, wr)
        self.nins += 1

    def pe(self, fns, rd=(), wr=()):
        self._need("pe", self._deps(rd, wr))
        ins = None
        for f in fns:
            ins = f(self.nc.tensor)
            self.nins += 1
        self.cnt["pe"] += 1
        ins.then_inc(self.sem["pe"], 1)
        self._mark(("pe", self.sem["pe"], self.cnt["pe"]), rd, wr)

    def dma(self, out, in_, ds, rd=(), wr=(), q="sp", **kw):
        self._need(q, self._deps(rd, wr))
        ins = self.eng[q].dma_start(out=out, in_=in_, **kw)
        ds.cnt += 16
        ins.then_inc(ds.sem, 16)
        self._mark((ds.name, ds.sem, ds.cnt), rd, wr)
        self.nins += 1

    def barrier(self):
        toks = [(k, self.sem[k], self.cnt[k]) for k in self.sem if self.cnt[k] > 0]
        toks += [(d.name, d.sem, d.cnt) for d in self.dss if d.cnt > 0]
        for e in self.eng:
            self._need(e, toks)


def host_consts(S):
    NT = S // 128
    j = np.arange(128)[:, None]
    i = np.arange(128)[None, :]
    c = {}
    c["identf"] = np.eye(128, dtype=np.float32)
    c["onesf"] = np.ones((128, 128), np.float32)
    c["uincl"] = (j <= i).astype(np.float32)
    c["uinclT"] = (j >= i).astype(np.float32)
    c["negJI_f"] = np.where(i >= j, 0.0, NEG).astype(np.float32)
    c["negJI_b"] = np.where(i <= j, 0.0, NEG).astype(np.float32)
    c["negIJ_f"] = np.where(j > i, 0.0, NEG).astype(np.float32).T.copy()
    c["negIJ_b"] = np.where(j < i, 0.0, NEG).astype(np.float32).T.copy()
    pi = np.arange(128)[:, None]
    fj = np.arange(128)[None, :]
    c["negIJ_f"] = np.where(pi > fj, 0.0, NEG).astype(np.float32)
    c["negIJ_b"] = np.where(pi < fj, 0.0, NEG).astype(np.float32)
    same = (j // 32) == (i // 32)
    c["bd_f"] = (same & (i >= j)).astype(np.float32)
    c["bd_b"] = (same & (i <= j)).astype(np.float32)
    rm4 = np.zeros((128, 128), np.float32)
    for q_ in range(4):
        rm4[q_ * 32:(q_ + 1) * 32, q_] = 1.0
    c["rm4"] = rm4
    cf = np.concatenate([c[k] for k in CF_NAMES], axis=1)
    inv = 1.0 / (ROPE_THETA ** (np.arange(0, 16, 2, dtype=np.float32) / 16.0))
    pos = np.arange(S, dtype=np.float32)
    ang = (pos[:, None] * inv[None, :]).astype(np.float32)
    cs = np.cos(ang).astype(np.float32).reshape(NT, 128, 8).transpose(1, 0, 2)
    sn = np.sin(ang).astype(np.float32).reshape(NT, 128, 8).transpose(1, 0, 2)
    rope = np.ascontiguousarray(np.stack([cs, sn], axis=2)).reshape(128, NT * 2 * 8)
    sm = np.ones((128, S), np.float32)
    sm[:, ::32] = 0.0
    return np.ascontiguousarray(cf), np.ascontiguousarray(rope), sm


CF_NAMES = ("identf", "onesf", "uincl", "uinclT", "negJI_f", "negJI_b", "negIJ_f", "negIJ_b", "bd_f", "bd_b", "rm4")


def build(S, NSEQ, DEPTH, dbg=False, skip=()):
    NT = S // 128
    NB = S // 512
    NC32 = S // 32
    nc = bass.Bass("TRN2", target_bir_lowering=False)

    def din(name, shape, dt=F32):
        return nc.dram_tensor(name, list(shape), dt, kind="ExternalInput").ap()

    xin = din("xin", [NSEQ, S, D])
    norm_g = din("norm_g", [DEPTH, D])
    w_in = din("w_in", [DEPTH, D, NIN])
    conv_w = din("conv_w", [DEPTH, 5, 1536])
    a_log = din("a_log", [DEPTH, 8])
    dt_bias = din("dt_bias", [DEPTH, 8])
    gdn_g = din("gdn_norm_g", [DEPTH, 128])
    lb_logits = din("hgrn_lb_logits", [2, DEPTH, 512])
    hgrn_g = din("hgrn_norm_g", [DEPTH, 128])
    qn_g = din("q_norm_g", [DEPTH, 128])
    kn_g = din("k_norm_g", [DEPTH, 128])
    dlam = din("diff_lambda", [DEPTH, 256])
    subln_g = din("subln_g", [DEPTH, 128])
    w_br = [din("w_br_a", [DEPTH, 512, D]), din("w_br_b", [DEPTH, 512, D]), din("w_br_c", [DEPTH, 512, D])]
    w_out = din("w_out", [DEPTH, D, D])
    cf_d = din("cst_f", [128, 128 * len(CF_NAMES)])
    rope_d = din("cst_rope", [128, NT * 16])
    smask_d = din("cst_smask", [128, S])
    yout = nc.dram_tensor("yout", [NSEQ, S, D], F32, kind="ExternalOutput").ap()
    yT_d = nc.dram_tensor("yT_scr", [3, 512, S], BF16, kind="Internal").ap()
    gate_d = nc.dram_tensor("gate_scr", [S, 3 * D], BF16, kind="Internal").ap()
    xres_d = nc.dram_tensor("xres_scr", [S, D], F32, kind="Internal").ap()
    dbg_d = None
    if dbg:
        dbg_d = nc.dram_tensor("dbg_yT", [3, 512, S], BF16, kind="ExternalOutput").ap()

    es = ExitStack()
    with es:
        kb = KB(nc, es)

        uniq = [0]

        def sb(name, shape, dt, stack=es):
            uniq[0] += 1
            return stack.enter_context(nc.sbuf_tensor("%s_%d" % (name, uniq[0]), list(shape), dt))

        import contextlib

        @contextlib.contextmanager
        def phase():
            with ExitStack() as st_:
                yield st_
                kb.barrier()

        dumped = set()

        dcur = {"l": 0, "dl": int(dbg) - 1}

        def dump(name, ap, r, dt):
            if not dbg or name in dumped or dcur["l"] != dcur["dl"]:
                return
            dumped.add(name)
            shp = list(ap.shape)
            dd = nc.dram_tensor("dbg_" + name, shp, dt, kind="ExternalOutput").ap()
            kb.dma(dd, ap, st_ds, rd=[r], wr=[Res()])

        cf = sb("cf", [128, len(CF_NAMES), 128], F32)
        C = {n: cf[:, k, :] for k, n in enumerate(CF_NAMES)}
        identb = sb("identb", [128, 128], BF16)
        onesb = sb("onesb", [128, 128], BF16)
        epsc = sb("epsc", [128, 1], F32)
        hT = sb("hT", [128, KC, S], BF16)
        hT_r = Res()
        stg = sb("stg", [128, KC, 512], F32)
        stg_r = Res()
        wbf = [sb("wbf%d" % k, [128, KC, 512], BF16) for k in range(2)]
        wbf_r = [Res(), Res()]
        ld_ds = kb.ds("ld")
        c_r = Res()
        c_ds = kb.ds("cst")
        st_ds = kb.ds("st")
        st2_ds = [kb.ds("st2_0"), kb.ds("st2_1")]
        c2_ds = kb.ds("cst2")
        xl_ds = [kb.ds("xl0"), kb.ds("xl1")]
        gcol = sb("gcol", [128, KC, 1], F32)
        convw = sb("convw", [128, 12, 5], F32)
        gatec = sb("gatec", [128, 16], F32)
        vecs = sb("vecs", [128, 8], F32)
        qkg = sb("qkg", [128, 2, 128], F32)
        lamt = sb("lamt", [128, 256], F32)
        lbt = sb("lbt", [128, 2, DEPTH, 4], F32)
        oml = sb("oml", [128, DEPTH, 2, 4], F32)
        par_r = Res()
        smallt = sb("smallt", [128, 64], F32)
        small_r = Res()

        pbig = es.enter_context(nc.psum_tensor("pbig", [128, 4096], F32))
        pb = [pbig[:, k * 512:(k + 1) * 512] for k in range(8)]
        pr = [Res() for _ in range(8)]

        def pbf(k):
            return pb[k][:].bitcast(BF16)

        wslot = [0]

        def load_w(src, kc, ncols, fold=False):
            s = wslot[0]
            wslot[0] ^= 1
            kb.dma(stg[:, 0:kc, 0:ncols], src.rearrange("(c p) n -> p c n", p=128), ld_ds, wr=[stg_r])
            dst = wbf[s][:, 0:kc, 0:ncols]
            if fold:
                kb.op("pool", lambda e: e.tensor_tensor(out=dst, in0=stg[:, 0:kc, 0:ncols],
                                                        in1=gcol[:, 0:kc, :].to_broadcast([128, kc, ncols]),
                                                        op=ALU.mult),
                      rd=[stg_r, par_r], wr=[wbf_r[s]])
            else:
                kb.op("pool", lambda e: e.tensor_copy(out=dst, in_=stg[:, 0:kc, 0:ncols]), rd=[stg_r], wr=[wbf_r[s]])
            return dst, wbf_r[s]

        def load_win(l, col0, ncols):
            return load_w(w_in[l, :, col0:col0 + ncols], KC, ncols, fold=True)

        def projF(out_ps, out_r, wt, wr_, c0, m, tok0, ntok):
            kb.pe([(lambda e, c=c: e.matmul(out_ps, lhsT=wt[:, c, c0:c0 + m], rhs=hT[:, c, tok0:tok0 + ntok],
                                            start=(c == 0), stop=(c == KC - 1))) for c in range(KC)],
                  rd=[wr_, hT_r], wr=[out_r])

        def projT(out_ps, out_r, wt, wr_, c0, n, tok0, ntok=128):
            kb.pe([(lambda e, c=c: e.matmul(out_ps, lhsT=hT[:, c, tok0:tok0 + ntok], rhs=wt[:, c, c0:c0 + n],
                                            start=(c == 0), stop=(c == KC - 1))) for c in range(KC)],
                  rd=[wr_, hT_r], wr=[out_r])

        def rstd_from(out_ap, in_ap, scale, rd, wr, tmp_ap):
            shp = list(in_ap.shape)
            bias = epsc[0:shp[0], 0:1]
            kb.op("act", lambda e: e.activation(out=tmp_ap, in_=in_ap, func=AF.Ln, scale=scale, bias=bias), rd=rd, wr=wr)
            kb.op("act", lambda e: e.activation(out=out_ap, in_=tmp_ap, func=AF.Exp, scale=-0.5), rd=wr, wr=wr)

        kb.dma(cf[:].rearrange("p k n -> p (k n)"), cf_d[:, :], c_ds, wr=[c_r])
        kb.op("pool", lambda e: e.tensor_copy(out=identb[:], in_=C["identf"]), rd=[c_r], wr=[c_r])
        kb.op("pool", lambda e: e.memset(onesb[:], 1.0), wr=[c_r])
        kb.op("pool", lambda e: e.memset(epsc[:], EPS), wr=[c_r])
        with nc.allow_non_contiguous_dma("tiny param loads"):
            for d_ in range(2):
                for l_ in range(DEPTH):
                    kb.dma(lbt[:, d_, l_, :], lb_logits[d_, l_].rearrange("(h k) -> k h", k=128), c2_ds, wr=[small_r])
        kb.op("act", lambda e: e.activation(out=lbt[:], in_=lbt[:], func=AF.Exp), rd=[small_r], wr=[small_r])
        den = smallt[:, 0:8].rearrange("p (d h) -> p d h", d=2)
        num = smallt[:, 8:16].rearrange("p (d h) -> p d h", d=2)
        kb.op("dve", lambda e: e.tensor_reduce(out=den, in_=lbt[:].rearrange("p d l h -> p d h l"), axis=AX.X, op=ALU.add),
              rd=[small_r], wr=[small_r])
        kb.op("dve", lambda e: e.reciprocal(out=den, in_=den), rd=[small_r], wr=[small_r])
        for l in range(DEPTH):
            if l == 0:
                kb.op("dve", lambda e: e.memset(oml[:, 0, :, :], 1.0), wr=[small_r])
            else:
                kb.op("dve", lambda e, l=l: e.tensor_reduce(out=num, in_=lbt[:, :, 1:l + 1, :].rearrange("p d l h -> p d h l"),
                                                            axis=AX.X, op=ALU.add), rd=[small_r], wr=[small_r])
                kb.op("dve", lambda e: e.tensor_tensor(out=num, in0=num, in1=den, op=ALU.mult), rd=[small_r], wr=[small_r])
                kb.op("dve", lambda e, l=l: e.tensor_scalar(out=oml[:, l, :, :], in0=num, scalar1=-1.0, scalar2=1.0,
                                                            op0=ALU.mult, op1=ALU.add), rd=[small_r], wr=[small_r])
        kb.barrier()

        def epilogue(oT_blk, o_r, gidx, zcol0, l, dst_rows, b, tmp, tmp_r, wz, wz_r):
            sq, rs, zg, yb = tmp["sq"], tmp["rs"], tmp["zg"], tmp["yb"]
            kb.op("act", lambda e: e.activation(out=sq[:], in_=oT_blk, func=AF.Square), rd=[o_r], wr=[tmp_r])
            kb.pe([lambda e: e.matmul(pb[6][:], lhsT=onesb[:], rhs=sq[:], start=True, stop=True)], rd=[tmp_r, c_r], wr=[pr[6]])
            kb.op("act", lambda e: e.activation(out=rs[:], in_=pb[6][:], func=AF.Ln, scale=1.0 / 128, bias=epsc[:, 0:1]),
                  rd=[pr[6], c_r], wr=[tmp_r])
            kb.op("act", lambda e: e.activation(out=rs[:], in_=rs[:], func=AF.Exp, scale=-0.5), rd=[tmp_r], wr=[tmp_r])
            projF(pb[7][:], pr[7], wz, wz_r, 0, 128, b * 512, 512)
            kb.op("act", lambda e: e.activation(out=zg[:], in_=pb[7][:], func=AF.Silu), rd=[pr[7]], wr=[tmp_r])
            kb.op("dve", lambda e: e.tensor_tensor(out=rs[:], in0=rs[:], in1=oT_blk, op=ALU.mult), rd=[tmp_r, o_r], wr=[tmp_r])
            kb.op("dve", lambda e: e.scalar_tensor_tensor(out=yb[:], in0=rs[:], scalar=vecs[:, gidx:gidx + 1], in1=zg[:],
                                                          op0=ALU.mult, op1=ALU.mult), rd=[tmp_r, par_r], wr=[tmp_r])
            kb.dma(dst_rows[:, b * 512:(b + 1) * 512], yb[:], st_ds, rd=[tmp_r], wr=[yT_r])

        yT_r = Res()
        gate_r = Res()
        xr = [Res()]

        for s in range(NSEQ):
            for l in range(DEPTH):
                dcur["l"] = l
                xsrc = xin[s] if l == 0 else xres_d
                xdst = yout[s] if l == DEPTH - 1 else xres_d
                lam_init = 0.8 - 0.6 * math.exp(-0.3 * l)
                with nc.allow_non_contiguous_dma("tiny param loads"):
                    kb.dma(gcol[:], norm_g[l].rearrange("(c p o) -> p c o", p=128, o=1), c_ds, wr=[par_r])
                    for j in range(5):
                        kb.dma(convw[:, :, j], conv_w[l, j].rearrange("(g p) -> p g", p=128), c_ds, wr=[par_r])
                    kb.dma(vecs[:, 0:1], gdn_g[l].rearrange("(p o) -> p o", o=1), c_ds, wr=[par_r])
                    kb.dma(vecs[:, 1:2], hgrn_g[l].rearrange("(p o) -> p o", o=1), c_ds, wr=[par_r])
                    kb.dma(vecs[:, 2:3], subln_g[l].rearrange("(p o) -> p o", o=1), c_ds, wr=[par_r])
                kb.dma(gatec[:, 0:8], a_log[l].partition_broadcast(128), c_ds, wr=[par_r])
                kb.dma(gatec[:, 8:16], dt_bias[l].partition_broadcast(128), c_ds, wr=[par_r])
                kb.dma(qkg[:, 0, :], qn_g[l].partition_broadcast(128), c_ds, wr=[par_r])
                kb.dma(qkg[:, 1, :], kn_g[l].partition_broadcast(128), c_ds, wr=[par_r])
                kb.dma(lamt[:], dlam[l].partition_broadcast(128), c_ds, wr=[par_r])
                kb.op("act", lambda e: e.activation(out=gatec[:, 0:8], in_=gatec[:, 0:8], func=AF.Exp), rd=[par_r], wr=[par_r])
                kb.op("dve", lambda e: e.tensor_scalar(out=gatec[:, 0:8], in0=gatec[:, 0:8], scalar1=-1.0, scalar2=None,
                                                       op0=ALU.mult), rd=[par_r], wr=[par_r])
                kb.op("dve", lambda e: e.tensor_scalar(out=qkg[:, 0, :], in0=qkg[:, 0, :], scalar1=0.125, scalar2=None,
                                                       op0=ALU.mult), rd=[par_r], wr=[par_r])
                kb.op("dve", lambda e: e.tensor_scalar(out=vecs[:, 2:3], in0=vecs[:, 2:3], scalar1=1.0 - lam_init,
                                                       scalar2=None, op0=ALU.mult), rd=[par_r], wr=[par_r])
                l4 = lamt[:].rearrange("p (a b d) -> p a b d", a=2, b=2)
                pr2 = smallt[:, 16:18]
                kb.op("dve", lambda e: e.tensor_tensor(out=l4[:, :, 0, :], in0=l4[:, :, 0, :], in1=l4[:, :, 1, :], op=ALU.mult),
                      rd=[par_r], wr=[par_r])
                kb.op("dve", lambda e: e.tensor_reduce(out=pr2, in_=l4[:, :, 0, :], axis=AX.X, op=ALU.add),
                      rd=[par_r], wr=[small_r])
                kb.op("act", lambda e: e.activation(out=pr2, in_=pr2, func=AF.Exp), rd=[small_r], wr=[small_r])
                kb.op("dve", lambda e: e.tensor_tensor(out=smallt[:, 18:19], in0=smallt[:, 17:18], in1=smallt[:, 16:17],
                                                       op=ALU.subtract), rd=[small_r], wr=[small_r])
                kb.op("dve", lambda e: e.tensor_scalar(out=vecs[:, 3:4], in0=smallt[:, 18:19], scalar1=-lam_init, scalar2=None,
                                                       op0=ALU.add), rd=[small_r], wr=[par_r])

                with phase() as ph:
                    xt = [sb("xt%d" % k, [128, D], F32, ph) for k in range(2)]
                    xt_r = [Res(), Res()]
                    hb = [sb("hb%d" % k, [128, D], BF16, ph) for k in range(2)]
                    hb_r = [Res(), Res()]
                    junk = sb("junk", [128, D], BF16, ph)
                    st = sb("nst", [128, 2, 4], F32, ph)
                    st_r = [Res(), Res()]
                    for t in range(NT):
                        k = t % 2
                        kb.dma(xt[k][:], xsrc[t * 128:(t + 1) * 128, :], xl_ds[k], rd=[xr[0]], wr=[xt_r[k]])
                        kb.op("act", lambda e, k=k: e.activation(out=junk[:], in_=xt[k][:], func=AF.Square,
                                                                 accum_out=st[:, k, 0:1]), rd=[xt_r[k]], wr=[st_r[k]])
                        rstd_from(st[:, k, 1:2], st[:, k, 0:1], 1.0 / D, [st_r[k], c_r], [st_r[k]], st[:, k, 2:3])
                        kb.op("dve", lambda e, k=k: e.tensor_scalar(out=hb[k][:], in0=xt[k][:], scalar1=st[:, k, 1:2],
                                                                    scalar2=None, op0=ALU.mult),
                              rd=[xt_r[k], st_r[k]], wr=[hb_r[k]])
                        pT = pbf(k)
                        kb.pe([(lambda e, c=c, k=k, pT=pT: e.transpose(out=pT[:, c * 128:(c + 1) * 128],
                                                                       in_=hb[k][:, c * 128:(c + 1) * 128],
                                                                       identity=identb[:])) for c in range(KC)],
                              rd=[hb_r[k], c_r], wr=[pr[k]])
                        kb.op("act", lambda e, t=t, pT=pT: e.activation(out=hT[:, :, t * 128:(t + 1) * 128],
                                                                        in_=pT.rearrange("p (c n) -> p c n", c=KC),
                                                                        func=AF.Copy), rd=[pr[k]], wr=[hT_r])
                kb.barrier()

                dump("hT", hT[:].rearrange("p c n -> p (c n)"), hT_r, BF16)
                with phase() as ph:
                  if 'G' not in skip:
                    gsb = [sb("gsb%d" % k, [128, 512], BF16, ph) for k in range(2)]
                    gsb_r = [Res(), Res()]
                    it = 0
                    for cb in range(6):
                        wt, wt_r = load_win(l, OFF["gate"] + cb * 512, 512)
                        for t in range(NT):
                            k = it % 2
                            it += 1
                            projT(pb[k][:], pr[k], wt, wt_r, 0, 512, t * 128)
                            kb.op("act", lambda e, k=k: e.activation(out=gsb[k][:], in_=pb[k][:], func=AF.Sigmoid),
                                  rd=[pr[k]], wr=[gsb_r[k]])
                            kb.dma(gate_d[t * 128:(t + 1) * 128, cb * 512:(cb + 1) * 512], gsb[k][:], st2_ds[k],
                                   rd=[gsb_r[k]], wr=[gate_r])
                kb.barrier()

                with phase() as ph:
                  if 'C' not in skip:
                    rope_t = sb("rope_t", [128, NT, 2, 8], F32, ph)
                    rp_r = Res()
                    kb.dma(rope_t[:].rearrange("p t a b -> p (t a b)"), rope_d[:, :], c_ds, wr=[rp_r])
                    qT = sb("c_qT", [128, 4, S], BF16, ph)
                    kT = sb("c_kT", [128, 4, S], BF16, ph)
                    qk_r = Res()
                    for which, dstT in ((0, qT), (1, kT)):
                        with phase() as ph2:
                            sq = sb("c_sq", [128, 8, 64], F32, ph2)
                            qn = sb("c_qn", [128, 8, 64], F32, ph2)
                            ssq = sb("c_ssq", [128, 8, 3], F32, ph2)
                            rt = sb("c_rt", [128, 4, 8, 8], F32, ph2)
                            qb16 = sb("c_qb16", [128, 8, 64], BF16, ph2)
                            w_r = Res()
                            wt, wt_r = load_win(l, OFF["c_q"] + which * 512, 512)
                            for t in range(NT):
                                k = t % 2
                                projT(pb[k][:], pr[k], wt, wt_r, 0, 512, t * 128)
                                p3 = pb[k][:].rearrange("p (g d) -> p g d", g=8)
                                kb.op("act", lambda e, p3=p3: e.activation(out=sq[:], in_=p3, func=AF.Square), rd=[pr[k]], wr=[w_r])
                                kb.op("dve", lambda e: e.tensor_reduce(out=ssq[:, :, 0], in_=sq[:], axis=AX.X, op=ALU.add),
                                      rd=[w_r], wr=[w_r])
                                rstd_from(ssq[:, :, 1], ssq[:, :, 0], 1.0 / 64, [w_r, c_r], [w_r], ssq[:, :, 2])
                                kb.op("dve", lambda e, p3=p3: e.tensor_tensor(out=qn[:], in0=p3,
                                                                              in1=ssq[:, :, 1:2].to_broadcast([128, 8, 64]),
                                                                              op=ALU.mult), rd=[pr[k], w_r], wr=[w_r])
                                g2 = qkg[:, which, :].rearrange("p (m d) -> p m d", m=2)
                                q4 = qn[:].rearrange("p (h m) d -> p h m d", h=4)
                                kb.op("dve", lambda e, g2=g2, q4=q4: e.tensor_tensor(
                                    out=q4, in0=q4, in1=g2.unsqueeze(1).to_broadcast([128, 4, 2, 64]), op=ALU.mult),
                                    rd=[w_r, par_r], wr=[w_r])
                                cs = rope_t[:, t, 0:1, :].to_broadcast([128, 8, 8])
                                sn = rope_t[:, t, 1:2, :].to_broadcast([128, 8, 8])
                                x1 = qn[:, :, 0:8]
                                x2 = qn[:, :, 8:16]
                                kb.op("pool", lambda e: e.tensor_tensor(out=rt[:, 0], in0=x1, in1=cs, op=ALU.mult), rd=[w_r, rp_r], wr=[w_r])
                                kb.op("pool", lambda e: e.tensor_tensor(out=rt[:, 1], in0=x2, in1=sn, op=ALU.mult), rd=[w_r, rp_r], wr=[w_r])
                                kb.op("pool", lambda e: e.tensor_tensor(out=rt[:, 2], in0=x2, in1=cs, op=ALU.mult), rd=[w_r, rp_r], wr=[w_r])
                                kb.op("pool", lambda e: e.tensor_tensor(out=rt[:, 3], in0=x1, in1=sn, op=ALU.mult), rd=[w_r, rp_r], wr=[w_r])
                                kb.op("dve", lambda e: e.tensor_copy(out=qb16[:, :, 16:64], in_=qn[:, :, 16:64]), rd=[w_r], wr=[w_r])
                                kb.op("dve", lambda e: e.tensor_tensor(out=qb16[:, :, 0:8], in0=rt[:, 0], in1=rt[:, 1], op=ALU.subtract),
                                      rd=[w_r], wr=[w_r])
                                kb.op("dve", lambda e: e.tensor_tensor(out=qb16[:, :, 8:16], in0=rt[:, 2], in1=rt[:, 3], op=ALU.add),
                                      rd=[w_r], wr=[w_r])
                                pT = pbf(2 + k)
                                q2 = qb16[:].rearrange("p g d -> p (g d)")
                                kb.pe([(lambda e, h=h, pT=pT, q2=q2: e.transpose(out=pT[:, h * 128:(h + 1) * 128],
                                                                                 in_=q2[:, h * 128:(h + 1) * 128],
                                                                                 identity=identb[:])) for h in range(4)],
                                      rd=[w_r, c_r], wr=[pr[2 + k]])
                                kb.op("act", lambda e, t=t, pT=pT, dstT=dstT: e.activation(
                                    out=dstT[:, :, t * 128:(t + 1) * 128], in_=pT[:, 0:512].rearrange("p (h n) -> p h n", h=4),
                                    func=AF.Copy), rd=[pr[2 + k]], wr=[qk_r])
                    for h in range(4):
                        with phase() as ph2:
                            vtm = sb("c_vtm", [128, NT, 128], BF16, ph2)
                            v_r = Res()
                            pt = [sb("c_p%d" % k, [128, 1024], BF16, ph2) for k in range(2)]
                            pt_r = [Res() for _ in range(2)]
                            pacc = sb("c_pacc", [128, 512], F32, ph2)
                            pacc_r = Res()
                            e0 = sb("c_e0", [128, 512], F32, ph2)
                            e1 = sb("c_e1", [128, 512], F32, ph2)
                            ot = sb("c_ot", [128, 512], F32, ph2)
                            ot_r = Res()
                            tmp = dict(sq=sb("c_esq", [128, 512], BF16, ph2), rs=sb("c_ers", [128, 512], F32, ph2),
                                       zg=sb("c_ezg", [128, 512], F32, ph2), yb=sb("c_eyb", [128, 512], BF16, ph2))
                            tmp_r = Res()
                            wzt = sb("c_wz", [128, KC, 128], BF16, ph2)
                            wz_r = Res()
                            wt, wt_r = load_win(l, OFF["c_v"] + h * 128, 128)
                            for t4 in range(0, NT, 4):
                                for t in range(t4, t4 + 4):
                                    projT(pb[0][:, (t - t4) * 128:(t - t4 + 1) * 128], pr[0], wt, wt_r, 0, 128, t * 128)
                                kb.op("act", lambda e, t4=t4: e.activation(out=vtm[:, t4:t4 + 4, :],
                                                                           in_=pb[0][:].rearrange("p (t n) -> p t n", t=4),
                                                                           func=AF.Copy), rd=[pr[0]], wr=[v_r])
                            wt, wt_r = load_win(l, OFF["c_z"] + h * 128, 128)
                            kb.op("pool", lambda e, wt=wt: e.tensor_copy(out=wzt[:], in_=wt), rd=[wt_r], wr=[wz_r])
                            for b in range(NB):
                                def qk_pair(kt):
                                    kb.pe([(lambda e, m=m: e.matmul(
                                        pb[2 * (kt % 2) + m][:], lhsT=kT[m * 64:(m + 1) * 64, h, kt * 128:(kt + 1) * 128],
                                        rhs=qT[m * 64:(m + 1) * 64, h, b * 512:(b + 1) * 512], start=True, stop=True)) for m in range(2)],
                                        rd=[qk_r], wr=[pr[2 * (kt % 2)], pr[2 * (kt % 2) + 1]])

                                qk_pair(0)
                                for kt in range(NT):
                                    if kt + 1 < NT:
                                        qk_pair(kt + 1)
                                    kp = kt % 2
                                    kb.op("act", lambda e, kp=kp: e.activation(
                                        out=pt[kp][:], in_=pbig[:, kp * 1024:(kp + 1) * 1024], func=AF.Exp),
                                        rd=[pr[2 * kp], pr[2 * kp + 1]], wr=[pt_r[kp]])
                                    kb.pe([lambda e, kp=kp, kt=kt: e.matmul(pb[4][:], lhsT=vtm[:, kt, :], rhs=pt[kp][:, 0:512], start=(kt == 0), stop=(kt == NT - 1)),
                                           lambda e, kp=kp, kt=kt: e.matmul(pb[5][:], lhsT=vtm[:, kt, :], rhs=pt[kp][:, 512:1024], start=(kt == 0), stop=(kt == NT - 1)),
                                           lambda e, kp=kp, kt=kt: e.matmul(pb[6][:], lhsT=onesb[:], rhs=pt[kp][:, 0:512], start=(kt == 0), stop=(kt == NT - 1))],
                                          rd=[v_r, pt_r[kp], c_r], wr=[pr[4], pr[5], pr[6]])
                                    if kt == 0:
                                        kb.op("dve", lambda e, kp=kp: e.tensor_copy(out=pacc[:], in_=pt[kp][:, 512:1024]),
                                              rd=[pt_r[kp]], wr=[pacc_r])
                                    else:
                                        kb.op("dve", lambda e, kp=kp: e.tensor_tensor(out=pacc[:], in0=pacc[:], in1=pt[kp][:, 512:1024], op=ALU.add),
                                              rd=[pt_r[kp], pacc_r], wr=[pacc_r])
                                kb.pe([lambda e: e.matmul(pb[7][:], lhsT=C["onesf"], rhs=pacc[:], start=True, stop=True)],
                                      rd=[pacc_r, c_r], wr=[pr[7]])
                                kb.op("dve", lambda e: e.reciprocal(out=e0[:], in_=pb[6][:]), rd=[pr[6]], wr=[ot_r])
                                kb.op("dve", lambda e: e.reciprocal(out=e1[:], in_=pb[7][:]), rd=[pr[7]], wr=[ot_r])
                                kb.op("dve", lambda e: e.tensor_tensor(out=e0[:], in0=e0[:], in1=pb[4][:], op=ALU.mult), rd=[pr[4], ot_r], wr=[ot_r])
                                kb.op("dve", lambda e: e.tensor_tensor(out=e1[:], in0=e1[:], in1=pb[5][:], op=ALU.mult), rd=[pr[5], ot_r], wr=[ot_r])
                                kb.op("dve", lambda e: e.scalar_tensor_tensor(out=ot[:], in0=e1[:], scalar=vecs[:, 3:4], in1=e0[:],
                                                                              op0=ALU.mult, op1=ALU.add), rd=[ot_r, par_r], wr=[ot_r])
                                epilogue(ot[:], ot_r, 2, OFF["c_z"] + h * 128, l, yT_d[2, h * 128:(h + 1) * 128, :], b, tmp, tmp_r, wzt[:], wz_r)
                kb.barrier()

                with phase() as ph:
                  if 'B' not in skip:
                    qT = sb("b_qT", [128, S], BF16, ph)
                    q_r = Res()
                    vtm = sb("b_vtm", [128, NT, 128], BF16, ph)
                    v_r = Res()
                    oT = sb("b_oT", [128, S], F32, ph)
                    o_r = Res()
                    qtmp = sb("b_qtmp", [128, 512], F32, ph)
                    qtmp_r = Res()
                    smask = sb("b_smask", [128, 512], F32, ph)
                    sm_r = Res()
                    tmp = dict(sq=sb("b_esq", [128, 512], BF16, ph), rs=sb("b_ers", [128, 512], F32, ph),
                               zg=sb("b_ezg", [128, 512], F32, ph), yb=sb("b_eyb", [128, 512], BF16, ph))
                    tmp_r = Res()
                    wzt = sb("b_wz", [128, KC, 128], BF16, ph)
                    wz_r = Res()
                    kb.dma(smask[:], smask_d[:, 0:512], c_ds, wr=[sm_r])
                    CH = []
                    for d in range(2):
                        T = dict(
                            kkf=sb("b_kkf%d" % d, [128, 512], F32, ph), lfb=sb("b_lfb%d" % d, [128, 512], F32, ph),
                            gcb=sb("b_gcb%d" % d, [128, 512], F32, ph), a1=sb("b_a1%d" % d, [128, 512], F32, ph),
                            a2=sb("b_a2%d" % d, [128, 512], F32, ph), ex=sb("b_ex%d" % d, [128, 512], F32, ph),
                            qd=sb("b_qd%d" % d, [128, 512], BF16, ph), qm=sb("b_qm%d" % d, [128, 512], BF16, ph),
                            km=sb("b_km%d" % d, [128, 512], BF16, ph), kd=sb("b_kd%d" % d, [128, 512], BF16, ph),
                            atm=sb("b_atm%d" % d, [128, 128], BF16, ph), kdtm=sb("b_kdtm%d" % d, [128, 4, 128], BF16, ph),
                            Sb=sb("b_S%d" % d, [128, 2, 128], BF16, ph), egl=sb("b_egl%d" % d, [128, 16], F32, ph))
                        CH.append(T)
                    for h in range(4):
                        wt, wt_r = load_win(l, OFF["b_q"] + h * 128, 128)
                        for b in range(NB):
                            k = b % 2
                            projF(pb[k][:], pr[k], wt, wt_r, 0, 128, b * 512, 512)
                            kb.op("act", lambda e, k=k: e.activation(out=qtmp[:], in_=pb[k][:], func=AF.Silu),
                                  rd=[pr[k]], wr=[qtmp_r])
                            kb.op("dve", lambda e, b=b: e.tensor_scalar(out=qT[:, b * 512:(b + 1) * 512], in0=qtmp[:], scalar1=128.0 ** -0.5,
                                                                        scalar2=None, op0=ALU.mult), rd=[qtmp_r], wr=[q_r])
                        wt, wt_r = load_win(l, OFF["b_i"] + h * 128, 128)
                        for t4 in range(0, NT, 4):
                            for t in range(t4, t4 + 4):
                                projT(pb[2][:, (t - t4) * 128:(t - t4 + 1) * 128], pr[2], wt, wt_r, 0, 128, t * 128)
                            kb.op("act", lambda e, t4=t4: e.activation(out=vtm[:, t4:t4 + 4, :],
                                                                       in_=pb[2][:].rearrange("p (t n) -> p t n", t=4),
                                                                       func=AF.Copy), rd=[pr[2]], wr=[v_r])
                        wt, wt_r = load_win(l, OFF["b_z"] + h * 128, 128)
                        kb.op("pool", lambda e, wt=wt: e.tensor_copy(out=wzt[:], in_=wt), rd=[wt_r], wr=[wz_r])
                        kb.op("pool", lambda e: e.memset(oT[:], 0.0), wr=[o_r])
                        wfs = [load_win(l, OFF["b_f"] + d * 512 + h * 128, 128) for d in range(2)]

                        def bchain(d):
                            T = CH[d]
                            kkf, lfb, gcb, ex = T["kkf"], T["lfb"], T["gcb"], T["ex"]
                            qd, qm, km, kd, atm, kdtm, Sb, egl = T["qd"], T["qm"], T["km"], T["kd"], T["atm"], T["kdtm"], T["Sb"], T["egl"]
                            bA, bB, bC, bD = [4 * d + i for i in range(4)]
                            wt, wt_r = wfs[d]
                            lf_r, g_r, wk_r, blk_r, atm_r, kdtm_r = Res(), Res(), Res(), Res(), Res(), Res()
                            S_r = [Res(), Res()]
                            last = 31 if d == 0 else 0
                            bmask = C["bd_f"] if d == 0 else C["bd_b"]
                            kb.op("pool", lambda e: e.memset(Sb[:, 0, :], 0.0), wr=[S_r[0]])
                            scur = 0
                            g3 = gcb[:].rearrange("p (c n) -> p c n", n=32)
                            l3 = lfb[:].rearrange("p (c n) -> p c n", n=32)
                            a1 = T["a1"][:].rearrange("p (c n) -> p c n", n=32)
                            a2 = T["a2"][:].rearrange("p (c n) -> p c n", n=32)
                            for b in (range(NB) if d == 0 else range(NB - 1, -1, -1)):
                                sl = slice(b * 512, (b + 1) * 512)
                                projF(pb[bA][:], pr[bA], wt, wt_r, 0, 128, b * 512, 512)
                                yield
                                kb.op("act", lambda e: e.activation(out=kkf[:], in_=pb[bA][:], func=AF.Sigmoid, scale=-1.0),
                                      rd=[pr[bA]], wr=[lf_r])
                                kb.op("dve", lambda e: e.tensor_scalar(out=kkf[:], in0=kkf[:], scalar1=oml[:, l, d, h:h + 1],
                                                                       scalar2=None, op0=ALU.mult), rd=[lf_r, small_r], wr=[lf_r])
                                kb.op("dve", lambda e: e.tensor_scalar(out=lfb[:], in0=kkf[:], scalar1=1.0 - 1e-6,
                                                                       scalar2=None, op0=ALU.min), rd=[lf_r], wr=[lf_r])
                                kb.op("act", lambda e: e.activation(out=lfb[:], in_=lfb[:], func=AF.Ln, scale=-1.0, bias=1.0),
                                      rd=[lf_r], wr=[lf_r])
                                yield
                                kb.op("dve", lambda e: e.tensor_tensor_scan(out=gcb[:], data0=smask[:], data1=lfb[:], initial=0.0,
                                                                            op0=ALU.mult, op1=ALU.add), rd=[lf_r, sm_r], wr=[g_r])
                                if d == 1:
                                    kb.op("dve", lambda e: e.tensor_tensor(out=l3, in0=l3, in1=g3, op=ALU.subtract), rd=[g_r, lf_r], wr=[lf_r])
                                    kb.op("dve", lambda e: e.tensor_tensor(out=g3, in0=l3, in1=g3[:, :, 31:32].to_broadcast([128, 16, 32]),
                                                                           op=ALU.add), rd=[g_r, lf_r], wr=[g_r])
                                kb.op("act", lambda e: e.activation(out=egl[:], in_=g3[:, :, last], func=AF.Exp), rd=[g_r], wr=[g_r])
                                kb.op("pool", lambda e: e.tensor_tensor(out=a1, in0=g3, in1=g3[:, :, 16:17].to_broadcast([128, 16, 32]),
                                                                        op=ALU.subtract), rd=[g_r], wr=[wk_r])
                                kb.op("pool", lambda e: e.tensor_tensor(out=a2, in0=g3, in1=g3[:, :, last:last + 1].to_broadcast([128, 16, 32]),
                                                                        op=ALU.subtract), rd=[g_r], wr=[wk_r])
                                yield
                                kb.op("act", lambda e: e.activation(out=ex[:], in_=T["a1"][:], func=AF.Exp), rd=[wk_r], wr=[wk_r])
                                kb.op("dve", lambda e, sl=sl: e.tensor_tensor(out=qm[:], in0=ex[:], in1=qT[:, sl], op=ALU.mult),
                                      rd=[wk_r, q_r], wr=[blk_r])
                                kb.op("act", lambda e: e.activation(out=ex[:], in_=T["a1"][:], func=AF.Exp, scale=-1.0), rd=[wk_r], wr=[wk_r])
                                kb.op("dve", lambda e: e.tensor_tensor(out=km[:], in0=ex[:], in1=kkf[:], op=ALU.mult),
                                      rd=[wk_r, lf_r], wr=[blk_r])
                                yield
                                kb.op("act", lambda e: e.activation(out=ex[:], in_=gcb[:], func=AF.Exp), rd=[wk_r, g_r], wr=[wk_r])
                                kb.op("dve", lambda e, sl=sl: e.tensor_tensor(out=qd[:], in0=ex[:], in1=qT[:, sl], op=ALU.mult),
                                      rd=[wk_r, q_r], wr=[blk_r])
                                kb.op("act", lambda e: e.activation(out=ex[:], in_=T["a2"][:], func=AF.Exp, scale=-1.0), rd=[wk_r], wr=[wk_r])
                                kb.op("dve", lambda e: e.tensor_tensor(out=kd[:], in0=ex[:], in1=kkf[:], op=ALU.mult),
                                      rd=[wk_r, lf_r], wr=[blk_r])
                                yield
                                for gq in (range(4) if d == 0 else range(3, -1, -1)):
                                    t = b * 4 + gq
                                    gs = slice(gq * 128, (gq + 1) * 128)
                                    pT = pbf(bB)
                                    kb.pe([lambda e, gs=gs: e.matmul(pb[bA][:, 0:128], lhsT=km[:, gs], rhs=qm[:, gs], start=True, stop=True),
                                           lambda e, gs=gs, pT=pT: e.transpose(out=pT[:, 0:128], in_=kd[:, gs], identity=identb[:])],
                                          rd=[blk_r, c_r], wr=[pr[bA], pr[bB]])
                                    yield
                                    kb.op("dve", lambda e: e.tensor_tensor(out=atm[:], in0=pb[bA][:, 0:128], in1=bmask, op=ALU.mult),
                                          rd=[pr[bA], c_r], wr=[atm_r])
                                    kb.op("dve", lambda e, pT=pT: e.tensor_tensor(
                                        out=kdtm[:], in0=pT[:, 0:128].unsqueeze(1).to_broadcast([128, 4, 128]),
                                        in1=C["rm4"][:, 0:4].unsqueeze(2).to_broadcast([128, 4, 128]), op=ALU.mult),
                                        rd=[pr[bB], c_r], wr=[kdtm_r])
                                    kb.pe([lambda e, t=t: e.matmul(pb[bC][:, 0:128], lhsT=vtm[:, t, :], rhs=atm[:], start=True, stop=False)],
                                          rd=[v_r, atm_r], wr=[pr[bC]])
                                    yield
                                    for cq in (range(4) if d == 0 else range(3, -1, -1)):
                                        cidx = gq * 4 + cq
                                        islast = (cq == (3 if d == 0 else 0))
                                        cs_ = slice(gq * 128 + cq * 32, gq * 128 + (cq + 1) * 32)
                                        ps_ = slice(cq * 32, (cq + 1) * 32)
                                        kb.pe([lambda e, cs_=cs_, ps_=ps_, scur=scur, islast=islast: e.matmul(
                                            pb[bC][:, ps_], lhsT=Sb[:, scur, :], rhs=qd[:, cs_], start=False, stop=islast),
                                            lambda e, cq=cq, t=t: e.matmul(pb[bD][:, 0:128], lhsT=kdtm[:, cq, :], rhs=vtm[:, t, :],
                                                                           start=True, stop=True)],
                                            rd=[S_r[scur], blk_r, kdtm_r, v_r], wr=[pr[bC], pr[bD]])
                                        yield
                                        kb.op("dve", lambda e, scur=scur, cidx=cidx: e.scalar_tensor_tensor(
                                            out=Sb[:, 1 - scur, :], in0=Sb[:, scur, :], scalar=egl[:, cidx:cidx + 1], in1=pb[bD][:, 0:128],
                                            op0=ALU.mult, op1=ALU.add), rd=[S_r[scur], pr[bD], g_r], wr=[S_r[1 - scur]])
                                        scur = 1 - scur
                                    osl = slice(t * 128, (t + 1) * 128)
                                    kb.op("pool" if False else "dve", lambda e, osl=osl: e.tensor_tensor(out=oT[:, osl], in0=oT[:, osl], in1=pb[bC][:, 0:128],
                                                                                                         op=ALU.add), rd=[pr[bC], o_r], wr=[o_r])

                        gens = [bchain(0), bchain(1)]
                        while gens:
                            for gen in list(gens):
                                try:
                                    next(gen)
                                except StopIteration:
                                    gens.remove(gen)
                        for b in range(NB):
                            epilogue(oT[:, b * 512:(b + 1) * 512], o_r, 1, OFF["b_z"] + h * 128, l,
                                     yT_d[1, h * 128:(h + 1) * 128, :], b, tmp, tmp_r, wzt[:], wz_r)
                kb.barrier()

                with phase() as ph:
                  if 'A' not in skip:
                    SC = sb("a_SC", [128, NT, 8, 8], F32, ph)
                    sc_r = Res()
                    phg = ExitStack()
                    gpre = sb("a_gpre", [128, NT, 16], F32, phg)
                    gtmp = sb("a_gtmp", [128, NT, 8], F32, phg)
                    gg = sb("a_gg", [128, NT, 8], F32, phg)
                    wt, wt_r = load_win(l, OFF["a_b"], 16)
                    for t in range(NT):
                        projT(pb[0][:, t * 16:(t + 1) * 16], pr[0], wt, wt_r, 0, 16, t * 128)
                    kb.op("act", lambda e: e.activation(out=gpre[:], in_=pb[0][:, 0:NT * 16].rearrange("p (t n) -> p t n", n=16),
                                                        func=AF.Copy), rd=[pr[0]], wr=[sc_r])
                    kb.op("act", lambda e: e.activation(out=SC[:, :, :, 5], in_=gpre[:, :, 0:8], func=AF.Sigmoid), rd=[sc_r], wr=[sc_r])
                    kb.op("dve", lambda e: e.tensor_tensor(out=gtmp[:], in0=gpre[:, :, 8:16],
                                                           in1=gatec[:, 8:16].unsqueeze(1).to_broadcast([128, NT, 8]), op=ALU.add),
                          rd=[sc_r, par_r], wr=[sc_r])
                    kb.op("act", lambda e: e.activation(out=gtmp[:], in_=gtmp[:], func=AF.Exp), rd=[sc_r], wr=[sc_r])
                    kb.op("act", lambda e: e.activation(out=gtmp[:], in_=gtmp[:], func=AF.Ln, bias=1.0), rd=[sc_r], wr=[sc_r])
                    kb.op("dve", lambda e: e.tensor_tensor(out=gg[:], in0=gtmp[:],
                                                           in1=gatec[:, 0:8].unsqueeze(1).to_broadcast([128, NT, 8]), op=ALU.mult),
                          rd=[sc_r, par_r], wr=[sc_r])
                    g2 = gg[:].rearrange("p t n -> p (t n)")
                    kb.pe([lambda e: e.matmul(pb[1][:, 0:NT * 8], lhsT=C["uincl"], rhs=g2, start=True, stop=True)], rd=[sc_r, c_r], wr=[pr[1]])
                    kb.pe([lambda e: e.matmul(pb[2][:, 0:NT * 8], lhsT=C["uinclT"], rhs=g2, start=True, stop=True)], rd=[sc_r, c_r], wr=[pr[2]])
                    kb.pe([lambda e: e.matmul(pb[3][:, 0:NT * 8], lhsT=C["onesf"], rhs=g2, start=True, stop=True)], rd=[sc_r, c_r], wr=[pr[3]])
                    p1 = pb[1][:, 0:NT * 8].rearrange("p (t n) -> p t n", n=8)
                    p2 = pb[2][:, 0:NT * 8].rearrange("p (t n) -> p t n", n=8)
                    p3_ = pb[3][:, 0:NT * 8].rearrange("p (t n) -> p t n", n=8)
                    kb.op("act", lambda e: e.activation(out=SC[:, :, 0:4, 1], in_=p1[:, :, 0:4], func=AF.Copy), rd=[pr[1]], wr=[sc_r])
                    kb.op("act", lambda e: e.activation(out=SC[:, :, 4:8, 1], in_=p2[:, :, 4:8], func=AF.Copy), rd=[pr[2]], wr=[sc_r])
                    kb.op("dve", lambda e: e.tensor_scalar(out=SC[:, :, :, 2], in0=SC[:, :, :, 1], scalar1=-1.0, scalar2=None, op0=ALU.mult),
                          rd=[sc_r], wr=[sc_r])
                    kb.op("act", lambda e: e.activation(out=SC[:, :, :, 7], in_=p3_, func=AF.Exp), rd=[pr[3]], wr=[sc_r])
                    kb.op("dve", lambda e: e.tensor_tensor(out=gtmp[:], in0=p3_, in1=SC[:, :, :, 1], op=ALU.subtract), rd=[pr[3], sc_r], wr=[sc_r])
                    kb.op("act", lambda e: e.activation(out=SC[:, :, :, 4], in_=gtmp[:], func=AF.Exp), rd=[sc_r], wr=[sc_r])
                    kb.op("act", lambda e: e.activation(out=gtmp[:], in_=SC[:, :, :, 1], func=AF.Exp), rd=[sc_r], wr=[sc_r])
                    kb.op("dve", lambda e: e.tensor_tensor(out=SC[:, :, :, 3], in0=gtmp[:], in1=SC[:, :, :, 5], op=ALU.mult), rd=[sc_r], wr=[sc_r])
                    kb.op("dve", lambda e: e.tensor_scalar(out=SC[:, :, :, 6], in0=gtmp[:], scalar1=128.0 ** -0.5, scalar2=None, op0=ALU.mult),
                          rd=[sc_r], wr=[sc_r])
                    kb.op("act", lambda e: e.activation(out=gtmp[:], in_=SC[:, :, :, 5], func=AF.Ln), rd=[sc_r], wr=[sc_r])
                    kb.op("dve", lambda e: e.tensor_tensor(out=SC[:, :, :, 0], in0=gtmp[:], in1=SC[:, :, :, 1], op=ALU.add), rd=[sc_r], wr=[sc_r])

                    kb.barrier()
                    phg.close()
                    dump("SC", SC[:].rearrange("p t n k -> p (t n k)"), sc_r, F32)
                    qT = sb("a_qT", [128, S], BF16, ph)
                    kT = sb("a_kT", [128, S], BF16, ph)
                    qkvtm = sb("a_qkvtm", [128, NT, 3, 128], BF16, ph)
                    qkv_r = Res()
                    oT = sb("a_oT", [128, S], F32, ph)
                    o_r = Res()
                    for h in range(4):
                        with phase() as ph2:
                            xpad = sb("a_xpad", [128, S + 4], F32, ph2)
                            xp_r = Res()
                            acc = sb("a_acc", [128, min(1024, S)], F32, ph2)
                            acc_r = Res()
                            vT = sb("a_vT", [128, S], BF16, ph2)
                            sqb = sb("a_sqb", [128, 512], BF16, ph2)
                            rsb = sb("a_rsb", [128, 512], F32, ph2)
                            tmp_r = Res()
                            kb.op("pool", lambda e: e.memset(xpad[:, 0:2], 0.0), wr=[xp_r])
                            kb.op("pool", lambda e: e.memset(xpad[:, S + 2:S + 4], 0.0), wr=[xp_r])
                            kb.op("pool", lambda e: e.memset(oT[:], 0.0), wr=[o_r])
                            for xi, (nm, dst) in enumerate((("a_q", qT), ("a_k", kT), ("a_v", vT))):
                                wt, wt_r = load_win(l, OFF[nm] + h * 128, 128)
                                for b in range(NB):
                                    k = b % 2
                                    projF(pb[k][:], pr[k], wt, wt_r, 0, 128, b * 512, 512)
                                    kb.op("act", lambda e, b=b, k=k: e.activation(out=xpad[:, 2 + b * 512:2 + (b + 1) * 512], in_=pb[k][:],
                                                                                  func=AF.Copy), rd=[pr[k]], wr=[xp_r])
                                grp = xi * 4 + h
                                QW = min(1024, S)
                                for q0 in range(0, S, QW):
                                    kb.op("dve", lambda e, grp=grp, q0=q0: e.tensor_scalar(out=acc[:], in0=xpad[:, q0:q0 + QW], scalar1=convw[:, grp, 0:1],
                                                                                           scalar2=None, op0=ALU.mult), rd=[xp_r, par_r], wr=[acc_r])
                                    for j in range(1, 5):
                                        kb.op("dve", lambda e, grp=grp, j=j, q0=q0: e.scalar_tensor_tensor(
                                            out=acc[:], in0=xpad[:, q0 + j:q0 + j + QW], scalar=convw[:, grp, j:j + 1], in1=acc[:],
                                            op0=ALU.mult, op1=ALU.add), rd=[xp_r, par_r, acc_r], wr=[acc_r])
                                    kb.op("act", lambda e: e.activation(out=acc[:], in_=acc[:], func=AF.Silu), rd=[acc_r], wr=[acc_r])
                                    if xi < 2:
                                        for b_ in range(QW // 512):
                                            sl = slice(b_ * 512, (b_ + 1) * 512)
                                            osl_ = slice(q0 + b_ * 512, q0 + (b_ + 1) * 512)
                                            kb.op("act", lambda e, sl=sl: e.activation(out=sqb[:], in_=acc[:, sl], func=AF.Square), rd=[acc_r], wr=[tmp_r])
                                            kb.pe([lambda e: e.matmul(pb[2][:], lhsT=onesb[:], rhs=sqb[:], start=True, stop=True)],
                                                  rd=[tmp_r, c_r], wr=[pr[2]])
                                            kb.op("act", lambda e: e.activation(out=rsb[:], in_=pb[2][:], func=AF.Ln, bias=epsc[:, 0:1]),
                                                  rd=[pr[2], c_r], wr=[tmp_r])
                                            kb.op("act", lambda e: e.activation(out=rsb[:], in_=rsb[:], func=AF.Exp, scale=-0.5), rd=[tmp_r], wr=[tmp_r])
                                            kb.op("dve", lambda e, sl=sl, osl_=osl_, dst=dst: e.tensor_tensor(out=dst[:, osl_], in0=acc[:, sl], in1=rsb[:], op=ALU.mult),
                                                  rd=[acc_r, tmp_r], wr=[qkv_r])
                                    else:
                                        kb.op("dve", lambda e, dst=dst, q0=q0: e.tensor_copy(out=dst[:, q0:q0 + QW], in_=acc[:]), rd=[acc_r], wr=[qkv_r])
                            for t in range(NT):
                                k = t % 2
                                pT = pbf(3 + k)
                                ts_ = slice(t * 128, (t + 1) * 128)
                                kb.pe([(lambda e, xi=xi, src=src, pT=pT, ts_=ts_: e.transpose(out=pT[:, xi * 128:(xi + 1) * 128], in_=src[:, ts_],
                                                                                              identity=identb[:]))
                                       for xi, src in enumerate((qT, kT, vT))], rd=[qkv_r, c_r], wr=[pr[3 + k]])
                                kb.op("act", lambda e, t=t, pT=pT: e.activation(out=qkvtm[:, t, :, :],
                                                                                in_=pT[:, 0:384].rearrange("p (x n) -> p x n", x=3),
                                                                                func=AF.Copy), rd=[pr[3 + k]], wr=[qkv_r])
                        dump("qT", qT[:], qkv_r, BF16)
                        with phase() as ph2:
                            G = 2
                            GW = G * 128

                            def chain(d, BK):
                                bA, bB, bC, bD = BK
                                sfx = "_%d" % d
                                dgF = sb("a_dgF" + sfx, [128, G, 3, 128], F32, ph2)
                                dgB = sb("a_dgB" + sfx, [128, G, 4, 128], BF16, ph2)
                                dg_r = Res()
                                DJI = sb("a_DJI" + sfx, [128, G, 128], F32, ph2)
                                DIJ = sb("a_DIJ" + sfx, [128, G, 128], F32, ph2)
                                dd_r = Res()
                                aqk = sb("a_aqk" + sfx, [128, G, 128], BF16, ph2)
                                aqk_r = Res()
                                YP = [sb("a_YP%d" % k + sfx, [128, G, 256], F32, ph2) for k in range(2)]
                                ZZ = [sb("a_ZZ%d" % k + sfx, [128, G, 128], F32, ph2) for k in range(2)]
                                yz_r = [Res(), Res()]
                                TTb = sb("a_TTb" + sfx, [128, G, 128], BF16, ph2)
                                tt_r = Res()
                                scl = sb("a_scl" + sfx, [128, G, 4, 128], BF16, ph2)
                                scl_r = Res()
                                wTt = sb("a_wT" + sfx, [128, G, 128], BF16, ph2)
                                ut = sb("a_u" + sfx, [128, G, 128], F32, ph2)
                                wu_r = Res()
                                vnew = sb("a_vnew" + sfx, [128, 128], BF16, ph2)
                                vn_r = Res()
                                Sf = sb("a_Sf" + sfx, [128, 128], F32, ph2)
                                Sbf = sb("a_Sbf" + sfx, [128, 128], BF16, ph2)
                                S_r = Res()
                                n = d * 4 + h
                                negJI = C["negJI_f"] if d == 0 else C["negJI_b"]
                                negIJ = C["negIJ_f"] if d == 0 else C["negIJ_b"]
                                kb.op("pool", lambda e: e.memset(Sf[:], 0.0), wr=[S_r])
                                kb.op("pool", lambda e: e.memset(Sbf[:], 0.0), wr=[S_r])
                                batches = list(range(0, NT, G))
                                if d == 1:
                                    batches = batches[::-1]
                                for t0 in batches:
                                    kb.op("pool", lambda e, t0=t0: e.tensor_tensor(
                                        out=dgF[:], in0=C["identf"].unsqueeze(1).unsqueeze(1).to_broadcast([128, G, 3, 128]),
                                        in1=SC[:, t0:t0 + G, n, 0:3].unsqueeze(3).to_broadcast([128, G, 3, 128]), op=ALU.mult),
                                        rd=[sc_r, c_r], wr=[dg_r])
                                    kb.op("pool", lambda e, t0=t0: e.tensor_tensor(
                                        out=dgB[:], in0=C["identf"].unsqueeze(1).unsqueeze(1).to_broadcast([128, G, 4, 128]),
                                        in1=SC[:, t0:t0 + G, n, 3:7].unsqueeze(3).to_broadcast([128, G, 4, 128]), op=ALU.mult),
                                        rd=[sc_r, c_r], wr=[dg_r])
                                    fns = []
                                    for g in range(G):
                                        o0 = pb[bA][:, g * 128:(g + 1) * 128]
                                        o1 = pb[bB][:, g * 128:(g + 1) * 128]
                                        fns += [lambda e, g=g, o0=o0: e.matmul(o0, lhsT=C["onesf"], rhs=dgF[:, g, 1, :], start=True, stop=False),
                                                lambda e, g=g, o0=o0: e.matmul(o0, lhsT=dgF[:, g, 2, :], rhs=C["onesf"], start=False, stop=False),
                                                lambda e, g=g, o0=o0: e.matmul(o0, lhsT=C["identf"], rhs=negJI, start=False, stop=True),
                                                lambda e, g=g, o1=o1: e.matmul(o1, lhsT=dgF[:, g, 0, :], rhs=C["onesf"], start=True, stop=False),
                                                lambda e, g=g, o1=o1: e.matmul(o1, lhsT=C["onesf"], rhs=dgF[:, g, 2, :], start=False, stop=False),
                                                lambda e, g=g, o1=o1: e.matmul(o1, lhsT=C["identf"], rhs=negIJ, start=False, stop=True)]
                                    kb.pe(fns, rd=[dg_r, c_r], wr=[pr[bA], pr[bB]])
                                    fns = []
                                    for g in range(G):
                                        ts_ = slice((t0 + g) * 128, (t0 + g + 1) * 128)
                                        fns += [lambda e, g=g, ts_=ts_: e.matmul(pb[bC][:, g * 128:(g + 1) * 128], lhsT=kT[:, ts_], rhs=kT[:, ts_], start=True, stop=True),
                                                lambda e, g=g, ts_=ts_: e.matmul(pb[bD][:, g * 128:(g + 1) * 128], lhsT=kT[:, ts_], rhs=qT[:, ts_], start=True, stop=True)]
                                    kb.pe(fns, rd=[qkv_r], wr=[pr[bC], pr[bD]])
                                    kb.op("act", lambda e: e.activation(out=DJI[:].rearrange("p g n -> p (g n)"), in_=pb[bA][:, 0:GW], func=AF.Exp),
                                          rd=[pr[bA]], wr=[dd_r])
                                    kb.op("act", lambda e: e.activation(out=DIJ[:].rearrange("p g n -> p (g n)"), in_=pb[bB][:, 0:GW], func=AF.Exp),
                                          rd=[pr[bB]], wr=[dd_r])
                                    yield
                                    kb.op("dve", lambda e: e.scalar_tensor_tensor(out=ZZ[0][:].rearrange("p g n -> p (g n)"), in0=pb[bC][:, 0:GW], scalar=-1.0,
                                                                                  in1=DIJ[:].rearrange("p g n -> p (g n)"), op0=ALU.mult, op1=ALU.mult),
                                          rd=[pr[bC], dd_r], wr=[yz_r[0]])
                                    kb.op("dve", lambda e: e.scalar_tensor_tensor(out=aqk[:].rearrange("p g n -> p (g n)"), in0=pb[bD][:, 0:GW], scalar=128.0 ** -0.5,
                                                                                  in1=DJI[:].rearrange("p g n -> p (g n)"), op0=ALU.mult, op1=ALU.mult),
                                          rd=[pr[bD], dd_r], wr=[aqk_r])
                                    for g in range(G):
                                        bank = (bA, bB)[g % 2]
                                        t = t0 + g
                                        kb.pe([lambda e, g=g, bank=bank, t=t: e.matmul(pb[bank][:, 0:128], lhsT=dgB[:, g, 0, :], rhs=qkvtm[:, t, 1, :], start=True, stop=True),
                                               lambda e, g=g, bank=bank, t=t: e.matmul(pb[bank][:, 128:256], lhsT=dgB[:, g, 1, :], rhs=qkvtm[:, t, 1, :], start=True, stop=True),
                                               lambda e, g=g, bank=bank, t=t: e.matmul(pb[bank][:, 256:384], lhsT=dgB[:, g, 2, :], rhs=qkvtm[:, t, 2, :], start=True, stop=True),
                                               lambda e, g=g, bank=bank, t=t: e.matmul(pb[bank][:, 384:512], lhsT=qkvtm[:, t, 0, :], rhs=dgB[:, g, 3, :], start=True, stop=True)],
                                              rd=[dg_r, qkv_r], wr=[pr[bank]])
                                        kb.op("act", lambda e, g=g, bank=bank: e.activation(
                                            out=scl[:, g, :, :].rearrange("p x n -> p (x n)"), in_=pb[bank][:], func=AF.Copy),
                                            rd=[pr[bank]], wr=[scl_r])
                                    kb.pe([(lambda e, g=g: e.transpose(out=pb[bC][:, g * 128:(g + 1) * 128], in_=ZZ[0][:, g, :], identity=C["identf"]))
                                           for g in range(G)], rd=[yz_r[0], c_r], wr=[pr[bC]])
                                    yield
                                    pT3 = pb[bC][:, 0:GW].rearrange("p (g n) -> p g n", g=G)
                                    kb.op("act", lambda e, pT3=pT3: e.activation(out=YP[0][:, :, 0:128], in_=pT3, func=AF.Copy), rd=[pr[bC]], wr=[yz_r[0]])
                                    kb.op("dve", lambda e, pT3=pT3: e.tensor_tensor(out=YP[0][:, :, 128:256], in0=pT3,
                                                                                    in1=C["identf"].unsqueeze(1).to_broadcast([128, G, 128]), op=ALU.add),
                                          rd=[pr[bC], c_r], wr=[yz_r[0]])
                                    cur = 0
                                    for lev in range(7):
                                        nxt = 1 - cur
                                        fns = []
                                        for g in range(G):
                                            o_ = g * 256
                                            if lev == 0:
                                                fns.append(lambda e, g=g, o_=o_, cur=cur: e.matmul(
                                                    pb[bC][:, o_:o_ + 128], lhsT=ZZ[cur][:, g, :], rhs=YP[cur][:, g, 0:128], start=True, stop=True))
                                            elif lev < 6:
                                                fns.append(lambda e, g=g, o_=o_, cur=cur: e.matmul(
                                                    pb[bC][:, o_:o_ + 256], lhsT=ZZ[cur][:, g, :], rhs=YP[cur][:, g, :], start=True, stop=True))
                                            else:
                                                fns.append(lambda e, g=g, o_=o_, cur=cur: e.matmul(
                                                    pb[bC][:, o_ + 128:o_ + 256], lhsT=ZZ[cur][:, g, :], rhs=YP[cur][:, g, 128:256], start=True, stop=True))
                                            if lev < 6:
                                                fns.append(lambda e, g=g, cur=cur: e.matmul(
                                                    pb[bD][:, g * 128:(g + 1) * 128], lhsT=YP[cur][:, g, 0:128], rhs=ZZ[cur][:, g, :], start=True, stop=True))
                                        kb.pe(fns, rd=[yz_r[cur]], wr=[pr[bC], pr[bD]])
                                        yield
                                        src = pb[bC][:, 0:G * 256].rearrange("p (g n) -> p g n", g=G)
                                        if lev < 6:
                                            kb.op("act", lambda e, src=src, nxt=nxt: e.activation(out=YP[nxt][:, :, 0:128], in_=src[:, :, 0:128], func=AF.Copy),
                                                  rd=[pr[bC]], wr=[yz_r[nxt]])
                                        if lev == 0:
                                            kb.op("dve", lambda e, nxt=nxt, cur=cur: e.tensor_copy(out=YP[nxt][:, :, 128:256], in_=YP[cur][:, :, 128:256]),
                                                  rd=[yz_r[cur]], wr=[yz_r[nxt]])
                                        else:
                                            kb.op("dve", lambda e, src=src, nxt=nxt, cur=cur: e.tensor_tensor(
                                                out=YP[nxt][:, :, 128:256], in0=src[:, :, 128:256], in1=YP[cur][:, :, 128:256], op=ALU.add),
                                                rd=[pr[bC], yz_r[cur]], wr=[yz_r[nxt]])
                                        if lev < 6:
                                            kb.op("act", lambda e, nxt=nxt: e.activation(out=ZZ[nxt][:].rearrange("p g n -> p (g n)"), in_=pb[bD][:, 0:GW], func=AF.Copy),
                                                  rd=[pr[bD]], wr=[yz_r[nxt]])
                                        cur = nxt
                                    kb.op("act", lambda e, cur=cur: e.activation(out=TTb[:], in_=YP[cur][:, :, 128:256], func=AF.Copy), rd=[yz_r[cur]], wr=[tt_r])
                                    fns = []
                                    for g in range(G):
                                        fns += [lambda e, g=g: e.matmul(pb[bA][:, g * 128:(g + 1) * 128], lhsT=scl[:, g, 0, :], rhs=TTb[:, g, :], start=True, stop=True),
                                                lambda e, g=g: e.matmul(pb[bA][:, GW + g * 128:GW + (g + 1) * 128], lhsT=TTb[:, g, :], rhs=scl[:, g, 2, :], start=True, stop=True)]
                                    kb.pe(fns, rd=[scl_r, tt_r], wr=[pr[bA]])
                                    yield
                                    kb.op("act", lambda e: e.activation(out=wTt[:].rearrange("p g n -> p (g n)"), in_=pb[bA][:, 0:GW], func=AF.Copy), rd=[pr[bA]], wr=[wu_r])
                                    kb.op("dve", lambda e: e.tensor_copy(out=ut[:].rearrange("p g n -> p (g n)"), in_=pb[bA][:, GW:2 * GW]), rd=[pr[bA]], wr=[wu_r])
                                    for g in (range(G) if d == 0 else range(G - 1, -1, -1)):
                                        t = t0 + g
                                        kb.pe([lambda e, g=g: e.matmul(pb[bB][:, 0:128], lhsT=wTt[:, g, :], rhs=Sbf[:], start=True, stop=True)],
                                              rd=[wu_r, S_r], wr=[pr[bB]])
                                        yield
                                        kb.op("dve", lambda e, g=g: e.tensor_tensor(out=vnew[:], in0=ut[:, g, :], in1=pb[bB][:, 0:128], op=ALU.subtract),
                                              rd=[wu_r, pr[bB]], wr=[vn_r])
                                        kb.pe([lambda e, g=g: e.matmul(pb[bB][:, 128:256], lhsT=Sbf[:], rhs=scl[:, g, 3, :], start=True, stop=False),
                                               lambda e, g=g: e.matmul(pb[bB][:, 128:256], lhsT=vnew[:], rhs=aqk[:, g, :], start=False, stop=True),
                                               lambda e, g=g: e.matmul(pb[bB][:, 256:384], lhsT=scl[:, g, 1, :], rhs=vnew[:], start=True, stop=True)],
                                              rd=[S_r, scl_r, vn_r, aqk_r], wr=[pr[bB]])
                                        yield
                                        kb.op("dve", lambda e, t=t: e.scalar_tensor_tensor(out=Sf[:], in0=Sf[:], scalar=SC[:, t, n, 7:8], in1=pb[bB][:, 256:384],
                                                                                           op0=ALU.mult, op1=ALU.add), rd=[S_r, sc_r, pr[bB]], wr=[S_r])
                                        kb.op("act", lambda e: e.activation(out=Sbf[:], in_=Sf[:], func=AF.Copy), rd=[S_r], wr=[S_r])
                                        osl = slice(t * 128, (t + 1) * 128)
                                        kb.op("dve", lambda e, osl=osl: e.tensor_tensor(out=oT[:, osl], in0=oT[:, osl], in1=pb[bB][:, 128:256], op=ALU.add),
                                              rd=[pr[bB], o_r], wr=[o_r])

                            gens = [chain(0, (0, 1, 2, 3)), chain(1, (4, 5, 6, 7))]
                            while gens:
                                for gen in list(gens):
                                    try:
                                        next(gen)
                                    except StopIteration:
                                        gens.remove(gen)
                        dump("oT", oT[:], o_r, F32)
                        with phase() as ph2:
                            tmp = dict(sq=sb("a_esq", [128, 512], BF16, ph2), rs=sb("a_ers", [128, 512], F32, ph2),
                                       zg=sb("a_ezg", [128, 512], F32, ph2), yb=sb("a_eyb", [128, 512], BF16, ph2))
                            tmp_r = Res()
                            wzt = sb("a_wz", [128, KC, 128], BF16, ph2)
                            wz_r = Res()
                            wt, wt_r = load_win(l, OFF["a_z"] + h * 128, 128)
                            kb.op("pool", lambda e, wt=wt: e.tensor_copy(out=wzt[:], in_=wt), rd=[wt_r], wr=[wz_r])
                            for b in range(NB):
                                epilogue(oT[:, b * 512:(b + 1) * 512], o_r, 0, OFF["a_z"] + h * 128, l,
                                         yT_d[0, h * 128:(h + 1) * 128, :], b, tmp, tmp_r, wzt[:], wz_r)
                kb.barrier()
                if dbg and s == 0 and l == dcur["dl"]:
                    kb.dma(dbg_d[:, :, :], yT_d[:, :, :], st_ds, rd=[yT_r], wr=[Res()])
                    kb.barrier()

                with phase() as ph:
                  if 'M' not in skip:
                    wbr = [sb("m_wbr%d" % k, [128, 4, D], BF16, ph) for k in range(3)]
                    wo = sb("m_wo", [128, KC, D], BF16, ph)
                    mw_r = Res()
                    ci = 0
                    for x in range(3):
                        for ch in range(2):
                            kb.dma(stg[:, 0:4, 0:512], w_br[x][l, :, ch * 512:(ch + 1) * 512].rearrange("(c p) n -> p c n", p=128), ld_ds, wr=[stg_r])
                            kb.op(("pool", "dve", "act")[ci % 3], lambda e, x=x, ch=ch, ci=ci: (
                                e.activation(out=wbr[x][:, :, ch * 512:(ch + 1) * 512], in_=stg[:, 0:4, 0:512], func=AF.Copy) if ci % 3 == 2
                                else e.tensor_copy(out=wbr[x][:, :, ch * 512:(ch + 1) * 512], in_=stg[:, 0:4, 0:512])), rd=[stg_r], wr=[mw_r])
                            ci += 1
                    for ch in range(2):
                        kb.dma(stg[:, 0:KC, 0:512], w_out[l, :, ch * 512:(ch + 1) * 512].rearrange("(c p) n -> p c n", p=128), ld_ds, wr=[stg_r])
                        kb.op(("pool", "dve", "act")[ci % 3], lambda e, ch=ch, ci=ci: (
                            e.activation(out=wo[:, :, ch * 512:(ch + 1) * 512], in_=stg[:, 0:KC, 0:512], func=AF.Copy) if ci % 3 == 2
                            else e.tensor_copy(out=wo[:, :, ch * 512:(ch + 1) * 512], in_=stg[:, 0:KC, 0:512])), rd=[stg_r], wr=[mw_r])
                        ci += 1
                    yt = [sb("m_yt%d" % k, [128, 3, 4, 128], BF16, ph) for k in range(2)]
                    gt = [sb("m_gt%d" % k, [128, 3 * D], BF16, ph) for k in range(2)]
                    xt = [sb("m_xt%d" % k, [128, D], F32, ph) for k in range(2)]
                    in_r = [Res(), Res()]
                    mg = [sb("m_mg%d" % k, [128, D], F32, ph) for k in range(2)]
                    mgb = [sb("m_mgb%d" % k, [128, D], BF16, ph) for k in range(2)]
                    mtmp = [sb("m_tmp%d" % k, [128, 2, 512], F32, ph) for k in range(2)]
                    mg_r = [Res(), Res()]
                    mt_r = [[Res(), Res()], [Res(), Res()]]
                    mT = [sb("m_mT%d" % k, [128, KC, 128], BF16, ph) for k in range(2)]
                    mT_r = [Res(), Res()]
                    ot = [sb("m_ot%d" % k, [128, D], F32, ph) for k in range(2)]
                    ot_r = [Res(), Res()]
                    x_new = Res()

                    def m_s1(t):
                        k = t % 2
                        ts_ = slice(t * 128, (t + 1) * 128)
                        kb.dma(yt[k][:].rearrange("p x c n -> p (x c) n"),
                               yT_d[:, :, ts_].rearrange("x (c p) n -> p (x c) n", p=128), xl_ds[k], rd=[yT_r], wr=[in_r[k]])
                        kb.dma(gt[k][:], gate_d[ts_, :], xl_ds[k], rd=[gate_r], wr=[in_r[k]])
                        kb.dma(xt[k][:], xsrc[ts_, :], xl_ds[k], rd=[xr[0]], wr=[in_r[k]])
                        for ch in range(2):
                            cs_ = slice(ch * 512, (ch + 1) * 512)
                            for x in range(3):
                                bank = (x + ch) % 3
                                kb.pe([(lambda e, c=c, x=x, bank=bank, cs_=cs_: e.matmul(pb[bank][:], lhsT=yt[k][:, x, c, :], rhs=wbr[x][:, c, cs_],
                                                                                        start=(c == 0), stop=(c == 3))) for c in range(4)],
                                      rd=[in_r[k], mw_r], wr=[pr[bank]])
                                gsl = gt[k][:, x * D + ch * 512:x * D + (ch + 1) * 512]
                                if x == 0:
                                    kb.op("dve", lambda e, bank=bank, gsl=gsl, cs_=cs_: e.tensor_tensor(out=mg[k][:, cs_], in0=pb[bank][:], in1=gsl, op=ALU.mult),
                                          rd=[pr[bank], in_r[k]], wr=[mg_r[k]])
                                else:
                                    kb.op("dve", lambda e, bank=bank, gsl=gsl, x=x: e.tensor_tensor(out=mtmp[k][:, x - 1, :], in0=pb[bank][:], in1=gsl, op=ALU.mult),
                                          rd=[pr[bank], in_r[k]], wr=[mt_r[k][x - 1]])
                                    kb.op("pool", lambda e, cs_=cs_, x=x: e.tensor_tensor(out=mg[k][:, cs_], in0=mg[k][:, cs_], in1=mtmp[k][:, x - 1, :], op=ALU.add),
                                          rd=[mg_r[k], mt_r[k][x - 1]], wr=[mg_r[k]])
                        kb.op("act", lambda e: e.activation(out=mgb[k][:], in_=mg[k][:], func=AF.Copy), rd=[mg_r[k]], wr=[mg_r[k]])

                    def m_s2(t):
                        k = t % 2
                        ts_ = slice(t * 128, (t + 1) * 128)
                        pT = pbf(3)
                        kb.pe([(lambda e, c=c: e.transpose(out=pT[:, c * 128:(c + 1) * 128], in_=mgb[k][:, c * 128:(c + 1) * 128], identity=identb[:]))
                               for c in range(KC)], rd=[mg_r[k], c_r], wr=[pr[3]])
                        kb.op("act", lambda e: e.activation(out=mT[k][:].rearrange("p c n -> p (c n)"), in_=pT, func=AF.Copy), rd=[pr[3]], wr=[mT_r[k]])
                        for ch in range(2):
                            cs_ = slice(ch * 512, (ch + 1) * 512)
                            bank = 4 + ch
                            kb.pe([(lambda e, c=c, bank=bank, cs_=cs_: e.matmul(pb[bank][:], lhsT=mT[k][:, c, :], rhs=wo[:, c, cs_], start=(c == 0), stop=(c == KC - 1)))
                                   for c in range(KC)], rd=[mT_r[k], mw_r], wr=[pr[bank]])
                            kb.op("dve", lambda e, bank=bank, cs_=cs_: e.tensor_tensor(out=ot[k][:, cs_], in0=pb[bank][:], in1=xt[k][:, cs_], op=ALU.add),
                                  rd=[pr[bank], in_r[k]], wr=[ot_r[k]])
                        kb.dma(xdst[ts_, :], ot[k][:], st2_ds[k], rd=[ot_r[k]], wr=[x_new])

                    m_s1(0)
                    for t in range(NT):
                        if t + 1 < NT:
                            m_s1(t + 1)
                        m_s2(t)
                    xr[0] = x_new
                kb.barrier()
        kb.barrier()
    print("instructions:", kb.nins, flush=True)
    return nc


_CACHE = {}


def kernel(**inputs):
    S = 4096
    NSEQ = 2
    DEPTH = 4
    NCORE = 8
    key = (S, NSEQ, DEPTH)
    if key not in _CACHE:
        _CACHE[key] = build(S, NSEQ, DEPTH)
    nc = _CACHE[key]
    xp = np.asarray(inputs["x_prompt"], dtype=np.float32)
    xs = np.asarray(inputs["x_sample"], dtype=np.float32)
    cf, rope, sm = host_consts(S)
    shared = {
        "norm_g": inputs["norm_g"], "w_in": inputs["w_in"], "conv_w": inputs["conv_w"],
        "a_log": np.asarray(inputs["a_log"]).reshape(DEPTH, 8), "dt_bias": np.asarray(inputs["dt_bias"]).reshape(DEPTH, 8),
        "gdn_norm_g": inputs["gdn_norm_g"], "hgrn_lb_logits": inputs["hgrn_lb_logits"], "hgrn_norm_g": inputs["hgrn_norm_g"],
        "q_norm_g": np.asarray(inputs["q_norm_g"]).reshape(DEPTH, 128), "k_norm_g": np.asarray(inputs["k_norm_g"]).reshape(DEPTH, 128),
        "diff_lambda": np.asarray(inputs["diff_lambda"]).reshape(DEPTH, 256), "subln_g": inputs["subln_g"],
        "w_br_a": inputs["w_br_a"], "w_br_b": inputs["w_br_b"], "w_br_c": inputs["w_br_c"], "w_out": inputs["w_out"],
        "cst_f": cf, "cst_rope": rope, "cst_smask": sm,
    }
    shared = {k: np.ascontiguousarray(np.asarray(v, dtype=np.float32)) for k, v in shared.items()}
    in_maps = []
    for c in range(NCORE):
        xin = np.ascontiguousarray(np.stack([xp[c], xs[c % 4]], axis=0))
        m = dict(shared)
        m["xin"] = xin
        in_maps.append(m)
    res = run_bass_kernel_spmd(nc, in_maps, core_ids=list(range(NCORE)))
    y_prompt = np.stack([np.asarray(res.results[c]["yout"][0]) for c in range(NCORE)], axis=0).astype(np.float32)
    y_sample = np.stack([np.asarray(res.results[c]["yout"][1]) for c in range(4)], axis=0).astype(np.float32)
    return (y_prompt, y_sample)
```

```python
import math
from contextlib import ExitStack

import numpy as np
import concourse.bass as bass
import concourse.mybir as mybir
from concourse.bass_utils import run_bass_kernel_spmd

F32 = mybir.dt.float32
BF16 = mybir.dt.bfloat16
AF = mybir.ActivationFunctionType
ALU = mybir.AluOpType
AX = mybir.AxisListType

D = 1024
NIN = 9744
KC = 8
EPS = 1e-6
OFF = dict(a_q=0, a_k=512, a_v=1024, a_z=1536, a_b=2048, a_a=2056, b_q=2064, b_i=2576, b_f=3088,
           b_z=4112, c_q=4624, c_k=5136, c_v=5648, c_z=6160, gate=6672)
NEG = -30000.0
ROPE_THETA = 500000.0


class Res:
    __slots__ = ("w", "rd")

    def __init__(self):
        self.w = None
        self.rd = {}


class DS:
    def __init__(self, sem, name):
        self.sem = sem
        self.cnt = 0
        self.name = name


class KB:
    def __init__(self, nc, es):
        self.nc = nc
        self.es = es
        self.eng = {"pe": nc.tensor, "dve": nc.vector, "act": nc.scalar, "pool": nc.gpsimd, "sp": nc.sync}
        self.sem = {k: es.enter_context(nc.semaphore("s_" + k)) for k in ("pe", "dve", "act", "pool")}
        self.cnt = {k: 0 for k in self.sem}
        self.waited = {k: {} for k in self.eng}
        self.dss = []
        self.nins = 0

    def ds(self, name):
        d = DS(self.es.enter_context(self.nc.semaphore(name)), name)
        self.dss.append(d)
        return d

    def _need(self, e, toks):
        wd = self.waited[e]
        for (key, sem, val) in toks:
            if e == "pe" and key == "pe":
                continue
            if wd.get(key, 0) >= val:
                continue
            self.eng[e].wait_ge(sem, val)
            wd[key] = val

    @staticmethod
    def _deps(rd, wr):
        toks = []
        for r in rd:
            if r.w is not None:
                toks.append(r.w)
        for w in wr:
            if w.w is not None:
                toks.append(w.w)
            toks.extend(w.rd.values())
        return toks

    @staticmethod
    def _mark(tok, rd, wr):
        for r in rd:
            r.rd[tok[0]] = tok
        for w in wr:
            w.w = tok
            w.rd = {}

    def op(self, e, fn, rd=(), wr=()):
        self._need(e, self._deps(rd, wr))
        ins = fn(self.eng[e])
        self.cnt[e] += 1
        ins.then_inc(self.sem[e], 1)
        self._mark((e, self.sem[e], self.cnt[e]), rd, wr)
        self.nins += 1

    def pe(self, fns, rd=(), wr=()):
        self._need("pe", self._deps(rd, wr))
        ins = None
        for f in fns:
            ins = f(self.nc.tensor)
            self.nins += 1
        self.cnt["pe"] += 1
        ins.then_inc(self.sem["pe"], 1)
        self._mark(("pe", self.sem["pe"], self.cnt["pe"]), rd, wr)

    def dma(self, out, in_, ds, rd=(), wr=(), q="sp", **kw):
        self._need(q, self._deps(rd, wr))
        ins = self.eng[q].dma_start(out=out, in_=in_, **kw)
        ds.cnt += 16
        ins.then_inc(ds.sem, 16)
        self._mark((ds.name, ds.sem, ds.cnt), rd, wr)
        self.nins += 1

    def barrier(self):
        toks = [(k, self.sem[k], self.cnt[k]) for k in self.sem if self.cnt[k] > 0]
        toks += [(d.name, d.sem, d.cnt) for d in self.dss if d.cnt > 0]
        for e in self.eng:
            self._need(e, toks)


def host_consts(S):
    NT = S // 128
    j = np.arange(128)[:, None]
    i = np.arange(128)[None, :]
    c = {}
    c["identf"] = np.eye(128, dtype=np.float32)
    c["onesf"] = np.ones((128, 128), np.float32)
    c["uincl"] = (j <= i).astype(np.float32)
    c["uinclT"] = (j >= i).astype(np.float32)
    c["negJI_f"] = np.where(i >= j, 0.0, NEG).astype(np.float32)
    c["negJI_b"] = np.where(i <= j, 0.0, NEG).astype(np.float32)
    c["negIJ_f"] = np.where(j > i, 0.0, NEG).astype(np.float32).T.copy()
    c["negIJ_b"] = np.where(j < i, 0.0, NEG).astype(np.float32).T.copy()
    pi = np.arange(128)[:, None]
    fj = np.arange(128)[None, :]
    c["negIJ_f"] = np.where(pi > fj, 0.0, NEG).astype(np.float32)
    c["negIJ_b"] = np.where(pi < fj, 0.0, NEG).astype(np.float32)
    same = (j // 32) == (i // 32)
    c["bd_f"] = (same & (i >= j)).astype(np.float32)
    c["bd_b"] = (same & (i <= j)).astype(np.float32)
    rm4 = np.zeros((128, 128), np.float32)
    for q_ in range(4):
        rm4[q_ * 32:(q_ + 1) * 32, q_] = 1.0
    c["rm4"] = rm4
    cf = np.concatenate([c[k] for k in CF_NAMES], axis=1)
    inv = 1.0 / (ROPE_THETA ** (np.arange(0, 16, 2, dtype=np.float32) / 16.0))
    pos = np.arange(S, dtype=np.float32)
    ang = (pos[:, None] * inv[None, :]).astype(np.float32)
    cs = np.cos(ang).astype(np.float32).reshape(NT, 128, 8).transpose(1, 0, 2)
    sn = np.sin(ang).astype(np.float32).reshape(NT, 128, 8).transpose(1, 0, 2)
    rope = np.ascontiguousarray(np.stack([cs, sn], axis=2)).reshape(128, NT * 2 * 8)
    sm = np.ones((128, S), np.float32)
    sm[:, ::32] = 0.0
    return np.ascontiguousarray(cf), np.ascontiguousarray(rope), sm


CF_NAMES = ("identf", "onesf", "uincl", "uinclT", "negJI_f", "negJI_b", "negIJ_f", "negIJ_b", "bd_f", "bd_b", "rm4")


def build(S, NSEQ, DEPTH, dbg=False, skip=()):
    NT = S // 128
    NB = S // 512
    NC32 = S // 32
    nc = bass.Bass("TRN2", target_bir_lowering=False)

    def din(name, shape, dt=F32):
        return nc.dram_tensor(name, list(shape), dt, kind="ExternalInput").ap()

    xin = din("xin", [NSEQ, S, D])
    norm_g = din("norm_g", [DEPTH, D])
    w_in = din("w_in", [DEPTH, D, NIN])
    conv_w = din("conv_w", [DEPTH, 5, 1536])
    a_log = din("a_log", [DEPTH, 8])
    dt_bias = din("dt_bias", [DEPTH, 8])
    gdn_g = din("gdn_norm_g", [DEPTH, 128])
    lb_logits = din("hgrn_lb_logits", [2, DEPTH, 512])
    hgrn_g = din("hgrn_norm_g", [DEPTH, 128])
    qn_g = din("q_norm_g", [DEPTH, 128])
    kn_g = din("k_norm_g", [DEPTH, 128])
    dlam = din("diff_lambda", [DEPTH, 256])
    subln_g = din("subln_g", [DEPTH, 128])
    w_br = [din("w_br_a", [DEPTH, 512, D]), din("w_br_b", [DEPTH, 512, D]), din("w_br_c", [DEPTH, 512, D])]
    w_out = din("w_out", [DEPTH, D, D])
    cf_d = din("cst_f", [128, 128 * len(CF_NAMES)])
    rope_d = din("cst_rope", [128, NT * 16])
    smask_d = din("cst_smask", [128, S])
    yout = nc.dram_tensor("yout", [NSEQ, S, D], F32, kind="ExternalOutput").ap()
    yT_d = nc.dram_tensor("yT_scr", [3, 512, S], BF16, kind="Internal").ap()
    gate_d = nc.dram_tensor("gate_scr", [S, 3 * D], BF16, kind="Internal").ap()
    xres_d = nc.dram_tensor("xres_scr", [S, D], F32, kind="Internal").ap()
    dbg_d = None
    if dbg:
        dbg_d = nc.dram_tensor("dbg_yT", [3, 512, S], BF16, kind="ExternalOutput").ap()

    es = ExitStack()
    with es:
        kb = KB(nc, es)

        uniq = [0]

        def sb(name, shape, dt, stack=es):
            uniq[0] += 1
            return stack.enter_context(nc.sbuf_tensor("%s_%d" % (name, uniq[0]), list(shape), dt))

        import contextlib

        @contextlib.contextmanager
        def phase():
            with ExitStack() as st_:
                yield st_
                kb.barrier()

        dumped = set()

        dcur = {"l": 0, "dl": int(dbg) - 1}

        def dump(name, ap, r, dt):
            if not dbg or name in dumped or dcur["l"] != dcur["dl"]:
                return
            dumped.add(name)
            shp = list(ap.shape)
            dd = nc.dram_tensor("dbg_" + name, shp, dt, kind="ExternalOutput").ap()
            kb.dma(dd, ap, st_ds, rd=[r], wr=[Res()])

        cf = sb("cf", [128, len(CF_NAMES), 128], F32)
        C = {n: cf[:, k, :] for k, n in enumerate(CF_NAMES)}
        identb = sb("identb", [128, 128], BF16)
        onesb = sb("onesb", [128, 128], BF16)
        epsc = sb("epsc", [128, 1], F32)
        hT = sb("hT", [128, KC, S], BF16)
        hT_r = Res()
        stg = sb("stg", [128, KC, 512], F32)
        stg_r = Res()
        wbf = [sb("wbf%d" % k, [128, KC, 512], BF16) for k in range(2)]
        wbf_r = [Res(), Res()]
        ld_ds = kb.ds("ld")
        c_r = Res()
        c_ds = kb.ds("cst")
        st_ds = kb.ds("st")
        st2_ds = [kb.ds("st2_0"), kb.ds("st2_1")]
        c2_ds = kb.ds("cst2")
        xl_ds = [kb.ds("xl0"), kb.ds("xl1")]
        ml_ds = [[kb.ds("ml%d_%d" % (k, i)) for i in range(3)] for k in range(2)]
        gcol = sb("gcol", [128, KC, 1], F32)
        convw = sb("convw", [128, 12, 5], F32)
        gatec = sb("gatec", [128, 16], F32)
        vecs = sb("vecs", [128, 8], F32)
        qkg = sb("qkg", [128, 2, 128], F32)
        lamt = sb("lamt", [128, 256], F32)
        lbt = sb("lbt", [128, 2, DEPTH, 4], F32)
        oml = sb("oml", [128, DEPTH, 2, 4], F32)
        par_r = Res()
        smallt = sb("smallt", [128, 64], F32)
        small_r = Res()

        pbig = es.enter_context(nc.psum_tensor("pbig", [128, 4096], F32))
        pb = [pbig[:, k * 512:(k + 1) * 512] for k in range(8)]
        pr = [Res() for _ in range(8)]

        def pbf(k):
            return pb[k][:].bitcast(BF16)

        wslot = [0]

        def load_w(src, kc, ncols, fold=False):
            s = wslot[0]
            wslot[0] ^= 1
            kb.dma(stg[:, 0:kc, 0:ncols], src.rearrange("(c p) n -> p c n", p=128), ld_ds, wr=[stg_r])
            dst = wbf[s][:, 0:kc, 0:ncols]
            if fold:
                kb.op("pool", lambda e: e.tensor_tensor(out=dst, in0=stg[:, 0:kc, 0:ncols],
                                                        in1=gcol[:, 0:kc, :].to_broadcast([128, kc, ncols]),
                                                        op=ALU.mult),
                      rd=[stg_r, par_r], wr=[wbf_r[s]])
            else:
                kb.op("pool", lambda e: e.tensor_copy(out=dst, in_=stg[:, 0:kc, 0:ncols]), rd=[stg_r], wr=[wbf_r[s]])
            return dst, wbf_r[s]

        def load_win(l, col0, ncols):
            return load_w(w_in[l, :, col0:col0 + ncols], KC, ncols, fold=True)

        def projF(out_ps, out_r, wt, wr_, c0, m, tok0, ntok):
            kb.pe([(lambda e, c=c: e.matmul(out_ps, lhsT=wt[:, c, c0:c0 + m], rhs=hT[:, c, tok0:tok0 + ntok],
                                            start=(c == 0), stop=(c == KC - 1))) for c in range(KC)],
                  rd=[wr_, hT_r], wr=[out_r])

        def projT(out_ps, out_r, wt, wr_, c0, n, tok0, ntok=128):
            kb.pe([(lambda e, c=c: e.matmul(out_ps, lhsT=hT[:, c, tok0:tok0 + ntok], rhs=wt[:, c, c0:c0 + n],
                                            start=(c == 0), stop=(c == KC - 1))) for c in range(KC)],
                  rd=[wr_, hT_r], wr=[out_r])

        def rstd_from(out_ap, in_ap, scale, rd, wr, tmp_ap):
            shp = list(in_ap.shape)
            bias = epsc[0:shp[0], 0:1]
            kb.op("act", lambda e: e.activation(out=tmp_ap, in_=in_ap, func=AF.Ln, scale=scale, bias=bias), rd=rd, wr=wr)
            kb.op("act", lambda e: e.activation(out=out_ap, in_=tmp_ap, func=AF.Exp, scale=-0.5), rd=wr, wr=wr)

        kb.dma(cf[:].rearrange("p k n -> p (k n)"), cf_d[:, :], c_ds, wr=[c_r])
        kb.op("pool", lambda e: e.tensor_copy(out=identb[:], in_=C["identf"]), rd=[c_r], wr=[c_r])
        kb.op("pool", lambda e: e.memset(onesb[:], 1.0), wr=[c_r])
        kb.op("pool", lambda e: e.memset(epsc[:], EPS), wr=[c_r])
        with nc.allow_non_contiguous_dma("tiny param loads"):
            for d_ in range(2):
                for l_ in range(DEPTH):
                    kb.dma(lbt[:, d_, l_, :], lb_logits[d_, l_].rearrange("(h k) -> k h", k=128), c2_ds, wr=[small_r])
        kb.op("act", lambda e: e.activation(out=lbt[:], in_=lbt[:], func=AF.Exp), rd=[small_r], wr=[small_r])
        den = smallt[:, 0:8].rearrange("p (d h) -> p d h", d=2)
        num = smallt[:, 8:16].rearrange("p (d h) -> p d h", d=2)
        kb.op("dve", lambda e: e.tensor_reduce(out=den, in_=lbt[:].rearrange("p d l h -> p d h l"), axis=AX.X, op=ALU.add),
              rd=[small_r], wr=[small_r])
        kb.op("dve", lambda e: e.reciprocal(out=den, in_=den), rd=[small_r], wr=[small_r])
        for l in range(DEPTH):
            if l == 0:
                kb.op("dve", lambda e: e.memset(oml[:, 0, :, :], 1.0), wr=[small_r])
            else:
                kb.op("dve", lambda e, l=l: e.tensor_reduce(out=num, in_=lbt[:, :, 1:l + 1, :].rearrange("p d l h -> p d h l"),
                                                            axis=AX.X, op=ALU.add), rd=[small_r], wr=[small_r])
                kb.op("dve", lambda e: e.tensor_tensor(out=num, in0=num, in1=den, op=ALU.mult), rd=[small_r], wr=[small_r])
                kb.op("dve", lambda e, l=l: e.tensor_scalar(out=oml[:, l, :, :], in0=num, scalar1=-1.0, scalar2=1.0,
                                                            op0=ALU.mult, op1=ALU.add), rd=[small_r], wr=[small_r])
        kb.barrier()

        def epilogue(oT_blk, o_r, gidx, zcol0, l, dst_rows, b, tmp, tmp_r, wz, wz_r):
            sq, rs, zg, yb = tmp["sq"], tmp["rs"], tmp["zg"], tmp["yb"]
            kb.op("act", lambda e: e.activation(out=sq[:], in_=oT_blk, func=AF.Square), rd=[o_r], wr=[tmp_r])
            kb.pe([lambda e: e.matmul(pb[6][:], lhsT=onesb[:], rhs=sq[:], start=True, stop=True)], rd=[tmp_r, c_r], wr=[pr[6]])
            kb.op("act", lambda e: e.activation(out=rs[:], in_=pb[6][:], func=AF.Ln, scale=1.0 / 128, bias=epsc[:, 0:1]),
                  rd=[pr[6], c_r], wr=[tmp_r])
            kb.op("act", lambda e: e.activation(out=rs[:], in_=rs[:], func=AF.Exp, scale=-0.5), rd=[tmp_r], wr=[tmp_r])
            projF(pb[7][:], pr[7], wz, wz_r, 0, 128, b * 512, 512)
            kb.op("act", lambda e: e.activation(out=zg[:], in_=pb[7][:], func=AF.Silu), rd=[pr[7]], wr=[tmp_r])
            kb.op("dve", lambda e: e.tensor_tensor(out=rs[:], in0=rs[:], in1=oT_blk, op=ALU.mult), rd=[tmp_r, o_r], wr=[tmp_r])
            kb.op("dve", lambda e: e.scalar_tensor_tensor(out=yb[:], in0=rs[:], scalar=vecs[:, gidx:gidx + 1], in1=zg[:],
                                                          op0=ALU.mult, op1=ALU.mult), rd=[tmp_r, par_r], wr=[tmp_r])
            kb.dma(dst_rows[:, b * 512:(b + 1) * 512], yb[:], st_ds, rd=[tmp_r], wr=[yT_r])

        yT_r = Res()
        gate_r = Res()
        xr = [Res()]

        for s in range(NSEQ):
            for l in range(DEPTH):
                dcur["l"] = l
                xsrc = xin[s] if l == 0 else xres_d
                xdst = yout[s] if l == DEPTH - 1 else xres_d
                lam_init = 0.8 - 0.6 * math.exp(-0.3 * l)
                with nc.allow_non_contiguous_dma("tiny param loads"):
                    kb.dma(gcol[:], norm_g[l].rearrange("(c p o) -> p c o", p=128, o=1), c_ds, wr=[par_r])
                    for j in range(5):
                        kb.dma(convw[:, :, j], conv_w[l, j].rearrange("(g p) -> p g", p=128), c_ds, wr=[par_r])
                    kb.dma(vecs[:, 0:1], gdn_g[l].rearrange("(p o) -> p o", o=1), c_ds, wr=[par_r])
                    kb.dma(vecs[:, 1:2], hgrn_g[l].rearrange("(p o) -> p o", o=1), c_ds, wr=[par_r])
                    kb.dma(vecs[:, 2:3], subln_g[l].rearrange("(p o) -> p o", o=1), c_ds, wr=[par_r])
                kb.dma(gatec[:, 0:8], a_log[l].partition_broadcast(128), c_ds, wr=[par_r])
                kb.dma(gatec[:, 8:16], dt_bias[l].partition_broadcast(128), c_ds, wr=[par_r])
                kb.dma(qkg[:, 0, :], qn_g[l].partition_broadcast(128), c_ds, wr=[par_r])
                kb.dma(qkg[:, 1, :], kn_g[l].partition_broadcast(128), c_ds, wr=[par_r])
                kb.dma(lamt[:], dlam[l].partition_broadcast(128), c_ds, wr=[par_r])
                kb.op("act", lambda e: e.activation(out=gatec[:, 0:8], in_=gatec[:, 0:8], func=AF.Exp), rd=[par_r], wr=[par_r])
                kb.op("dve", lambda e: e.tensor_scalar(out=gatec[:, 0:8], in0=gatec[:, 0:8], scalar1=-1.0, scalar2=None,
                                                       op0=ALU.mult), rd=[par_r], wr=[par_r])
                kb.op("dve", lambda e: e.tensor_scalar(out=qkg[:, 0, :], in0=qkg[:, 0, :], scalar1=0.125, scalar2=None,
                                                       op0=ALU.mult), rd=[par_r], wr=[par_r])
                kb.op("dve", lambda e: e.tensor_scalar(out=vecs[:, 2:3], in0=vecs[:, 2:3], scalar1=1.0 - lam_init,
                                                       scalar2=None, op0=ALU.mult), rd=[par_r], wr=[par_r])
                l4 = lamt[:].rearrange("p (a b d) -> p a b d", a=2, b=2)
                pr2 = smallt[:, 16:18]
                kb.op("dve", lambda e: e.tensor_tensor(out=l4[:, :, 0, :], in0=l4[:, :, 0, :], in1=l4[:, :, 1, :], op=ALU.mult),
                      rd=[par_r], wr=[par_r])
                kb.op("dve", lambda e: e.tensor_reduce(out=pr2, in_=l4[:, :, 0, :], axis=AX.X, op=ALU.add),
                      rd=[par_r], wr=[small_r])
                kb.op("act", lambda e: e.activation(out=pr2, in_=pr2, func=AF.Exp), rd=[small_r], wr=[small_r])
                kb.op("dve", lambda e: e.tensor_tensor(out=smallt[:, 18:19], in0=smallt[:, 17:18], in1=smallt[:, 16:17],
                                                       op=ALU.subtract), rd=[small_r], wr=[small_r])
                kb.op("dve", lambda e: e.tensor_scalar(out=vecs[:, 3:4], in0=smallt[:, 18:19], scalar1=-lam_init, scalar2=None,
                                                       op0=ALU.add), rd=[small_r], wr=[par_r])

                with phase() as ph:
                    xt = [sb("xt%d" % k, [128, D], F32, ph) for k in range(2)]
                    xt_r = [Res(), Res()]
                    hb = [sb("hb%d" % k, [128, D], BF16, ph) for k in range(2)]
                    hb_r = [Res(), Res()]
                    junk = sb("junk", [128, D], BF16, ph)
                    st = sb("nst", [128, 2, 4], F32, ph)
                    st_r = [Res(), Res()]
                    for t in range(NT):
                        k = t % 2
                        kb.dma(xt[k][:], xsrc[t * 128:(t + 1) * 128, :], xl_ds[k], rd=[xr[0]], wr=[xt_r[k]])
                        kb.op("act", lambda e, k=k: e.activation(out=junk[:], in_=xt[k][:], func=AF.Square,
                                                                 accum_out=st[:, k, 0:1]), rd=[xt_r[k]], wr=[st_r[k]])
                        rstd_from(st[:, k, 1:2], st[:, k, 0:1], 1.0 / D, [st_r[k], c_r], [st_r[k]], st[:, k, 2:3])
                        kb.op("dve", lambda e, k=k: e.tensor_scalar(out=hb[k][:], in0=xt[k][:], scalar1=st[:, k, 1:2],
                                                                    scalar2=None, op0=ALU.mult),
                              rd=[xt_r[k], st_r[k]], wr=[hb_r[k]])
                        pT = pbf(k)
                        kb.pe([(lambda e, c=c, k=k, pT=pT: e.transpose(out=pT[:, c * 128:(c + 1) * 128],
                                                                       in_=hb[k][:, c * 128:(c + 1) * 128],
                                                                       identity=identb[:])) for c in range(KC)],
                              rd=[hb_r[k], c_r], wr=[pr[k]])
                        kb.op("act", lambda e, t=t, pT=pT: e.activation(out=hT[:, :, t * 128:(t + 1) * 128],
                                                                        in_=pT.rearrange("p (c n) -> p c n", c=KC),
                                                                        func=AF.Copy), rd=[pr[k]], wr=[hT_r])
                kb.barrier()

                dump("hT", hT[:].rearrange("p c n -> p (c n)"), hT_r, BF16)
                with phase() as ph:
                  if 'G' not in skip:
                    gsb = [sb("gsb%d" % k, [128, 512], BF16, ph) for k in range(2)]
                    gsb_r = [Res(), Res()]
                    it = 0
                    for cb in range(6):
                        wt, wt_r = load_win(l, OFF["gate"] + cb * 512, 512)
                        for t in range(NT):
                            k = it % 2
                            it += 1
                            projT(pb[k][:], pr[k], wt, wt_r, 0, 512, t * 128)
                            kb.op("act", lambda e, k=k: e.activation(out=gsb[k][:], in_=pb[k][:], func=AF.Sigmoid),
                                  rd=[pr[k]], wr=[gsb_r[k]])
                            kb.dma(gate_d[t * 128:(t + 1) * 128, cb * 512:(cb + 1) * 512], gsb[k][:], st2_ds[k],
                                   rd=[gsb_r[k]], wr=[gate_r])
                kb.barrier()

                with phase() as ph:
                  if 'C' not in skip:
                    rope_t = sb("rope_t", [128, NT, 2, 8], F32, ph)
                    rp_r = Res()
                    kb.dma(rope_t[:].rearrange("p t a b -> p (t a b)"), rope_d[:, :], c_ds, wr=[rp_r])
                    qT = sb("c_qT", [128, 4, S], BF16, ph)
                    kT = sb("c_kT", [128, 4, S], BF16, ph)
                    qk_r = Res()
                    with phase() as ph2:
                        wqk = [load_win(l, OFF["c_q"] + which * 512, 512) for which in range(2)]

                        def prep_chain(which, par, bP, bT):
                            dstT = qT if which == 0 else kT
                            sfx = "_%d%d" % (which, par)
                            sq = sb("c_sq" + sfx, [128, 8, 64], F32, ph2)
                            qn = sb("c_qn" + sfx, [128, 8, 64], F32, ph2)
                            ssq = sb("c_ssq" + sfx, [128, 8, 3], F32, ph2)
                            rt = sb("c_rt" + sfx, [128, 4, 8, 8], F32, ph2)
                            qb16 = sb("c_qb16" + sfx, [128, 8, 64], BF16, ph2)
                            w_r = Res()
                            wt, wt_r = wqk[which]
                            for t in range(par, NT, 2):
                                projT(pb[bP][:], pr[bP], wt, wt_r, 0, 512, t * 128)
                                yield
                                p3 = pb[bP][:].rearrange("p (g d) -> p g d", g=8)
                                kb.op("act", lambda e, p3=p3: e.activation(out=sq[:], in_=p3, func=AF.Square), rd=[pr[bP]], wr=[w_r])
                                yield
                                kb.op("dve", lambda e: e.tensor_reduce(out=ssq[:, :, 0], in_=sq[:], axis=AX.X, op=ALU.add),
                                      rd=[w_r], wr=[w_r])
                                yield
                                kb.op("act", lambda e: e.activation(out=ssq[:, :, 2], in_=ssq[:, :, 0], func=AF.Ln, scale=1.0 / 64, bias=epsc[:, 0:1]),
                                      rd=[w_r, c_r], wr=[w_r])
                                yield
                                kb.op("act", lambda e: e.activation(out=ssq[:, :, 1], in_=ssq[:, :, 2], func=AF.Exp, scale=-0.5), rd=[w_r], wr=[w_r])
                                yield
                                kb.op("dve", lambda e, p3=p3: e.tensor_tensor(out=qn[:], in0=p3,
                                                                              in1=ssq[:, :, 1:2].to_broadcast([128, 8, 64]),
                                                                              op=ALU.mult), rd=[pr[bP], w_r], wr=[w_r])
                                yield
                                g2 = qkg[:, which, :].rearrange("p (m d) -> p m d", m=2)
                                q4 = qn[:].rearrange("p (h m) d -> p h m d", h=4)
                                kb.op("dve", lambda e, g2=g2, q4=q4: e.tensor_tensor(
                                    out=q4, in0=q4, in1=g2.unsqueeze(1).to_broadcast([128, 4, 2, 64]), op=ALU.mult),
                                    rd=[w_r, par_r], wr=[w_r])
                                yield
                                cs = rope_t[:, t, 0:1, :].to_broadcast([128, 8, 8])
                                sn = rope_t[:, t, 1:2, :].to_broadcast([128, 8, 8])
                                x1 = qn[:, :, 0:8]
                                x2 = qn[:, :, 8:16]
                                kb.op("pool", lambda e: e.tensor_tensor(out=rt[:, 0], in0=x1, in1=cs, op=ALU.mult), rd=[w_r, rp_r], wr=[w_r])
                                kb.op("pool", lambda e: e.tensor_tensor(out=rt[:, 1], in0=x2, in1=sn, op=ALU.mult), rd=[w_r, rp_r], wr=[w_r])
                                kb.op("pool", lambda e: e.tensor_tensor(out=rt[:, 2], in0=x2, in1=cs, op=ALU.mult), rd=[w_r, rp_r], wr=[w_r])
                                kb.op("pool", lambda e: e.tensor_tensor(out=rt[:, 3], in0=x1, in1=sn, op=ALU.mult), rd=[w_r, rp_r], wr=[w_r])
                                yield
                                kb.op("dve", lambda e: e.tensor_copy(out=qb16[:, :, 16:64], in_=qn[:, :, 16:64]), rd=[w_r], wr=[w_r])
                                kb.op("dve", lambda e: e.tensor_tensor(out=qb16[:, :, 0:8], in0=rt[:, 0], in1=rt[:, 1], op=ALU.subtract),
                                      rd=[w_r], wr=[w_r])
                                kb.op("dve", lambda e: e.tensor_tensor(out=qb16[:, :, 8:16], in0=rt[:, 2], in1=rt[:, 3], op=ALU.add),
                                      rd=[w_r], wr=[w_r])
                                yield
                                pT = pbf(bT)
                                q2 = qb16[:].rearrange("p g d -> p (g d)")
                                kb.pe([(lambda e, h=h, pT=pT, q2=q2: e.transpose(out=pT[:, h * 128:(h + 1) * 128],
                                                                                 in_=q2[:, h * 128:(h + 1) * 128],
                                                                                 identity=identb[:])) for h in range(4)],
                                      rd=[w_r, c_r], wr=[pr[bT]])
                                yield
                                kb.op("act", lambda e, t=t, pT=pT: e.activation(
                                    out=dstT[:, :, t * 128:(t + 1) * 128], in_=pT[:, 0:512].rearrange("p (h n) -> p h n", h=4),
                                    func=AF.Copy), rd=[pr[bT]], wr=[qk_r])

                        gens = [prep_chain(0, 0, 0, 1), prep_chain(1, 0, 2, 3), prep_chain(0, 1, 4, 5), prep_chain(1, 1, 6, 7)]
                        while gens:
                            for gen in list(gens):
                                try:
                                    next(gen)
                                except StopIteration:
                                    gens.remove(gen)
                    for h in range(4):
                        with phase() as ph2:
                            vtm = sb("c_vtm", [128, NT, 128], BF16, ph2)
                            v_r = Res()
                            pt = [sb("c_p%d" % k, [128, 1024], BF16, ph2) for k in range(2)]
                            pt_r = [Res() for _ in range(2)]
                            pacc = sb("c_pacc", [128, 512], F32, ph2)
                            pacc_r = Res()
                            e0 = sb("c_e0", [128, 512], F32, ph2)
                            e1 = sb("c_e1", [128, 512], F32, ph2)
                            ot = sb("c_ot", [128, 512], F32, ph2)
                            ot_r = Res()
                            tmp = dict(sq=sb("c_esq", [128, 512], BF16, ph2), rs=sb("c_ers", [128, 512], F32, ph2),
                                       zg=sb("c_ezg", [128, 512], F32, ph2), yb=sb("c_eyb", [128, 512], BF16, ph2))
                            tmp_r = Res()
                            wzt = sb("c_wz", [128, KC, 128], BF16, ph2)
                            wz_r = Res()
                            wt, wt_r = load_win(l, OFF["c_v"] + h * 128, 128)
                            for t4 in range(0, NT, 4):
                                for t in range(t4, t4 + 4):
                                    projT(pb[0][:, (t - t4) * 128:(t - t4 + 1) * 128], pr[0], wt, wt_r, 0, 128, t * 128)
                                kb.op("act", lambda e, t4=t4: e.activation(out=vtm[:, t4:t4 + 4, :],
                                                                           in_=pb[0][:].rearrange("p (t n) -> p t n", t=4),
                                                                           func=AF.Copy), rd=[pr[0]], wr=[v_r])
                            wt, wt_r = load_win(l, OFF["c_z"] + h * 128, 128)
                            kb.op("pool", lambda e, wt=wt: e.tensor_copy(out=wzt[:], in_=wt), rd=[wt_r], wr=[wz_r])
                            for b in range(NB):
                                def qk_pair(kt):
                                    kb.pe([(lambda e, m=m: e.matmul(
                                        pb[2 * (kt % 2) + m][:], lhsT=kT[m * 64:(m + 1) * 64, h, kt * 128:(kt + 1) * 128],
                                        rhs=qT[m * 64:(m + 1) * 64, h, b * 512:(b + 1) * 512], start=True, stop=True)) for m in range(2)],
                                        rd=[qk_r], wr=[pr[2 * (kt % 2)], pr[2 * (kt % 2) + 1]])

                                qk_pair(0)
                                for kt in range(NT):
                                    if kt + 1 < NT:
                                        qk_pair(kt + 1)
                                    kp = kt % 2
                                    kb.op("act", lambda e, kp=kp: e.activation(
                                        out=pt[kp][:], in_=pbig[:, kp * 1024:(kp + 1) * 1024], func=AF.Exp),
                                        rd=[pr[2 * kp], pr[2 * kp + 1]], wr=[pt_r[kp]])
                                    kb.pe([lambda e, kp=kp, kt=kt: e.matmul(pb[4][:], lhsT=vtm[:, kt, :], rhs=pt[kp][:, 0:512], start=(kt == 0), stop=(kt == NT - 1)),
                                           lambda e, kp=kp, kt=kt: e.matmul(pb[5][:], lhsT=vtm[:, kt, :], rhs=pt[kp][:, 512:1024], start=(kt == 0), stop=(kt == NT - 1)),
                                           lambda e, kp=kp, kt=kt: e.matmul(pb[6][:], lhsT=onesb[:], rhs=pt[kp][:, 0:512], start=(kt == 0), stop=(kt == NT - 1))],
                                          rd=[v_r, pt_r[kp], c_r], wr=[pr[4], pr[5], pr[6]])
                                    if kt == 0:
                                        kb.op("dve", lambda e, kp=kp: e.tensor_copy(out=pacc[:], in_=pt[kp][:, 512:1024]),
                                              rd=[pt_r[kp]], wr=[pacc_r])
                                    else:
                                        kb.op("dve", lambda e, kp=kp: e.tensor_tensor(out=pacc[:], in0=pacc[:], in1=pt[kp][:, 512:1024], op=ALU.add),
                                              rd=[pt_r[kp], pacc_r], wr=[pacc_r])
                                kb.pe([lambda e: e.matmul(pb[7][:], lhsT=C["onesf"], rhs=pacc[:], start=True, stop=True)],
                                      rd=[pacc_r, c_r], wr=[pr[7]])
                                kb.op("dve", lambda e: e.reciprocal(out=e0[:], in_=pb[6][:]), rd=[pr[6]], wr=[ot_r])
                                kb.op("dve", lambda e: e.reciprocal(out=e1[:], in_=pb[7][:]), rd=[pr[7]], wr=[ot_r])
                                kb.op("dve", lambda e: e.tensor_tensor(out=e0[:], in0=e0[:], in1=pb[4][:], op=ALU.mult), rd=[pr[4], ot_r], wr=[ot_r])
                                kb.op("dve", lambda e: e.tensor_tensor(out=e1[:], in0=e1[:], in1=pb[5][:], op=ALU.mult), rd=[pr[5], ot_r], wr=[ot_r])
                                kb.op("dve", lambda e: e.scalar_tensor_tensor(out=ot[:], in0=e1[:], scalar=vecs[:, 3:4], in1=e0[:],
                                                                              op0=ALU.mult, op1=ALU.add), rd=[ot_r, par_r], wr=[ot_r])
                                epilogue(ot[:], ot_r, 2, OFF["c_z"] + h * 128, l, yT_d[2, h * 128:(h + 1) * 128, :], b, tmp, tmp_r, wzt[:], wz_r)
                kb.barrier()

                with phase() as ph:
                  if 'B' not in skip:
                    qT = sb("b_qT", [128, S], BF16, ph)
                    q_r = Res()
                    vtm = sb("b_vtm", [128, NT, 128], BF16, ph)
                    v_r = Res()
                    oT = sb("b_oT", [128, S], F32, ph)
                    o_r = Res()
                    qtmp = sb("b_qtmp", [128, 512], F32, ph)
                    qtmp_r = Res()
                    smask = sb("b_smask", [128, 512], F32, ph)
                    sm_r = Res()
                    tmp = dict(sq=sb("b_esq", [128, 512], BF16, ph), rs=sb("b_ers", [128, 512], F32, ph),
                               zg=sb("b_ezg", [128, 512], F32, ph), yb=sb("b_eyb", [128, 512], BF16, ph))
                    tmp_r = Res()
                    wzt = sb("b_wz", [128, KC, 128], BF16, ph)
                    wz_r = Res()
                    kb.dma(smask[:], smask_d[:, 0:512], c_ds, wr=[sm_r])
                    CH = []
                    for d in range(2):
                        T = dict(
                            kkf=sb("b_kkf%d" % d, [128, 512], F32, ph), lfb=sb("b_lfb%d" % d, [128, 512], F32, ph),
                            gcb=sb("b_gcb%d" % d, [128, 512], F32, ph), a1=sb("b_a1%d" % d, [128, 512], F32, ph),
                            a2=sb("b_a2%d" % d, [128, 512], F32, ph), ex=sb("b_ex%d" % d, [128, 512], F32, ph),
                            qd=sb("b_qd%d" % d, [128, 512], BF16, ph), qm=sb("b_qm%d" % d, [128, 512], BF16, ph),
                            km=sb("b_km%d" % d, [128, 512], BF16, ph), kd=sb("b_kd%d" % d, [128, 512], BF16, ph),
                            atm=sb("b_atm%d" % d, [128, 128], BF16, ph), kdtm=sb("b_kdtm%d" % d, [128, 4, 128], BF16, ph),
                            Sb=sb("b_S%d" % d, [128, 4, 128], BF16, ph), egl=sb("b_egl%d" % d, [128, 16], F32, ph))
                        CH.append(T)
                    for h in range(4):
                        wt, wt_r = load_win(l, OFF["b_q"] + h * 128, 128)
                        for b in range(NB):
                            k = b % 2
                            projF(pb[k][:], pr[k], wt, wt_r, 0, 128, b * 512, 512)
                            kb.op("act", lambda e, k=k: e.activation(out=qtmp[:], in_=pb[k][:], func=AF.Silu),
                                  rd=[pr[k]], wr=[qtmp_r])
                            kb.op("dve", lambda e, b=b: e.tensor_scalar(out=qT[:, b * 512:(b + 1) * 512], in0=qtmp[:], scalar1=128.0 ** -0.5,
                                                                        scalar2=None, op0=ALU.mult), rd=[qtmp_r], wr=[q_r])
                        wt, wt_r = load_win(l, OFF["b_i"] + h * 128, 128)
                        for t4 in range(0, NT, 4):
                            for t in range(t4, t4 + 4):
                                projT(pb[2][:, (t - t4) * 128:(t - t4 + 1) * 128], pr[2], wt, wt_r, 0, 128, t * 128)
                            kb.op("act", lambda e, t4=t4: e.activation(out=vtm[:, t4:t4 + 4, :],
                                                                       in_=pb[2][:].rearrange("p (t n) -> p t n", t=4),
                                                                       func=AF.Copy), rd=[pr[2]], wr=[v_r])
                        wt, wt_r = load_win(l, OFF["b_z"] + h * 128, 128)
                        kb.op("pool", lambda e, wt=wt: e.tensor_copy(out=wzt[:], in_=wt), rd=[wt_r], wr=[wz_r])
                        kb.op("pool", lambda e: e.memset(oT[:], 0.0), wr=[o_r])
                        wfs = [load_win(l, OFF["b_f"] + d * 512 + h * 128, 128) for d in range(2)]

                        def bchain(d):
                            T = CH[d]
                            kkf, lfb, gcb, ex = T["kkf"], T["lfb"], T["gcb"], T["ex"]
                            qd, qm, km, kd, atm, kdtm, Sb, egl = T["qd"], T["qm"], T["km"], T["kd"], T["atm"], T["kdtm"], T["Sb"], T["egl"]
                            bA, bB, bC, bD = [4 * d + i for i in range(4)]
                            wt, wt_r = wfs[d]
                            lf_r, g_r, wk_r, blk_r, atm_r, kdtm_r = Res(), Res(), Res(), Res(), Res(), Res()
                            S_r = [Res() for _ in range(4)]
                            last = 31 if d == 0 else 0
                            bmask = C["bd_f"] if d == 0 else C["bd_b"]
                            kb.op("pool", lambda e: e.memset(Sb[:, 0, :], 0.0), wr=[S_r[0]])
                            scur = 0
                            g3 = gcb[:].rearrange("p (c n) -> p c n", n=32)
                            l3 = lfb[:].rearrange("p (c n) -> p c n", n=32)
                            a1 = T["a1"][:].rearrange("p (c n) -> p c n", n=32)
                            a2 = T["a2"][:].rearrange("p (c n) -> p c n", n=32)
                            for b in (range(NB) if d == 0 else range(NB - 1, -1, -1)):
                                sl = slice(b * 512, (b + 1) * 512)
                                projF(pb[bA][:], pr[bA], wt, wt_r, 0, 128, b * 512, 512)
                                yield
                                kb.op("act", lambda e: e.activation(out=kkf[:], in_=pb[bA][:], func=AF.Sigmoid, scale=-1.0),
                                      rd=[pr[bA]], wr=[lf_r])
                                kb.op("dve", lambda e: e.tensor_scalar(out=kkf[:], in0=kkf[:], scalar1=oml[:, l, d, h:h + 1],
                                                                       scalar2=None, op0=ALU.mult), rd=[lf_r, small_r], wr=[lf_r])
                                kb.op("dve", lambda e: e.tensor_scalar(out=lfb[:], in0=kkf[:], scalar1=1.0 - 1e-6,
                                                                       scalar2=None, op0=ALU.min), rd=[lf_r], wr=[lf_r])
                                kb.op("act", lambda e: e.activation(out=lfb[:], in_=lfb[:], func=AF.Ln, scale=-1.0, bias=1.0),
                                      rd=[lf_r], wr=[lf_r])
                                yield
                                kb.op("dve", lambda e: e.tensor_tensor_scan(out=gcb[:], data0=smask[:], data1=lfb[:], initial=0.0,
                                                                            op0=ALU.mult, op1=ALU.add), rd=[lf_r, sm_r], wr=[g_r])
                                if d == 1:
                                    kb.op("dve", lambda e: e.tensor_tensor(out=l3, in0=l3, in1=g3, op=ALU.subtract), rd=[g_r, lf_r], wr=[lf_r])
                                    kb.op("dve", lambda e: e.tensor_tensor(out=g3, in0=l3, in1=g3[:, :, 31:32].to_broadcast([128, 16, 32]),
                                                                           op=ALU.add), rd=[g_r, lf_r], wr=[g_r])
                                kb.op("act", lambda e: e.activation(out=egl[:], in_=g3[:, :, last], func=AF.Exp), rd=[g_r], wr=[g_r])
                                kb.op("pool", lambda e: e.tensor_tensor(out=a1, in0=g3, in1=g3[:, :, 16:17].to_broadcast([128, 16, 32]),
                                                                        op=ALU.subtract), rd=[g_r], wr=[wk_r])
                                kb.op("pool", lambda e: e.tensor_tensor(out=a2, in0=g3, in1=g3[:, :, last:last + 1].to_broadcast([128, 16, 32]),
                                                                        op=ALU.subtract), rd=[g_r], wr=[wk_r])
                                yield
                                kb.op("act", lambda e: e.activation(out=ex[:], in_=T["a1"][:], func=AF.Exp), rd=[wk_r], wr=[wk_r])
                                kb.op("dve", lambda e, sl=sl: e.tensor_tensor(out=qm[:], in0=ex[:], in1=qT[:, sl], op=ALU.mult),
                                      rd=[wk_r, q_r], wr=[blk_r])
                                kb.op("act", lambda e: e.activation(out=ex[:], in_=T["a1"][:], func=AF.Exp, scale=-1.0), rd=[wk_r], wr=[wk_r])
                                kb.op("dve", lambda e: e.tensor_tensor(out=km[:], in0=ex[:], in1=kkf[:], op=ALU.mult),
                                      rd=[wk_r, lf_r], wr=[blk_r])
                                yield
                                kb.op("act", lambda e: e.activation(out=ex[:], in_=gcb[:], func=AF.Exp), rd=[wk_r, g_r], wr=[wk_r])
                                kb.op("dve", lambda e, sl=sl: e.tensor_tensor(out=qd[:], in0=ex[:], in1=qT[:, sl], op=ALU.mult),
                                      rd=[wk_r, q_r], wr=[blk_r])
                                kb.op("act", lambda e: e.activation(out=ex[:], in_=T["a2"][:], func=AF.Exp, scale=-1.0), rd=[wk_r], wr=[wk_r])
                                kb.op("dve", lambda e: e.tensor_tensor(out=kd[:], in0=ex[:], in1=kkf[:], op=ALU.mult),
                                      rd=[wk_r, lf_r], wr=[blk_r])
                                yield
                                for gq in (range(4) if d == 0 else range(3, -1, -1)):
                                    t = b * 4 + gq
                                    gs = slice(gq * 128, (gq + 1) * 128)
                                    pT = pbf(bB)
                                    kb.pe([lambda e, gs=gs: e.matmul(pb[bA][:, 0:128], lhsT=km[:, gs], rhs=qm[:, gs], start=True, stop=True),
                                           lambda e, gs=gs, pT=pT: e.transpose(out=pT[:, 0:128], in_=kd[:, gs], identity=identb[:])],
                                          rd=[blk_r, c_r], wr=[pr[bA], pr[bB]])
                                    yield
                                    kb.op("dve", lambda e: e.tensor_tensor(out=atm[:], in0=pb[bA][:, 0:128], in1=bmask, op=ALU.mult),
                                          rd=[pr[bA], c_r], wr=[atm_r])
                                    for cq_ in range(4):
                                        kb.op("act", lambda e, pT=pT, cq_=cq_: e.activation(out=kdtm[:, cq_, :], in_=pT[:, 0:128], func=AF.Copy,
                                                                                           scale=C["rm4"][:, cq_:cq_ + 1]),
                                              rd=[pr[bB], c_r], wr=[kdtm_r])
                                    kb.pe([lambda e, t=t: e.matmul(pb[bC][:, 0:128], lhsT=vtm[:, t, :], rhs=atm[:], start=True, stop=False)] +
                                          [(lambda e, cq=cq, t=t: e.matmul(pb[bD][:, cq * 128:(cq + 1) * 128], lhsT=kdtm[:, cq, :], rhs=vtm[:, t, :],
                                                                           start=True, stop=True)) for cq in range(4)],
                                          rd=[v_r, atm_r, kdtm_r], wr=[pr[bC], pr[bD]])
                                    yield
                                    for cq in (range(4) if d == 0 else range(3, -1, -1)):
                                        cidx = gq * 4 + cq
                                        islast = (cq == (3 if d == 0 else 0))
                                        cs_ = slice(gq * 128 + cq * 32, gq * 128 + (cq + 1) * 32)
                                        ps_ = slice(cq * 32, (cq + 1) * 32)
                                        snx = (scur + 1) % 4
                                        kb.pe([lambda e, cs_=cs_, ps_=ps_, scur=scur, islast=islast: e.matmul(
                                            pb[bC][:, ps_], lhsT=Sb[:, scur, :], rhs=qd[:, cs_], start=False, stop=islast)],
                                            rd=[S_r[scur], blk_r], wr=[pr[bC]])
                                        kb.op("dve", lambda e, scur=scur, snx=snx, cidx=cidx, cq=cq: e.scalar_tensor_tensor(
                                            out=Sb[:, snx, :], in0=Sb[:, scur, :], scalar=egl[:, cidx:cidx + 1], in1=pb[bD][:, cq * 128:(cq + 1) * 128],
                                            op0=ALU.mult, op1=ALU.add), rd=[S_r[scur], pr[bD], g_r], wr=[S_r[snx]])
                                        scur = snx
                                    yield
                                    osl = slice(t * 128, (t + 1) * 128)
                                    kb.op("pool" if False else "dve", lambda e, osl=osl: e.tensor_tensor(out=oT[:, osl], in0=oT[:, osl], in1=pb[bC][:, 0:128],
                                                                                                         op=ALU.add), rd=[pr[bC], o_r], wr=[o_r])

                        gens = [bchain(0), bchain(1)]
                        while gens:
                            for gen in list(gens):
                                try:
                                    next(gen)
                                except StopIteration:
                                    gens.remove(gen)
                        for b in range(NB):
                            epilogue(oT[:, b * 512:(b + 1) * 512], o_r, 1, OFF["b_z"] + h * 128, l,
                                     yT_d[1, h * 128:(h + 1) * 128, :], b, tmp, tmp_r, wzt[:], wz_r)
                kb.barrier()

                with phase() as ph:
                  if 'A' not in skip:
                    SC = sb("a_SC", [128, NT, 8, 8], F32, ph)
                    sc_r = Res()
                    phg = ExitStack()
                    gpre = sb("a_gpre", [128, NT, 16], F32, phg)
                    gtmp = sb("a_gtmp", [128, NT, 8], F32, phg)
                    gg = sb("a_gg", [128, NT, 8], F32, phg)
                    wt, wt_r = load_win(l, OFF["a_b"], 16)
                    for t in range(NT):
                        projT(pb[0][:, t * 16:(t + 1) * 16], pr[0], wt, wt_r, 0, 16, t * 128)
                    kb.op("act", lambda e: e.activation(out=gpre[:], in_=pb[0][:, 0:NT * 16].rearrange("p (t n) -> p t n", n=16),
                                                        func=AF.Copy), rd=[pr[0]], wr=[sc_r])
                    kb.op("act", lambda e: e.activation(out=SC[:, :, :, 5], in_=gpre[:, :, 0:8], func=AF.Sigmoid), rd=[sc_r], wr=[sc_r])
                    kb.op("dve", lambda e: e.tensor_tensor(out=gtmp[:], in0=gpre[:, :, 8:16],
                                                           in1=gatec[:, 8:16].unsqueeze(1).to_broadcast([128, NT, 8]), op=ALU.add),
                          rd=[sc_r, par_r], wr=[sc_r])
                    kb.op("act", lambda e: e.activation(out=gtmp[:], in_=gtmp[:], func=AF.Exp), rd=[sc_r], wr=[sc_r])
                    kb.op("act", lambda e: e.activation(out=gtmp[:], in_=gtmp[:], func=AF.Ln, bias=1.0), rd=[sc_r], wr=[sc_r])
                    kb.op("dve", lambda e: e.tensor_tensor(out=gg[:], in0=gtmp[:],
                                                           in1=gatec[:, 0:8].unsqueeze(1).to_broadcast([128, NT, 8]), op=ALU.mult),
                          rd=[sc_r, par_r], wr=[sc_r])
                    g2 = gg[:].rearrange("p t n -> p (t n)")
                    kb.pe([lambda e: e.matmul(pb[1][:, 0:NT * 8], lhsT=C["uincl"], rhs=g2, start=True, stop=True)], rd=[sc_r, c_r], wr=[pr[1]])
                    kb.pe([lambda e: e.matmul(pb[2][:, 0:NT * 8], lhsT=C["uinclT"], rhs=g2, start=True, stop=True)], rd=[sc_r, c_r], wr=[pr[2]])
                    kb.pe([lambda e: e.matmul(pb[3][:, 0:NT * 8], lhsT=C["onesf"], rhs=g2, start=True, stop=True)], rd=[sc_r, c_r], wr=[pr[3]])
                    p1 = pb[1][:, 0:NT * 8].rearrange("p (t n) -> p t n", n=8)
                    p2 = pb[2][:, 0:NT * 8].rearrange("p (t n) -> p t n", n=8)
                    p3_ = pb[3][:, 0:NT * 8].rearrange("p (t n) -> p t n", n=8)
                    kb.op("act", lambda e: e.activation(out=SC[:, :, 0:4, 1], in_=p1[:, :, 0:4], func=AF.Copy), rd=[pr[1]], wr=[sc_r])
                    kb.op("act", lambda e: e.activation(out=SC[:, :, 4:8, 1], in_=p2[:, :, 4:8], func=AF.Copy), rd=[pr[2]], wr=[sc_r])
                    kb.op("dve", lambda e: e.tensor_scalar(out=SC[:, :, :, 2], in0=SC[:, :, :, 1], scalar1=-1.0, scalar2=None, op0=ALU.mult),
                          rd=[sc_r], wr=[sc_r])
                    kb.op("act", lambda e: e.activation(out=SC[:, :, :, 7], in_=p3_, func=AF.Exp), rd=[pr[3]], wr=[sc_r])
                    kb.op("dve", lambda e: e.tensor_tensor(out=gtmp[:], in0=p3_, in1=SC[:, :, :, 1], op=ALU.subtract), rd=[pr[3], sc_r], wr=[sc_r])
                    kb.op("act", lambda e: e.activation(out=SC[:, :, :, 4], in_=gtmp[:], func=AF.Exp), rd=[sc_r], wr=[sc_r])
                    kb.op("act", lambda e: e.activation(out=gtmp[:], in_=SC[:, :, :, 1], func=AF.Exp), rd=[sc_r], wr=[sc_r])
                    kb.op("dve", lambda e: e.tensor_tensor(out=SC[:, :, :, 3], in0=gtmp[:], in1=SC[:, :, :, 5], op=ALU.mult), rd=[sc_r], wr=[sc_r])
                    kb.op("dve", lambda e: e.tensor_scalar(out=SC[:, :, :, 6], in0=gtmp[:], scalar1=128.0 ** -0.5, scalar2=None, op0=ALU.mult),
                          rd=[sc_r], wr=[sc_r])
                    kb.op("act", lambda e: e.activation(out=gtmp[:], in_=SC[:, :, :, 5], func=AF.Ln), rd=[sc_r], wr=[sc_r])
                    kb.op("dve", lambda e: e.tensor_tensor(out=SC[:, :, :, 0], in0=gtmp[:], in1=SC[:, :, :, 1], op=ALU.add), rd=[sc_r], wr=[sc_r])

                    kb.barrier()
                    phg.close()
                    dump("SC", SC[:].rearrange("p t n k -> p (t n k)"), sc_r, F32)
                    qT = sb("a_qT", [128, S], BF16, ph)
                    kT = sb("a_kT", [128, S], BF16, ph)
                    qkvtm = sb("a_qkvtm", [128, NT, 3, 128], BF16, ph)
                    qkv_r = Res()
                    oT = sb("a_oT", [128, S], F32, ph)
                    o_r = Res()
                    for h in range(4):
                        with phase() as ph2:
                            xpad = sb("a_xpad", [128, S + 4], F32, ph2)
                            xp_r = Res()
                            acc = sb("a_acc", [128, min(1024, S)], F32, ph2)
                            acc_r = Res()
                            vT = sb("a_vT", [128, S], BF16, ph2)
                            sqb = sb("a_sqb", [128, 512], BF16, ph2)
                            rsb = sb("a_rsb", [128, 512], F32, ph2)
                            tmp_r = Res()
                            kb.op("pool", lambda e: e.memset(xpad[:, 0:2], 0.0), wr=[xp_r])
                            kb.op("pool", lambda e: e.memset(xpad[:, S + 2:S + 4], 0.0), wr=[xp_r])
                            kb.op("pool", lambda e: e.memset(oT[:], 0.0), wr=[o_r])
                            for xi, (nm, dst) in enumerate((("a_q", qT), ("a_k", kT), ("a_v", vT))):
                                wt, wt_r = load_win(l, OFF[nm] + h * 128, 128)
                                for b in range(NB):
                                    k = b % 2
                                    projF(pb[k][:], pr[k], wt, wt_r, 0, 128, b * 512, 512)
                                    kb.op("act", lambda e, b=b, k=k: e.activation(out=xpad[:, 2 + b * 512:2 + (b + 1) * 512], in_=pb[k][:],
                                                                                  func=AF.Copy), rd=[pr[k]], wr=[xp_r])
                                grp = xi * 4 + h
                                QW = min(1024, S)
                                for q0 in range(0, S, QW):
                                    kb.op("dve", lambda e, grp=grp, q0=q0: e.tensor_scalar(out=acc[:], in0=xpad[:, q0:q0 + QW], scalar1=convw[:, grp, 0:1],
                                                                                           scalar2=None, op0=ALU.mult), rd=[xp_r, par_r], wr=[acc_r])
                                    for j in range(1, 5):
                                        kb.op("dve", lambda e, grp=grp, j=j, q0=q0: e.scalar_tensor_tensor(
                                            out=acc[:], in0=xpad[:, q0 + j:q0 + j + QW], scalar=convw[:, grp, j:j + 1], in1=acc[:],
                                            op0=ALU.mult, op1=ALU.add), rd=[xp_r, par_r, acc_r], wr=[acc_r])
                                    kb.op("act", lambda e: e.activation(out=acc[:], in_=acc[:], func=AF.Silu), rd=[acc_r], wr=[acc_r])
                                    if xi < 2:
                                        for b_ in range(QW // 512):
                                            sl = slice(b_ * 512, (b_ + 1) * 512)
                                            osl_ = slice(q0 + b_ * 512, q0 + (b_ + 1) * 512)
                                            kb.op("act", lambda e, sl=sl: e.activation(out=sqb[:], in_=acc[:, sl], func=AF.Square), rd=[acc_r], wr=[tmp_r])
                                            kb.pe([lambda e: e.matmul(pb[2][:], lhsT=onesb[:], rhs=sqb[:], start=True, stop=True)],
                                                  rd=[tmp_r, c_r], wr=[pr[2]])
                                            kb.op("act", lambda e: e.activation(out=rsb[:], in_=pb[2][:], func=AF.Ln, bias=epsc[:, 0:1]),
                                                  rd=[pr[2], c_r], wr=[tmp_r])
                                            kb.op("act", lambda e: e.activation(out=rsb[:], in_=rsb[:], func=AF.Exp, scale=-0.5), rd=[tmp_r], wr=[tmp_r])
                                            kb.op("dve", lambda e, sl=sl, osl_=osl_, dst=dst: e.tensor_tensor(out=dst[:, osl_], in0=acc[:, sl], in1=rsb[:], op=ALU.mult),
                                                  rd=[acc_r, tmp_r], wr=[qkv_r])
                                    else:
                                        kb.op("dve", lambda e, dst=dst, q0=q0: e.tensor_copy(out=dst[:, q0:q0 + QW], in_=acc[:]), rd=[acc_r], wr=[qkv_r])
                            for t in range(NT):
                                k = t % 2
                                pT = pbf(3 + k)
                                ts_ = slice(t * 128, (t + 1) * 128)
                                kb.pe([(lambda e, xi=xi, src=src, pT=pT, ts_=ts_: e.transpose(out=pT[:, xi * 128:(xi + 1) * 128], in_=src[:, ts_],
                                                                                              identity=identb[:]))
                                       for xi, src in enumerate((qT, kT, vT))], rd=[qkv_r, c_r], wr=[pr[3 + k]])
                                kb.op("act", lambda e, t=t, pT=pT: e.activation(out=qkvtm[:, t, :, :],
                                                                                in_=pT[:, 0:384].rearrange("p (x n) -> p x n", x=3),
                                                                                func=AF.Copy), rd=[pr[3 + k]], wr=[qkv_r])
                        dump("qT", qT[:], qkv_r, BF16)
                        with phase() as ph2:
                            G = 2
                            GW = G * 128

                            def chain(d, BK):
                                bA, bB, bC, bD = BK
                                sfx = "_%d" % d
                                dgF = sb("a_dgF" + sfx, [128, G, 3, 128], F32, ph2)
                                dgB = sb("a_dgB" + sfx, [128, G, 4, 128], BF16, ph2)
                                dg_r = Res()
                                DJI = sb("a_DJI" + sfx, [128, G, 128], F32, ph2)
                                DIJ = sb("a_DIJ" + sfx, [128, G, 128], F32, ph2)
                                dd_r = Res()
                                aqk = sb("a_aqk" + sfx, [128, G, 128], BF16, ph2)
                                aqk_r = Res()
                                YP = [sb("a_YP%d" % k + sfx, [128, G, 256], F32, ph2) for k in range(2)]
                                ZZ = [sb("a_ZZ%d" % k + sfx, [128, G, 128], F32, ph2) for k in range(2)]
                                yz_r = [Res(), Res()]
                                TTb = sb("a_TTb" + sfx, [128, G, 128], BF16, ph2)
                                tt_r = Res()
                                scl = sb("a_scl" + sfx, [128, G, 4, 128], BF16, ph2)
                                scl_r = Res()
                                wTt = sb("a_wT" + sfx, [128, G, 128], BF16, ph2)
                                ut = sb("a_u" + sfx, [128, G, 128], F32, ph2)
                                wu_r = Res()
                                vnew = sb("a_vnew" + sfx, [128, 128], BF16, ph2)
                                vn_r = Res()
                                Sf = sb("a_Sf" + sfx, [128, 128], F32, ph2)
                                Sbf = sb("a_Sbf" + sfx, [128, 128], BF16, ph2)
                                S_r = Res()
                                Sf_r = Res()
                                n = d * 4 + h
                                negJI = C["negJI_f"] if d == 0 else C["negJI_b"]
                                negIJ = C["negIJ_f"] if d == 0 else C["negIJ_b"]
                                kb.op("pool", lambda e: e.memset(Sf[:], 0.0), wr=[Sf_r])
                                kb.op("pool", lambda e: e.memset(Sbf[:], 0.0), wr=[S_r])
                                batches = list(range(0, NT, G))
                                if d == 1:
                                    batches = batches[::-1]
                                for t0 in batches:
                                    kb.op("pool", lambda e, t0=t0: e.tensor_tensor(
                                        out=dgF[:], in0=C["identf"].unsqueeze(1).unsqueeze(1).to_broadcast([128, G, 3, 128]),
                                        in1=SC[:, t0:t0 + G, n, 0:3].unsqueeze(3).to_broadcast([128, G, 3, 128]), op=ALU.mult),
                                        rd=[sc_r, c_r], wr=[dg_r])
                                    kb.op("pool", lambda e, t0=t0: e.tensor_tensor(
                                        out=dgB[:], in0=C["identf"].unsqueeze(1).unsqueeze(1).to_broadcast([128, G, 4, 128]),
                                        in1=SC[:, t0:t0 + G, n, 3:7].unsqueeze(3).to_broadcast([128, G, 4, 128]), op=ALU.mult),
                                        rd=[sc_r, c_r], wr=[dg_r])
                                    fns = []
                                    for g in range(G):
                                        o0 = pb[bA][:, g * 128:(g + 1) * 128]
                                        o1 = pb[bB][:, g * 128:(g + 1) * 128]
                                        fns += [lambda e, g=g, o0=o0: e.matmul(o0, lhsT=C["onesf"], rhs=dgF[:, g, 1, :], start=True, stop=False),
                                                lambda e, g=g, o0=o0: e.matmul(o0, lhsT=dgF[:, g, 2, :], rhs=C["onesf"], start=False, stop=False),
                                                lambda e, g=g, o0=o0: e.matmul(o0, lhsT=C["identf"], rhs=negJI, start=False, stop=True),
                                                lambda e, g=g, o1=o1: e.matmul(o1, lhsT=dgF[:, g, 0, :], rhs=C["onesf"], start=True, stop=False),
                                                lambda e, g=g, o1=o1: e.matmul(o1, lhsT=C["onesf"], rhs=dgF[:, g, 2, :], start=False, stop=False),
                                                lambda e, g=g, o1=o1: e.matmul(o1, lhsT=C["identf"], rhs=negIJ, start=False, stop=True)]
                                    kb.pe(fns, rd=[dg_r, c_r], wr=[pr[bA], pr[bB]])
                                    fns = []
                                    for g in range(G):
                                        ts_ = slice((t0 + g) * 128, (t0 + g + 1) * 128)
                                        fns += [lambda e, g=g, ts_=ts_: e.matmul(pb[bC][:, g * 128:(g + 1) * 128], lhsT=kT[:, ts_], rhs=kT[:, ts_], start=True, stop=True),
                                                lambda e, g=g, ts_=ts_: e.matmul(pb[bD][:, g * 128:(g + 1) * 128], lhsT=kT[:, ts_], rhs=qT[:, ts_], start=True, stop=True)]
                                    kb.pe(fns, rd=[qkv_r], wr=[pr[bC], pr[bD]])
                                    kb.op("act", lambda e: e.activation(out=DJI[:].rearrange("p g n -> p (g n)"), in_=pb[bA][:, 0:GW], func=AF.Exp),
                                          rd=[pr[bA]], wr=[dd_r])
                                    kb.op("act", lambda e: e.activation(out=DIJ[:].rearrange("p g n -> p (g n)"), in_=pb[bB][:, 0:GW], func=AF.Exp),
                                          rd=[pr[bB]], wr=[dd_r])
                                    yield
                                    kb.op("dve", lambda e: e.scalar_tensor_tensor(out=ZZ[0][:].rearrange("p g n -> p (g n)"), in0=pb[bC][:, 0:GW], scalar=-1.0,
                                                                                  in1=DIJ[:].rearrange("p g n -> p (g n)"), op0=ALU.mult, op1=ALU.mult),
                                          rd=[pr[bC], dd_r], wr=[yz_r[0]])
                                    kb.op("dve", lambda e: e.scalar_tensor_tensor(out=aqk[:].rearrange("p g n -> p (g n)"), in0=pb[bD][:, 0:GW], scalar=128.0 ** -0.5,
                                                                                  in1=DJI[:].rearrange("p g n -> p (g n)"), op0=ALU.mult, op1=ALU.mult),
                                          rd=[pr[bD], dd_r], wr=[aqk_r])
                                    for g in range(G):
                                        bank = (bA, bB)[g % 2]
                                        t = t0 + g
                                        kb.pe([lambda e, g=g, bank=bank, t=t: e.matmul(pb[bank][:, 0:128], lhsT=dgB[:, g, 0, :], rhs=qkvtm[:, t, 1, :], start=True, stop=True),
                                               lambda e, g=g, bank=bank, t=t: e.matmul(pb[bank][:, 128:256], lhsT=dgB[:, g, 1, :], rhs=qkvtm[:, t, 1, :], start=True, stop=True),
                                               lambda e, g=g, bank=bank, t=t: e.matmul(pb[bank][:, 256:384], lhsT=dgB[:, g, 2, :], rhs=qkvtm[:, t, 2, :], start=True, stop=True),
                                               lambda e, g=g, bank=bank, t=t: e.matmul(pb[bank][:, 384:512], lhsT=qkvtm[:, t, 0, :], rhs=dgB[:, g, 3, :], start=True, stop=True)],
                                              rd=[dg_r, qkv_r], wr=[pr[bank]])
                                        kb.op("act", lambda e, g=g, bank=bank: e.activation(
                                            out=scl[:, g, :, :].rearrange("p x n -> p (x n)"), in_=pb[bank][:], func=AF.Copy),
                                            rd=[pr[bank]], wr=[scl_r])
                                    kb.pe([(lambda e, g=g: e.transpose(out=pb[bC][:, g * 128:(g + 1) * 128], in_=ZZ[0][:, g, :], identity=C["identf"]))
                                           for g in range(G)], rd=[yz_r[0], c_r], wr=[pr[bC]])
                                    yield
                                    pT3 = pb[bC][:, 0:GW].rearrange("p (g n) -> p g n", g=G)
                                    kb.op("act", lambda e, pT3=pT3: e.activation(out=YP[0][:, :, 0:128], in_=pT3, func=AF.Copy), rd=[pr[bC]], wr=[yz_r[0]])
                                    kb.op("dve", lambda e, pT3=pT3: e.tensor_tensor(out=YP[0][:, :, 128:256], in0=pT3,
                                                                                    in1=C["identf"].unsqueeze(1).to_broadcast([128, G, 128]), op=ALU.add),
                                          rd=[pr[bC], c_r], wr=[yz_r[0]])
                                    cur = 0
                                    for lev in range(7):
                                        nxt = 1 - cur
                                        fns = []
                                        for g in range(G):
                                            o_ = g * 256
                                            if lev == 0:
                                                fns.append(lambda e, g=g, o_=o_, cur=cur: e.matmul(
                                                    pb[bC][:, o_:o_ + 128], lhsT=ZZ[cur][:, g, :], rhs=YP[cur][:, g, 0:128], start=True, stop=True))
                                            elif lev < 6:
                                                fns.append(lambda e, g=g, o_=o_, cur=cur: e.matmul(
                                                    pb[bC][:, o_:o_ + 256], lhsT=ZZ[cur][:, g, :], rhs=YP[cur][:, g, :], start=True, stop=True))
                                            else:
                                                fns.append(lambda e, g=g, o_=o_, cur=cur: e.matmul(
                                                    pb[bC][:, o_ + 128:o_ + 256], lhsT=ZZ[cur][:, g, :], rhs=YP[cur][:, g, 128:256], start=True, stop=True))
                                            if lev < 6:
                                                fns.append(lambda e, g=g, cur=cur: e.matmul(
                                                    pb[bD][:, g * 128:(g + 1) * 128], lhsT=YP[cur][:, g, 0:128], rhs=ZZ[cur][:, g, :], start=True, stop=True))
                                        kb.pe(fns, rd=[yz_r[cur]], wr=[pr[bC], pr[bD]])
                                        yield
                                        src = pb[bC][:, 0:G * 256].rearrange("p (g n) -> p g n", g=G)
                                        if lev < 6:
                                            kb.op("act", lambda e, src=src, nxt=nxt: e.activation(out=YP[nxt][:, :, 0:128], in_=src[:, :, 0:128], func=AF.Copy),
                                                  rd=[pr[bC]], wr=[yz_r[nxt]])
                                        if lev == 0:
                                            kb.op("dve", lambda e, nxt=nxt, cur=cur: e.tensor_copy(out=YP[nxt][:, :, 128:256], in_=YP[cur][:, :, 128:256]),
                                                  rd=[yz_r[cur]], wr=[yz_r[nxt]])
                                        else:
                                            kb.op("dve", lambda e, src=src, nxt=nxt, cur=cur: e.tensor_tensor(
                                                out=YP[nxt][:, :, 128:256], in0=src[:, :, 128:256], in1=YP[cur][:, :, 128:256], op=ALU.add),
                                                rd=[pr[bC], yz_r[cur]], wr=[yz_r[nxt]])
                                        if lev < 6:
                                            kb.op("act", lambda e, nxt=nxt: e.activation(out=ZZ[nxt][:].rearrange("p g n -> p (g n)"), in_=pb[bD][:, 0:GW], func=AF.Copy),
                                                  rd=[pr[bD]], wr=[yz_r[nxt]])
                                        cur = nxt
                                    kb.op("act", lambda e, cur=cur: e.activation(out=TTb[:], in_=YP[cur][:, :, 128:256], func=AF.Copy), rd=[yz_r[cur]], wr=[tt_r])
                                    fns = []
                                    for g in range(G):
                                        fns += [lambda e, g=g: e.matmul(pb[bA][:, g * 128:(g + 1) * 128], lhsT=scl[:, g, 0, :], rhs=TTb[:, g, :], start=True, stop=True),
                                                lambda e, g=g: e.matmul(pb[bA][:, GW + g * 128:GW + (g + 1) * 128], lhsT=TTb[:, g, :], rhs=scl[:, g, 2, :], start=True, stop=True)]
                                    kb.pe(fns, rd=[scl_r, tt_r], wr=[pr[bA]])
                                    yield
                                    kb.op("act", lambda e: e.activation(out=wTt[:].rearrange("p g n -> p (g n)"), in_=pb[bA][:, 0:GW], func=AF.Copy), rd=[pr[bA]], wr=[wu_r])
                                    kb.op("dve", lambda e: e.tensor_copy(out=ut[:].rearrange("p g n -> p (g n)"), in_=pb[bA][:, GW:2 * GW]), rd=[pr[bA]], wr=[wu_r])
                                    for g in (range(G) if d == 0 else range(G - 1, -1, -1)):
                                        t = t0 + g
                                        kb.pe([lambda e, g=g: e.matmul(pb[bB][:, 0:128], lhsT=wTt[:, g, :], rhs=Sbf[:], start=True, stop=True)],
                                              rd=[wu_r, S_r], wr=[pr[bB]])
                                        yield
                                        kb.op("dve", lambda e, g=g: e.tensor_tensor(out=vnew[:], in0=ut[:, g, :], in1=pb[bB][:, 0:128], op=ALU.subtract),
                                              rd=[wu_r, pr[bB]], wr=[vn_r])
                                        kb.pe([lambda e, g=g: e.matmul(pb[bB][:, 128:256], lhsT=Sbf[:], rhs=scl[:, g, 3, :], start=True, stop=False),
                                               lambda e, g=g: e.matmul(pb[bB][:, 128:256], lhsT=vnew[:], rhs=aqk[:, g, :], start=False, stop=True),
                                               lambda e, g=g: e.matmul(pb[bB][:, 256:384], lhsT=scl[:, g, 1, :], rhs=vnew[:], start=True, stop=True)],
                                              rd=[S_r, scl_r, vn_r, aqk_r], wr=[pr[bB]])
                                        yield
                                        kb.op("dve", lambda e, t=t: e.scalar_tensor_tensor(out=Sbf[:], in0=Sf[:], scalar=SC[:, t, n, 7:8], in1=pb[bB][:, 256:384],
                                                                                           op0=ALU.mult, op1=ALU.add), rd=[Sf_r, sc_r, pr[bB], S_r], wr=[S_r])
                                        kb.op("dve", lambda e, t=t: e.scalar_tensor_tensor(out=Sf[:], in0=Sf[:], scalar=SC[:, t, n, 7:8], in1=pb[bB][:, 256:384],
                                                                                           op0=ALU.mult, op1=ALU.add), rd=[sc_r, pr[bB], Sf_r], wr=[Sf_r])
                                        osl = slice(t * 128, (t + 1) * 128)
                                        kb.op("dve", lambda e, osl=osl: e.tensor_tensor(out=oT[:, osl], in0=oT[:, osl], in1=pb[bB][:, 128:256], op=ALU.add),
                                              rd=[pr[bB], o_r], wr=[o_r])

                            gens = [chain(0, (0, 1, 2, 3)), chain(1, (4, 5, 6, 7))]
                            while gens:
                                for gen in list(gens):
                                    try:
                                        next(gen)
                                    except StopIteration:
                                        gens.remove(gen)
                        dump("oT", oT[:], o_r, F32)
                        with phase() as ph2:
                            tmp = dict(sq=sb("a_esq", [128, 512], BF16, ph2), rs=sb("a_ers", [128, 512], F32, ph2),
                                       zg=sb("a_ezg", [128, 512], F32, ph2), yb=sb("a_eyb", [128, 512], BF16, ph2))
                            tmp_r = Res()
                            wzt = sb("a_wz", [128, KC, 128], BF16, ph2)
                            wz_r = Res()
                            wt, wt_r = load_win(l, OFF["a_z"] + h * 128, 128)
                            kb.op("pool", lambda e, wt=wt: e.tensor_copy(out=wzt[:], in_=wt), rd=[wt_r], wr=[wz_r])
                            for b in range(NB):
                                epilogue(oT[:, b * 512:(b + 1) * 512], o_r, 0, OFF["a_z"] + h * 128, l,
                                         yT_d[0, h * 128:(h + 1) * 128, :], b, tmp, tmp_r, wzt[:], wz_r)
                kb.barrier()
                if dbg and s == 0 and l == dcur["dl"]:
                    kb.dma(dbg_d[:, :, :], yT_d[:, :, :], st_ds, rd=[yT_r], wr=[Res()])
                    kb.barrier()

                with phase() as ph:
                  if 'M' not in skip:
                    wbr = [sb("m_wbr%d" % k, [128, 4, D], BF16, ph) for k in range(3)]
                    wo = sb("m_wo", [128, KC, D], BF16, ph)
                    mw_r = Res()
                    ci = 0
                    for x in range(3):
                        for ch in range(2):
                            kb.dma(stg[:, 0:4, 0:512], w_br[x][l, :, ch * 512:(ch + 1) * 512].rearrange("(c p) n -> p c n", p=128), ld_ds, wr=[stg_r])
                            kb.op(("pool", "dve", "act")[ci % 3], lambda e, x=x, ch=ch, ci=ci: (
                                e.activation(out=wbr[x][:, :, ch * 512:(ch + 1) * 512], in_=stg[:, 0:4, 0:512], func=AF.Copy) if ci % 3 == 2
                                else e.tensor_copy(out=wbr[x][:, :, ch * 512:(ch + 1) * 512], in_=stg[:, 0:4, 0:512])), rd=[stg_r], wr=[mw_r])
                            ci += 1
                    for ch in range(2):
                        kb.dma(stg[:, 0:KC, 0:512], w_out[l, :, ch * 512:(ch + 1) * 512].rearrange("(c p) n -> p c n", p=128), ld_ds, wr=[stg_r])
                        kb.op(("pool", "dve", "act")[ci % 3], lambda e, ch=ch, ci=ci: (
                            e.activation(out=wo[:, :, ch * 512:(ch + 1) * 512], in_=stg[:, 0:KC, 0:512], func=AF.Copy) if ci % 3 == 2
                            else e.tensor_copy(out=wo[:, :, ch * 512:(ch + 1) * 512], in_=stg[:, 0:KC, 0:512])), rd=[stg_r], wr=[mw_r])
                        ci += 1
                    yt = [sb("m_yt%d" % k, [128, 3, 4, 128], BF16, ph) for k in range(2)]
                    gt = [sb("m_gt%d" % k, [128, 3 * D], BF16, ph) for k in range(2)]
                    xt = [sb("m_xt%d" % k, [128, D], F32, ph) for k in range(2)]
                    in_r = [Res(), Res()]
                    iny_r = [Res(), Res()]
                    ing_r = [Res(), Res()]
                    mg = [sb("m_mg%d" % k, [128, D], F32, ph) for k in range(2)]
                    mgb = [sb("m_mgb%d" % k, [128, D], BF16, ph) for k in range(2)]
                    mtmp = [sb("m_tmp%d" % k, [128, 2, 512], F32, ph) for k in range(2)]
                    mg_r = [Res(), Res()]
                    mt_r = [[Res(), Res()], [Res(), Res()]]
                    mT = [sb("m_mT%d" % k, [128, KC, 128], BF16, ph) for k in range(2)]
                    mT_r = [Res(), Res()]
                    ot = [sb("m_ot%d" % k, [128, D], F32, ph) for k in range(2)]
                    ot_r = [Res(), Res()]
                    x_new = Res()

                    def m_s1(t):
                        k = t % 2
                        ts_ = slice(t * 128, (t + 1) * 128)
                        kb.dma(yt[k][:].rearrange("p x c n -> p (x c) n"),
                               yT_d[:, :, ts_].rearrange("x (c p) n -> p (x c) n", p=128), ml_ds[k][0], rd=[yT_r], wr=[iny_r[k]])
                        kb.dma(gt[k][:], gate_d[ts_, :], ml_ds[k][1], rd=[gate_r], wr=[ing_r[k]])
                        kb.dma(xt[k][:], xsrc[ts_, :], ml_ds[k][2], rd=[xr[0]], wr=[in_r[k]])
                        for ch in range(2):
                            cs_ = slice(ch * 512, (ch + 1) * 512)
                            for x in range(3):
                                bank = (x + ch) % 3
                                kb.pe([(lambda e, c=c, x=x, bank=bank, cs_=cs_: e.matmul(pb[bank][:], lhsT=yt[k][:, x, c, :], rhs=wbr[x][:, c, cs_],
                                                                                        start=(c == 0), stop=(c == 3))) for c in range(4)],
                                      rd=[iny_r[k], mw_r], wr=[pr[bank]])
                                gsl = gt[k][:, x * D + ch * 512:x * D + (ch + 1) * 512]
                                if x == 0:
                                    kb.op("dve", lambda e, bank=bank, gsl=gsl, cs_=cs_: e.tensor_tensor(out=mg[k][:, cs_], in0=pb[bank][:], in1=gsl, op=ALU.mult),
                                          rd=[pr[bank], ing_r[k]], wr=[mg_r[k]])
                                else:
                                    kb.op("dve", lambda e, bank=bank, gsl=gsl, x=x: e.tensor_tensor(out=mtmp[k][:, x - 1, :], in0=pb[bank][:], in1=gsl, op=ALU.mult),
                                          rd=[pr[bank], ing_r[k]], wr=[mt_r[k][x - 1]])
                                    kb.op("pool", lambda e, cs_=cs_, x=x: e.tensor_tensor(out=mg[k][:, cs_], in0=mg[k][:, cs_], in1=mtmp[k][:, x - 1, :], op=ALU.add),
                                          rd=[mg_r[k], mt_r[k][x - 1]], wr=[mg_r[k]])
                        kb.op("act", lambda e: e.activation(out=mgb[k][:], in_=mg[k][:], func=AF.Copy), rd=[mg_r[k]], wr=[mg_r[k]])

                    def m_s2(t):
                        k = t % 2
                        ts_ = slice(t * 128, (t + 1) * 128)
                        pT = pbf(3)
                        kb.pe([(lambda e, c=c: e.transpose(out=pT[:, c * 128:(c + 1) * 128], in_=mgb[k][:, c * 128:(c + 1) * 128], identity=identb[:]))
                               for c in range(KC)], rd=[mg_r[k], c_r], wr=[pr[3]])
                        kb.op("act", lambda e: e.activation(out=mT[k][:].rearrange("p c n -> p (c n)"), in_=pT, func=AF.Copy), rd=[pr[3]], wr=[mT_r[k]])
                        for ch in range(2):
                            cs_ = slice(ch * 512, (ch + 1) * 512)
                            bank = 4 + ch
                            kb.pe([(lambda e, c=c, bank=bank, cs_=cs_: e.matmul(pb[bank][:], lhsT=mT[k][:, c, :], rhs=wo[:, c, cs_], start=(c == 0), stop=(c == KC - 1)))
                                   for c in range(KC)], rd=[mT_r[k], mw_r], wr=[pr[bank]])
                            kb.op("dve", lambda e, bank=bank, cs_=cs_: e.tensor_tensor(out=ot[k][:, cs_], in0=pb[bank][:], in1=xt[k][:, cs_], op=ALU.add),
                                  rd=[pr[bank], in_r[k]], wr=[ot_r[k]])
                        kb.dma(xdst[ts_, :], ot[k][:], st2_ds[k], rd=[ot_r[k]], wr=[x_new])

                    m_s1(0)
                    for t in range(NT):
                        if t + 1 < NT:
                            m_s1(t + 1)
                        m_s2(t)
                    xr[0] = x_new
                kb.barrier()
        kb.barrier()
    print("instructions:", kb.nins, flush=True)
    return nc


_CACHE = {}


def kernel(**inputs):
    S = 4096
    NSEQ = 2
    DEPTH = 4
    NCORE = 8
    key = (S, NSEQ, DEPTH)
    if key not in _CACHE:
        _CACHE[key] = build(S, NSEQ, DEPTH)
    nc = _CACHE[key]
    xp = np.asarray(inputs["x_prompt"], dtype=np.float32)
    xs = np.asarray(inputs["x_sample"], dtype=np.float32)
    cf, rope, sm = host_consts(S)
    shared = {
        "norm_g": inputs["norm_g"], "w_in": inputs["w_in"], "conv_w": inputs["conv_w"],
        "a_log": np.asarray(inputs["a_log"]).reshape(DEPTH, 8), "dt_bias": np.asarray(inputs["dt_bias"]).reshape(DEPTH, 8),
        "gdn_norm_g": inputs["gdn_norm_g"], "hgrn_lb_logits": inputs["hgrn_lb_logits"], "hgrn_norm_g": inputs["hgrn_norm_g"],
        "q_norm_g": np.asarray(inputs["q_norm_g"]).reshape(DEPTH, 128), "k_norm_g": np.asarray(inputs["k_norm_g"]).reshape(DEPTH, 128),
        "diff_lambda": np.asarray(inputs["diff_lambda"]).reshape(DEPTH, 256), "subln_g": inputs["subln_g"],
        "w_br_a": inputs["w_br_a"], "w_br_b": inputs["w_br_b"], "w_br_c": inputs["w_br_c"], "w_out": inputs["w_out"],
        "cst_f": cf, "cst_rope": rope, "cst_smask": sm,
    }
    shared = {k: np.ascontiguousarray(np.asarray(v, dtype=np.float32)) for k, v in shared.items()}
    in_maps = []
    for c in range(NCORE):
        xin = np.ascontiguousarray(np.stack([xp[c], xs[c % 4]], axis=0))
        m = dict(shared)
        m["xin"] = xin
        in_maps.append(m)
    res = run_bass_kernel_spmd(nc, in_maps, core_ids=list(range(NCORE)))
    y_prompt = np.stack([np.asarray(res.results[c]["yout"][0]) for c in range(NCORE)], axis=0).astype(np.float32)
    y_sample = np.stack([np.asarray(res.results[c]["yout"][1]) for c in range(4)], axis=0).astype(np.float32)
    return (y_prompt, y_sample)
```

```python
import math
from contextlib import ExitStack

import numpy as np
import concourse.bass as bass
import concourse.mybir as mybir
from concourse.bass_utils import run_bass_kernel_spmd

F32 = mybir.dt.float32
BF16 = mybir.dt.bfloat16
AF = mybir.ActivationFunctionType
ALU = mybir.AluOpType
AX = mybir.AxisListType

D = 1024
NIN = 9744
KC = 8
EPS = 1e-6
OFF = dict(a_q=0, a_k=512, a_v=1024, a_z=1536, a_b=2048, a_a=2056, b_q=2064, b_i=2576, b_f=3088,
           b_z=4112, c_q=4624, c_k=5136, c_v=5648, c_z=6160, gate=6672)
NEG = -30000.0
ROPE_THETA = 500000.0


class Res:
    __slots__ = ("w", "rd")

    def __init__(self):
        self.w = None
        self.rd = {}


class DS:
    def __init__(self, sem, name):
        self.sem = sem
        self.cnt = 0
        self.name = name


class KB:
    def __init__(self, nc, es):
        self.nc = nc
        self.es = es
        self.eng = {"pe": nc.tensor, "dve": nc.vector, "act": nc.scalar, "pool": nc.gpsimd, "sp": nc.sync}
        self.sem = {k: es.enter_context(nc.semaphore("s_" + k)) for k in ("pe", "dve", "act", "pool")}
        self.cnt = {k: 0 for k in self.sem}
        self.waited = {k: {} for k in self.eng}
        self.dss = []
        self.nins = 0

    def ds(self, name):
        d = DS(self.es.enter_context(self.nc.semaphore(name)), name)
        self.dss.append(d)
        return d

    def _need(self, e, toks):
        wd = self.waited[e]
        for (key, sem, val) in toks:
            if e == "pe" and key == "pe":
                continue
            if wd.get(key, 0) >= val:
                continue
            self.eng[e].wait_ge(sem, val)
            wd[key] = val

    @staticmethod
    def _deps(rd, wr):
        toks = []
        for r in rd:
            if r.w is not None:
                toks.append(r.w)
        for w in wr:
            if w.w is not None:
                toks.append(w.w)
            toks.extend(w.rd.values())
        return toks

    @staticmethod
    def _mark(tok, rd, wr):
        for r in rd:
            r.rd[tok[0]] = tok
        for w in wr:
            w.w = tok
            w.rd = {}

    def op(self, e, fn, rd=(), wr=()):
        self._need(e, self._deps(rd, wr))
        ins = fn(self.eng[e])
        self.cnt[e] += 1
        ins.then_inc(self.sem[e], 1)
        self._mark((e, self.sem[e], self.cnt[e]), rd, wr)
        self.nins += 1

    def pe(self, fns, rd=(), wr=()):
        self._need("pe", self._deps(rd, wr))
        ins = None
        for f in fns:
            ins = f(self.nc.tensor)
            self.nins += 1
        self.cnt["pe"] += 1
        ins.then_inc(self.sem["pe"], 1)
        self._mark(("pe", self.sem["pe"], self.cnt["pe"]), rd, wr)

    def dma(self, out, in_, ds, rd=(), wr=(), q="sp", **kw):
        self._need(q, self._deps(rd, wr))
        ins = self.eng[q].dma_start(out=out, in_=in_, **kw)
        ds.cnt += 16
        ins.then_inc(ds.sem, 16)
        self._mark((ds.name, ds.sem, ds.cnt), rd, wr)
        self.nins += 1

    def barrier(self):
        toks = [(k, self.sem[k], self.cnt[k]) for k in self.sem if self.cnt[k] > 0]
        toks += [(d.name, d.sem, d.cnt) for d in self.dss if d.cnt > 0]
        for e in self.eng:
            self._need(e, toks)


def host_consts(S):
    NT = S // 128
    j = np.arange(128)[:, None]
    i = np.arange(128)[None, :]
    c = {}
    c["identf"] = np.eye(128, dtype=np.float32)
    c["onesf"] = np.ones((128, 128), np.float32)
    c["uincl"] = (j <= i).astype(np.float32)
    c["uinclT"] = (j >= i).astype(np.float32)
    c["negJI_f"] = np.where(i >= j, 0.0, NEG).astype(np.float32)
    c["negJI_b"] = np.where(i <= j, 0.0, NEG).astype(np.float32)
    c["negIJ_f"] = np.where(j > i, 0.0, NEG).astype(np.float32).T.copy()
    c["negIJ_b"] = np.where(j < i, 0.0, NEG).astype(np.float32).T.copy()
    pi = np.arange(128)[:, None]
    fj = np.arange(128)[None, :]
    c["negIJ_f"] = np.where(pi > fj, 0.0, NEG).astype(np.float32)
    c["negIJ_b"] = np.where(pi < fj, 0.0, NEG).astype(np.float32)
    same = (j // 32) == (i // 32)
    c["bd_f"] = (same & (i >= j)).astype(np.float32)
    c["bd_b"] = (same & (i <= j)).astype(np.float32)
    rm4 = np.zeros((128, 128), np.float32)
    for q_ in range(4):
        rm4[q_ * 32:(q_ + 1) * 32, q_] = 1.0
    c["rm4"] = rm4
    cf = np.concatenate([c[k] for k in CF_NAMES], axis=1)
    inv = 1.0 / (ROPE_THETA ** (np.arange(0, 16, 2, dtype=np.float32) / 16.0))
    pos = np.arange(S, dtype=np.float32)
    ang = (pos[:, None] * inv[None, :]).astype(np.float32)
    cs = np.cos(ang).astype(np.float32).reshape(NT, 128, 8).transpose(1, 0, 2)
    sn = np.sin(ang).astype(np.float32).reshape(NT, 128, 8).transpose(1, 0, 2)
    rope = np.ascontiguousarray(np.stack([cs, sn], axis=2)).reshape(128, NT * 2 * 8)
    sm = np.ones((128, S), np.float32)
    sm[:, ::32] = 0.0
    return np.ascontiguousarray(cf), np.ascontiguousarray(rope), sm


CF_NAMES = ("identf", "onesf", "uincl", "uinclT", "negJI_f", "negJI_b", "negIJ_f", "negIJ_b", "bd_f", "bd_b", "rm4")


def build(S, NSEQ, DEPTH, dbg=False, skip=()):
    NT = S // 128
    NB = S // 512
    NC32 = S // 32
    nc = bass.Bass("TRN2", target_bir_lowering=False)

    def din(name, shape, dt=F32):
        return nc.dram_tensor(name, list(shape), dt, kind="ExternalInput").ap()

    xin = din("xin", [NSEQ, S, D])
    norm_g = din("norm_g", [DEPTH, D])
    w_in = din("w_in", [DEPTH, D, NIN])
    conv_w = din("conv_w", [DEPTH, 5, 1536])
    a_log = din("a_log", [DEPTH, 8])
    dt_bias = din("dt_bias", [DEPTH, 8])
    gdn_g = din("gdn_norm_g", [DEPTH, 128])
    lb_logits = din("hgrn_lb_logits", [2, DEPTH, 512])
    hgrn_g = din("hgrn_norm_g", [DEPTH, 128])
    qn_g = din("q_norm_g", [DEPTH, 128])
    kn_g = din("k_norm_g", [DEPTH, 128])
    dlam = din("diff_lambda", [DEPTH, 256])
    subln_g = din("subln_g", [DEPTH, 128])
    w_br = [din("w_br_a", [DEPTH, 512, D]), din("w_br_b", [DEPTH, 512, D]), din("w_br_c", [DEPTH, 512, D])]
    w_out = din("w_out", [DEPTH, D, D])
    cf_d = din("cst_f", [128, 128 * len(CF_NAMES)])
    rope_d = din("cst_rope", [128, NT * 16])
    smask_d = din("cst_smask", [128, S])
    yout = nc.dram_tensor("yout", [NSEQ, S, D], F32, kind="ExternalOutput").ap()
    yT_d = nc.dram_tensor("yT_scr", [3, 512, S], BF16, kind="Internal").ap()
    gate_d = nc.dram_tensor("gate_scr", [S, 3 * D], BF16, kind="Internal").ap()
    xres_d = nc.dram_tensor("xres_scr", [S, D], F32, kind="Internal").ap()
    dbg_d = None
    if dbg:
        dbg_d = nc.dram_tensor("dbg_yT", [3, 512, S], BF16, kind="ExternalOutput").ap()

    es = ExitStack()
    with es:
        kb = KB(nc, es)

        uniq = [0]

        def sb(name, shape, dt, stack=es):
            uniq[0] += 1
            return stack.enter_context(nc.sbuf_tensor("%s_%d" % (name, uniq[0]), list(shape), dt))

        import contextlib

        @contextlib.contextmanager
        def phase():
            with ExitStack() as st_:
                yield st_
                kb.barrier()

        dumped = set()

        dcur = {"l": 0, "dl": int(dbg) - 1}

        def dump(name, ap, r, dt):
            if not dbg or name in dumped or dcur["l"] != dcur["dl"]:
                return
            dumped.add(name)
            shp = list(ap.shape)
            dd = nc.dram_tensor("dbg_" + name, shp, dt, kind="ExternalOutput").ap()
            kb.dma(dd, ap, st_ds, rd=[r], wr=[Res()])

        cf = sb("cf", [128, len(CF_NAMES), 128], F32)
        C = {n: cf[:, k, :] for k, n in enumerate(CF_NAMES)}
        identb = sb("identb", [128, 128], BF16)
        onesb = sb("onesb", [128, 128], BF16)
        epsc = sb("epsc", [128, 1], F32)
        hT = sb("hT", [128, KC, S], BF16)
        hT_r = Res()
        stg = sb("stg", [128, KC, 512], F32)
        stg_r = Res()
        wbf = [sb("wbf%d" % k, [128, KC, 512], BF16) for k in range(2)]
        wbf_r = [Res(), Res()]
        ld_ds = kb.ds("ld")
        c_r = Res()
        c_ds = kb.ds("cst")
        st_ds = kb.ds("st")
        st2_ds = [kb.ds("st2_0"), kb.ds("st2_1")]
        c2_ds = kb.ds("cst2")
        xl_ds = [kb.ds("xl0"), kb.ds("xl1")]
        ml_ds = [[kb.ds("ml%d_%d" % (k, i)) for i in range(3)] for k in range(2)]
        gcol = sb("gcol", [128, KC, 1], F32)
        convw = sb("convw", [128, 12, 5], F32)
        gatec = sb("gatec", [128, 16], F32)
        vecs = sb("vecs", [128, 8], F32)
        qkg = sb("qkg", [128, 2, 128], F32)
        lamt = sb("lamt", [128, 256], F32)
        lbt = sb("lbt", [128, 2, DEPTH, 4], F32)
        oml = sb("oml", [128, DEPTH, 2, 4], F32)
        par_r = Res()
        smallt = sb("smallt", [128, 64], F32)
        small_r = Res()

        pbig = es.enter_context(nc.psum_tensor("pbig", [128, 4096], F32))
        pb = [pbig[:, k * 512:(k + 1) * 512] for k in range(8)]
        pr = [Res() for _ in range(8)]

        def pbf(k):
            return pb[k][:].bitcast(BF16)

        wslot = [0]

        def load_w(src, kc, ncols, fold=False):
            s = wslot[0]
            wslot[0] ^= 1
            kb.dma(stg[:, 0:kc, 0:ncols], src.rearrange("(c p) n -> p c n", p=128), ld_ds, wr=[stg_r])
            dst = wbf[s][:, 0:kc, 0:ncols]
            if fold:
                kb.op("pool", lambda e: e.tensor_tensor(out=dst, in0=stg[:, 0:kc, 0:ncols],
                                                        in1=gcol[:, 0:kc, :].to_broadcast([128, kc, ncols]),
                                                        op=ALU.mult),
                      rd=[stg_r, par_r], wr=[wbf_r[s]])
            else:
                kb.op("pool", lambda e: e.tensor_copy(out=dst, in_=stg[:, 0:kc, 0:ncols]), rd=[stg_r], wr=[wbf_r[s]])
            return dst, wbf_r[s]

        def load_win(l, col0, ncols):
            return load_w(w_in[l, :, col0:col0 + ncols], KC, ncols, fold=True)

        def projF(out_ps, out_r, wt, wr_, c0, m, tok0, ntok):
            kb.pe([(lambda e, c=c: e.matmul(out_ps, lhsT=wt[:, c, c0:c0 + m], rhs=hT[:, c, tok0:tok0 + ntok],
                                            start=(c == 0), stop=(c == KC - 1))) for c in range(KC)],
                  rd=[wr_, hT_r], wr=[out_r])

        def projT(out_ps, out_r, wt, wr_, c0, n, tok0, ntok=128):
            kb.pe([(lambda e, c=c: e.matmul(out_ps, lhsT=hT[:, c, tok0:tok0 + ntok], rhs=wt[:, c, c0:c0 + n],
                                            start=(c == 0), stop=(c == KC - 1))) for c in range(KC)],
                  rd=[wr_, hT_r], wr=[out_r])

        def rstd_from(out_ap, in_ap, scale, rd, wr, tmp_ap):
            shp = list(in_ap.shape)
            bias = epsc[0:shp[0], 0:1]
            kb.op("act", lambda e: e.activation(out=tmp_ap, in_=in_ap, func=AF.Ln, scale=scale, bias=bias), rd=rd, wr=wr)
            kb.op("act", lambda e: e.activation(out=out_ap, in_=tmp_ap, func=AF.Exp, scale=-0.5), rd=wr, wr=wr)

        kb.dma(cf[:].rearrange("p k n -> p (k n)"), cf_d[:, :], c_ds, wr=[c_r])
        kb.op("pool", lambda e: e.tensor_copy(out=identb[:], in_=C["identf"]), rd=[c_r], wr=[c_r])
        kb.op("pool", lambda e: e.memset(onesb[:], 1.0), wr=[c_r])
        kb.op("pool", lambda e: e.memset(epsc[:], EPS), wr=[c_r])
        with nc.allow_non_contiguous_dma("tiny param loads"):
            for d_ in range(2):
                for l_ in range(DEPTH):
                    kb.dma(lbt[:, d_, l_, :], lb_logits[d_, l_].rearrange("(h k) -> k h", k=128), c2_ds, wr=[small_r])
        kb.op("act", lambda e: e.activation(out=lbt[:], in_=lbt[:], func=AF.Exp), rd=[small_r], wr=[small_r])
        den = smallt[:, 0:8].rearrange("p (d h) -> p d h", d=2)
        num = smallt[:, 8:16].rearrange("p (d h) -> p d h", d=2)
        kb.op("dve", lambda e: e.tensor_reduce(out=den, in_=lbt[:].rearrange("p d l h -> p d h l"), axis=AX.X, op=ALU.add),
              rd=[small_r], wr=[small_r])
        kb.op("dve", lambda e: e.reciprocal(out=den, in_=den), rd=[small_r], wr=[small_r])
        for l in range(DEPTH):
            if l == 0:
                kb.op("dve", lambda e: e.memset(oml[:, 0, :, :], 1.0), wr=[small_r])
            else:
                kb.op("dve", lambda e, l=l: e.tensor_reduce(out=num, in_=lbt[:, :, 1:l + 1, :].rearrange("p d l h -> p d h l"),
                                                            axis=AX.X, op=ALU.add), rd=[small_r], wr=[small_r])
                kb.op("dve", lambda e: e.tensor_tensor(out=num, in0=num, in1=den, op=ALU.mult), rd=[small_r], wr=[small_r])
                kb.op("dve", lambda e, l=l: e.tensor_scalar(out=oml[:, l, :, :], in0=num, scalar1=-1.0, scalar2=1.0,
                                                            op0=ALU.mult, op1=ALU.add), rd=[small_r], wr=[small_r])
        kb.barrier()

        def epilogue(oT_blk, o_r, gidx, zcol0, l, dst_rows, b, tmp, tmp_r, wz, wz_r):
            sq, rs, zg, yb = tmp["sq"], tmp["rs"], tmp["zg"], tmp["yb"]
            kb.op("act", lambda e: e.activation(out=sq[:], in_=oT_blk, func=AF.Square), rd=[o_r], wr=[tmp_r])
            kb.pe([lambda e: e.matmul(pb[6][:], lhsT=onesb[:], rhs=sq[:], start=True, stop=True)], rd=[tmp_r, c_r], wr=[pr[6]])
            kb.op("act", lambda e: e.activation(out=rs[:], in_=pb[6][:], func=AF.Ln, scale=1.0 / 128, bias=epsc[:, 0:1]),
                  rd=[pr[6], c_r], wr=[tmp_r])
            kb.op("act", lambda e: e.activation(out=rs[:], in_=rs[:], func=AF.Exp, scale=-0.5), rd=[tmp_r], wr=[tmp_r])
            projF(pb[7][:], pr[7], wz, wz_r, 0, 128, b * 512, 512)
            kb.op("act", lambda e: e.activation(out=zg[:], in_=pb[7][:], func=AF.Silu), rd=[pr[7]], wr=[tmp_r])
            kb.op("dve", lambda e: e.tensor_tensor(out=rs[:], in0=rs[:], in1=oT_blk, op=ALU.mult), rd=[tmp_r, o_r], wr=[tmp_r])
            kb.op("dve", lambda e: e.scalar_tensor_tensor(out=yb[:], in0=rs[:], scalar=vecs[:, gidx:gidx + 1], in1=zg[:],
                                                          op0=ALU.mult, op1=ALU.mult), rd=[tmp_r, par_r], wr=[tmp_r])
            kb.dma(dst_rows[:, b * 512:(b + 1) * 512], yb[:], st_ds, rd=[tmp_r], wr=[yT_r])

        yT_r = Res()
        gate_r = Res()
        xr = [Res()]

        for s in range(NSEQ):
            for l in range(DEPTH):
                dcur["l"] = l
                xsrc = xin[s] if l == 0 else xres_d
                xdst = yout[s] if l == DEPTH - 1 else xres_d
                lam_init = 0.8 - 0.6 * math.exp(-0.3 * l)
                with nc.allow_non_contiguous_dma("tiny param loads"):
                    kb.dma(gcol[:], norm_g[l].rearrange("(c p o) -> p c o", p=128, o=1), c_ds, wr=[par_r])
                    for j in range(5):
                        kb.dma(convw[:, :, j], conv_w[l, j].rearrange("(g p) -> p g", p=128), c_ds, wr=[par_r])
                    kb.dma(vecs[:, 0:1], gdn_g[l].rearrange("(p o) -> p o", o=1), c_ds, wr=[par_r])
                    kb.dma(vecs[:, 1:2], hgrn_g[l].rearrange("(p o) -> p o", o=1), c_ds, wr=[par_r])
                    kb.dma(vecs[:, 2:3], subln_g[l].rearrange("(p o) -> p o", o=1), c_ds, wr=[par_r])
                kb.dma(gatec[:, 0:8], a_log[l].partition_broadcast(128), c_ds, wr=[par_r])
                kb.dma(gatec[:, 8:16], dt_bias[l].partition_broadcast(128), c_ds, wr=[par_r])
                kb.dma(qkg[:, 0, :], qn_g[l].partition_broadcast(128), c_ds, wr=[par_r])
                kb.dma(qkg[:, 1, :], kn_g[l].partition_broadcast(128), c_ds, wr=[par_r])
                kb.dma(lamt[:], dlam[l].partition_broadcast(128), c_ds, wr=[par_r])
                kb.op("act", lambda e: e.activation(out=gatec[:, 0:8], in_=gatec[:, 0:8], func=AF.Exp), rd=[par_r], wr=[par_r])
                kb.op("dve", lambda e: e.tensor_scalar(out=gatec[:, 0:8], in0=gatec[:, 0:8], scalar1=-1.0, scalar2=None,
                                                       op0=ALU.mult), rd=[par_r], wr=[par_r])
                kb.op("dve", lambda e: e.tensor_scalar(out=qkg[:, 0, :], in0=qkg[:, 0, :], scalar1=0.125, scalar2=None,
                                                       op0=ALU.mult), rd=[par_r], wr=[par_r])
                kb.op("dve", lambda e: e.tensor_scalar(out=vecs[:, 2:3], in0=vecs[:, 2:3], scalar1=1.0 - lam_init,
                                                       scalar2=None, op0=ALU.mult), rd=[par_r], wr=[par_r])
                l4 = lamt[:].rearrange("p (a b d) -> p a b d", a=2, b=2)
                pr2 = smallt[:, 16:18]
                kb.op("dve", lambda e: e.tensor_tensor(out=l4[:, :, 0, :], in0=l4[:, :, 0, :], in1=l4[:, :, 1, :], op=ALU.mult),
                      rd=[par_r], wr=[par_r])
                kb.op("dve", lambda e: e.tensor_reduce(out=pr2, in_=l4[:, :, 0, :], axis=AX.X, op=ALU.add),
                      rd=[par_r], wr=[small_r])
                kb.op("act", lambda e: e.activation(out=pr2, in_=pr2, func=AF.Exp), rd=[small_r], wr=[small_r])
                kb.op("dve", lambda e: e.tensor_tensor(out=smallt[:, 18:19], in0=smallt[:, 17:18], in1=smallt[:, 16:17],
                                                       op=ALU.subtract), rd=[small_r], wr=[small_r])
                kb.op("dve", lambda e: e.tensor_scalar(out=vecs[:, 3:4], in0=smallt[:, 18:19], scalar1=-lam_init, scalar2=None,
                                                       op0=ALU.add), rd=[small_r], wr=[par_r])

                with phase() as ph:
                    xt = [sb("xt%d" % k, [128, D], F32, ph) for k in range(2)]
                    xt_r = [Res(), Res()]
                    hb = [sb("hb%d" % k, [128, D], BF16, ph) for k in range(2)]
                    hb_r = [Res(), Res()]
                    junk = sb("junk", [128, D], BF16, ph)
                    st = sb("nst", [128, 2, 4], F32, ph)
                    st_r = [Res(), Res()]
                    for t in range(NT):
                        k = t % 2
                        kb.dma(xt[k][:], xsrc[t * 128:(t + 1) * 128, :], xl_ds[k], rd=[xr[0]], wr=[xt_r[k]])
                        kb.op("act", lambda e, k=k: e.activation(out=junk[:], in_=xt[k][:], func=AF.Square,
                                                                 accum_out=st[:, k, 0:1]), rd=[xt_r[k]], wr=[st_r[k]])
                        rstd_from(st[:, k, 1:2], st[:, k, 0:1], 1.0 / D, [st_r[k], c_r], [st_r[k]], st[:, k, 2:3])
                        kb.op("dve", lambda e, k=k: e.tensor_scalar(out=hb[k][:], in0=xt[k][:], scalar1=st[:, k, 1:2],
                                                                    scalar2=None, op0=ALU.mult),
                              rd=[xt_r[k], st_r[k]], wr=[hb_r[k]])
                        pT = pbf(k)
                        kb.pe([(lambda e, c=c, k=k, pT=pT: e.transpose(out=pT[:, c * 128:(c + 1) * 128],
                                                                       in_=hb[k][:, c * 128:(c + 1) * 128],
                                                                       identity=identb[:])) for c in range(KC)],
                              rd=[hb_r[k], c_r], wr=[pr[k]])
                        kb.op("act", lambda e, t=t, pT=pT: e.activation(out=hT[:, :, t * 128:(t + 1) * 128],
                                                                        in_=pT.rearrange("p (c n) -> p c n", c=KC),
                                                                        func=AF.Copy), rd=[pr[k]], wr=[hT_r])
                kb.barrier()

                dump("hT", hT[:].rearrange("p c n -> p (c n)"), hT_r, BF16)
                with phase() as ph:
                  if 'G' not in skip:
                    gsb = [sb("gsb%d" % k, [128, 512], BF16, ph) for k in range(2)]
                    gsb_r = [Res(), Res()]
                    it = 0
                    for cb in range(6):
                        wt, wt_r = load_win(l, OFF["gate"] + cb * 512, 512)
                        for t in range(NT):
                            k = it % 2
                            it += 1
                            projT(pb[k][:], pr[k], wt, wt_r, 0, 512, t * 128)
                            kb.op("act", lambda e, k=k: e.activation(out=gsb[k][:], in_=pb[k][:], func=AF.Sigmoid),
                                  rd=[pr[k]], wr=[gsb_r[k]])
                            kb.dma(gate_d[t * 128:(t + 1) * 128, cb * 512:(cb + 1) * 512], gsb[k][:], st2_ds[k],
                                   rd=[gsb_r[k]], wr=[gate_r])
                kb.barrier()

                with phase() as ph:
                  if 'C' not in skip:
                    rope_t = sb("rope_t", [128, NT, 2, 8], F32, ph)
                    rp_r = Res()
                    kb.dma(rope_t[:].rearrange("p t a b -> p (t a b)"), rope_d[:, :], c_ds, wr=[rp_r])
                    qT = sb("c_qT", [128, 4, S], BF16, ph)
                    kT = sb("c_kT", [128, 4, S], BF16, ph)
                    qk_r = Res()
                    with phase() as ph2:
                        wqk = [load_win(l, OFF["c_q"] + which * 512, 512) for which in range(2)]

                        def prep_chain(which, par, bP, bT):
                            dstT = qT if which == 0 else kT
                            sfx = "_%d%d" % (which, par)
                            sq = sb("c_sq" + sfx, [128, 8, 64], F32, ph2)
                            qn = sb("c_qn" + sfx, [128, 8, 64], F32, ph2)
                            ssq = sb("c_ssq" + sfx, [128, 8, 3], F32, ph2)
                            rt = sb("c_rt" + sfx, [128, 4, 8, 8], F32, ph2)
                            qb16 = sb("c_qb16" + sfx, [128, 8, 64], BF16, ph2)
                            w_r = Res()
                            wt, wt_r = wqk[which]
                            for t in range(par, NT, 2):
                                projT(pb[bP][:], pr[bP], wt, wt_r, 0, 512, t * 128)
                                yield
                                p3 = pb[bP][:].rearrange("p (g d) -> p g d", g=8)
                                kb.op("act", lambda e, p3=p3: e.activation(out=sq[:], in_=p3, func=AF.Square), rd=[pr[bP]], wr=[w_r])
                                yield
                                kb.op("dve", lambda e: e.tensor_reduce(out=ssq[:, :, 0], in_=sq[:], axis=AX.X, op=ALU.add),
                                      rd=[w_r], wr=[w_r])
                                yield
                                kb.op("act", lambda e: e.activation(out=ssq[:, :, 2], in_=ssq[:, :, 0], func=AF.Ln, scale=1.0 / 64, bias=epsc[:, 0:1]),
                                      rd=[w_r, c_r], wr=[w_r])
                                yield
                                kb.op("act", lambda e: e.activation(out=ssq[:, :, 1], in_=ssq[:, :, 2], func=AF.Exp, scale=-0.5), rd=[w_r], wr=[w_r])
                                yield
                                kb.op("dve", lambda e, p3=p3: e.tensor_tensor(out=qn[:], in0=p3,
                                                                              in1=ssq[:, :, 1:2].to_broadcast([128, 8, 64]),
                                                                              op=ALU.mult), rd=[pr[bP], w_r], wr=[w_r])
                                yield
                                g2 = qkg[:, which, :].rearrange("p (m d) -> p m d", m=2)
                                q4 = qn[:].rearrange("p (h m) d -> p h m d", h=4)
                                kb.op("dve", lambda e, g2=g2, q4=q4: e.tensor_tensor(
                                    out=q4, in0=q4, in1=g2.unsqueeze(1).to_broadcast([128, 4, 2, 64]), op=ALU.mult),
                                    rd=[w_r, par_r], wr=[w_r])
                                yield
                                cs = rope_t[:, t, 0:1, :].to_broadcast([128, 8, 8])
                                sn = rope_t[:, t, 1:2, :].to_broadcast([128, 8, 8])
                                x1 = qn[:, :, 0:8]
                                x2 = qn[:, :, 8:16]
                                kb.op("pool", lambda e: e.tensor_tensor(out=rt[:, 0], in0=x1, in1=cs, op=ALU.mult), rd=[w_r, rp_r], wr=[w_r])
                                kb.op("pool", lambda e: e.tensor_tensor(out=rt[:, 1], in0=x2, in1=sn, op=ALU.mult), rd=[w_r, rp_r], wr=[w_r])
                                kb.op("pool", lambda e: e.tensor_tensor(out=rt[:, 2], in0=x2, in1=cs, op=ALU.mult), rd=[w_r, rp_r], wr=[w_r])
                                kb.op("pool", lambda e: e.tensor_tensor(out=rt[:, 3], in0=x1, in1=sn, op=ALU.mult), rd=[w_r, rp_r], wr=[w_r])
                                yield
                                kb.op("dve", lambda e: e.tensor_copy(out=qb16[:, :, 16:64], in_=qn[:, :, 16:64]), rd=[w_r], wr=[w_r])
                                kb.op("dve", lambda e: e.tensor_tensor(out=qb16[:, :, 0:8], in0=rt[:, 0], in1=rt[:, 1], op=ALU.subtract),
                                      rd=[w_r], wr=[w_r])
                                kb.op("dve", lambda e: e.tensor_tensor(out=qb16[:, :, 8:16], in0=rt[:, 2], in1=rt[:, 3], op=ALU.add),
                                      rd=[w_r], wr=[w_r])
                                yield
                                pT = pbf(bT)
                                q2 = qb16[:].rearrange("p g d -> p (g d)")
                                kb.pe([(lambda e, h=h, pT=pT, q2=q2: e.transpose(out=pT[:, h * 128:(h + 1) * 128],
                                                                                 in_=q2[:, h * 128:(h + 1) * 128],
                                                                                 identity=identb[:])) for h in range(4)],
                                      rd=[w_r, c_r], wr=[pr[bT]])
                                yield
                                kb.op("act", lambda e, t=t, pT=pT: e.activation(
                                    out=dstT[:, :, t * 128:(t + 1) * 128], in_=pT[:, 0:512].rearrange("p (h n) -> p h n", h=4),
                                    func=AF.Copy), rd=[pr[bT]], wr=[qk_r])

                        gens = [prep_chain(0, 0, 0, 1), prep_chain(1, 0, 2, 3), prep_chain(0, 1, 4, 5), prep_chain(1, 1, 6, 7)]
                        while gens:
                            for gen in list(gens):
                                try:
                                    next(gen)
                                except StopIteration:
                                    gens.remove(gen)
                    for h in range(4):
                        with phase() as ph2:
                            vtm = sb("c_vtm", [128, NT, 128], BF16, ph2)
                            v_r = Res()
                            pt = [sb("c_p%d" % k, [128, 1024], BF16, ph2) for k in range(2)]
                            pt_r = [Res() for _ in range(2)]
                            pacc = sb("c_pacc", [128, 512], F32, ph2)
                            pacc_r = Res()
                            e0 = sb("c_e0", [128, 512], F32, ph2)
                            e1 = sb("c_e1", [128, 512], F32, ph2)
                            ot = sb("c_ot", [128, 512], F32, ph2)
                            ot_r = Res()
                            tmp = dict(sq=sb("c_esq", [128, 512], BF16, ph2), rs=sb("c_ers", [128, 512], F32, ph2),
                                       zg=sb("c_ezg", [128, 512], F32, ph2), yb=sb("c_eyb", [128, 512], BF16, ph2))
                            tmp_r = Res()
                            wzt = sb("c_wz", [128, KC, 128], BF16, ph2)
                            wz_r = Res()
                            wt, wt_r = load_win(l, OFF["c_v"] + h * 128, 128)
                            for t4 in range(0, NT, 4):
                                for t in range(t4, t4 + 4):
                                    projT(pb[0][:, (t - t4) * 128:(t - t4 + 1) * 128], pr[0], wt, wt_r, 0, 128, t * 128)
                                kb.op("act", lambda e, t4=t4: e.activation(out=vtm[:, t4:t4 + 4, :],
                                                                           in_=pb[0][:].rearrange("p (t n) -> p t n", t=4),
                                                                           func=AF.Copy), rd=[pr[0]], wr=[v_r])
                            wt, wt_r = load_win(l, OFF["c_z"] + h * 128, 128)
                            kb.op("pool", lambda e, wt=wt: e.tensor_copy(out=wzt[:], in_=wt), rd=[wt_r], wr=[wz_r])
                            for b in range(NB):
                                def qk_pair(kt):
                                    kb.pe([(lambda e, m=m: e.matmul(
                                        pb[2 * (kt % 2) + m][:], lhsT=kT[m * 64:(m + 1) * 64, h, kt * 128:(kt + 1) * 128],
                                        rhs=qT[m * 64:(m + 1) * 64, h, b * 512:(b + 1) * 512], start=True, stop=True)) for m in range(2)],
                                        rd=[qk_r], wr=[pr[2 * (kt % 2)], pr[2 * (kt % 2) + 1]])

                                qk_pair(0)
                                for kt in range(NT):
                                    if kt + 1 < NT:
                                        qk_pair(kt + 1)
                                    kp = kt % 2
                                    kb.op("act", lambda e, kp=kp: e.activation(
                                        out=pt[kp][:], in_=pbig[:, kp * 1024:(kp + 1) * 1024], func=AF.Exp),
                                        rd=[pr[2 * kp], pr[2 * kp + 1]], wr=[pt_r[kp]])
                                    kb.pe([lambda e, kp=kp, kt=kt: e.matmul(pb[4][:], lhsT=vtm[:, kt, :], rhs=pt[kp][:, 0:512], start=(kt == 0), stop=(kt == NT - 1)),
                                           lambda e, kp=kp, kt=kt: e.matmul(pb[5][:], lhsT=vtm[:, kt, :], rhs=pt[kp][:, 512:1024], start=(kt == 0), stop=(kt == NT - 1)),
                                           lambda e, kp=kp, kt=kt: e.matmul(pb[6][:], lhsT=onesb[:], rhs=pt[kp][:, 0:512], start=(kt == 0), stop=(kt == NT - 1))],
                                          rd=[v_r, pt_r[kp], c_r], wr=[pr[4], pr[5], pr[6]])
                                    if kt == 0:
                                        kb.op("dve", lambda e, kp=kp: e.tensor_copy(out=pacc[:], in_=pt[kp][:, 512:1024]),
                                              rd=[pt_r[kp]], wr=[pacc_r])
                                    else:
                                        kb.op("dve", lambda e, kp=kp: e.tensor_tensor(out=pacc[:], in0=pacc[:], in1=pt[kp][:, 512:1024], op=ALU.add),
                                              rd=[pt_r[kp], pacc_r], wr=[pacc_r])
                                kb.pe([lambda e: e.matmul(pb[7][:], lhsT=C["onesf"], rhs=pacc[:], start=True, stop=True)],
                                      rd=[pacc_r, c_r], wr=[pr[7]])
                                kb.op("act", lambda e: e.activation(out=e0[:], in_=pb[6][:], func=AF.Ln), rd=[pr[6]], wr=[ot_r])
                                kb.op("act", lambda e: e.activation(out=e1[:], in_=pb[7][:], func=AF.Ln), rd=[pr[7]], wr=[ot_r])
                                kb.op("act", lambda e: e.activation(out=e0[:], in_=e0[:], func=AF.Exp, scale=-1.0), rd=[ot_r], wr=[ot_r])
                                kb.op("act", lambda e: e.activation(out=e1[:], in_=e1[:], func=AF.Exp, scale=-1.0), rd=[ot_r], wr=[ot_r])
                                kb.op("dve", lambda e: e.tensor_tensor(out=e0[:], in0=e0[:], in1=pb[4][:], op=ALU.mult), rd=[pr[4], ot_r], wr=[ot_r])
                                kb.op("dve", lambda e: e.tensor_tensor(out=e1[:], in0=e1[:], in1=pb[5][:], op=ALU.mult), rd=[pr[5], ot_r], wr=[ot_r])
                                kb.op("dve", lambda e: e.scalar_tensor_tensor(out=ot[:], in0=e1[:], scalar=vecs[:, 3:4], in1=e0[:],
                                                                              op0=ALU.mult, op1=ALU.add), rd=[ot_r, par_r], wr=[ot_r])
                                epilogue(ot[:], ot_r, 2, OFF["c_z"] + h * 128, l, yT_d[2, h * 128:(h + 1) * 128, :], b, tmp, tmp_r, wzt[:], wz_r)
                kb.barrier()

                with phase() as ph:
                  if 'B' not in skip:
                    qT = sb("b_qT", [128, S], BF16, ph)
                    q_r = Res()
                    vtm = sb("b_vtm", [128, NT, 128], BF16, ph)
                    v_r = Res()
                    oT = sb("b_oT", [128, S], F32, ph)
                    o_r = Res()
                    qtmp = sb("b_qtmp", [128, 512], F32, ph)
                    qtmp_r = Res()
                    smask = sb("b_smask", [128, 512], F32, ph)
                    sm_r = Res()
                    tmp = dict(sq=sb("b_esq", [128, 512], BF16, ph), rs=sb("b_ers", [128, 512], F32, ph),
                               zg=sb("b_ezg", [128, 512], F32, ph), yb=sb("b_eyb", [128, 512], BF16, ph))
                    tmp_r = Res()
                    wzt = sb("b_wz", [128, KC, 128], BF16, ph)
                    wz_r = Res()
                    kb.dma(smask[:], smask_d[:, 0:512], c_ds, wr=[sm_r])
                    CH = []
                    for d in range(2):
                        T = dict(
                            kkf=sb("b_kkf%d" % d, [128, 512], F32, ph), lfb=sb("b_lfb%d" % d, [128, 512], F32, ph),
                            gcb=sb("b_gcb%d" % d, [128, 512], F32, ph), a1=sb("b_a1%d" % d, [128, 512], F32, ph),
                            a2=sb("b_a2%d" % d, [128, 512], F32, ph), ex=sb("b_ex%d" % d, [128, 512], F32, ph),
                            qd=sb("b_qd%d" % d, [128, 512], BF16, ph), qm=sb("b_qm%d" % d, [128, 512], BF16, ph),
                            km=sb("b_km%d" % d, [128, 512], BF16, ph), kd=sb("b_kd%d" % d, [128, 512], BF16, ph),
                            atm=sb("b_atm%d" % d, [128, 128], BF16, ph), kdtm=sb("b_kdtm%d" % d, [128, 4, 128], BF16, ph),
                            Sb=sb("b_S%d" % d, [128, 4, 128], BF16, ph), egl=sb("b_egl%d" % d, [128, 16], F32, ph))
                        CH.append(T)
                    for h in range(4):
                        wt, wt_r = load_win(l, OFF["b_q"] + h * 128, 128)
                        for b in range(NB):
                            k = b % 2
                            projF(pb[k][:], pr[k], wt, wt_r, 0, 128, b * 512, 512)
                            kb.op("act", lambda e, k=k: e.activation(out=qtmp[:], in_=pb[k][:], func=AF.Silu),
                                  rd=[pr[k]], wr=[qtmp_r])
                            kb.op("dve", lambda e, b=b: e.tensor_scalar(out=qT[:, b * 512:(b + 1) * 512], in0=qtmp[:], scalar1=128.0 ** -0.5,
                                                                        scalar2=None, op0=ALU.mult), rd=[qtmp_r], wr=[q_r])
                        wt, wt_r = load_win(l, OFF["b_i"] + h * 128, 128)
                        for t4 in range(0, NT, 4):
                            for t in range(t4, t4 + 4):
                                projT(pb[2][:, (t - t4) * 128:(t - t4 + 1) * 128], pr[2], wt, wt_r, 0, 128, t * 128)
                            kb.op("act", lambda e, t4=t4: e.activation(out=vtm[:, t4:t4 + 4, :],
                                                                       in_=pb[2][:].rearrange("p (t n) -> p t n", t=4),
                                                                       func=AF.Copy), rd=[pr[2]], wr=[v_r])
                        wt, wt_r = load_win(l, OFF["b_z"] + h * 128, 128)
                        kb.op("pool", lambda e, wt=wt: e.tensor_copy(out=wzt[:], in_=wt), rd=[wt_r], wr=[wz_r])
                        kb.op("pool", lambda e: e.memset(oT[:], 0.0), wr=[o_r])
                        wfs = [load_win(l, OFF["b_f"] + d * 512 + h * 128, 128) for d in range(2)]

                        def bchain(d):
                            T = CH[d]
                            kkf, lfb, gcb, ex = T["kkf"], T["lfb"], T["gcb"], T["ex"]
                            qd, qm, km, kd, atm, kdtm, Sb, egl = T["qd"], T["qm"], T["km"], T["kd"], T["atm"], T["kdtm"], T["Sb"], T["egl"]
                            bA, bB, bC, bD = [4 * d + i for i in range(4)]
                            wt, wt_r = wfs[d]
                            lf_r, g_r, wk_r, blk_r, atm_r, kdtm_r = Res(), Res(), Res(), Res(), Res(), Res()
                            S_r = [Res() for _ in range(4)]
                            last = 31 if d == 0 else 0
                            bmask = C["bd_f"] if d == 0 else C["bd_b"]
                            kb.op("pool", lambda e: e.memset(Sb[:, 0, :], 0.0), wr=[S_r[0]])
                            scur = 0
                            g3 = gcb[:].rearrange("p (c n) -> p c n", n=32)
                            l3 = lfb[:].rearrange("p (c n) -> p c n", n=32)
                            a1 = T["a1"][:].rearrange("p (c n) -> p c n", n=32)
                            a2 = T["a2"][:].rearrange("p (c n) -> p c n", n=32)
                            for b in (range(NB) if d == 0 else range(NB - 1, -1, -1)):
                                sl = slice(b * 512, (b + 1) * 512)
                                projF(pb[bA][:], pr[bA], wt, wt_r, 0, 128, b * 512, 512)
                                yield
                                kb.op("act", lambda e: e.activation(out=kkf[:], in_=pb[bA][:], func=AF.Sigmoid, scale=-1.0),
                                      rd=[pr[bA]], wr=[lf_r])
                                kb.op("dve", lambda e: e.tensor_scalar(out=kkf[:], in0=kkf[:], scalar1=oml[:, l, d, h:h + 1],
                                                                       scalar2=None, op0=ALU.mult), rd=[lf_r, small_r], wr=[lf_r])
                                kb.op("dve", lambda e: e.tensor_scalar(out=lfb[:], in0=kkf[:], scalar1=1.0 - 1e-6,
                                                                       scalar2=None, op0=ALU.min), rd=[lf_r], wr=[lf_r])
                                kb.op("act", lambda e: e.activation(out=lfb[:], in_=lfb[:], func=AF.Ln, scale=-1.0, bias=1.0),
                                      rd=[lf_r], wr=[lf_r])
                                yield
                                kb.op("dve", lambda e: e.tensor_tensor_scan(out=gcb[:], data0=smask[:], data1=lfb[:], initial=0.0,
                                                                            op0=ALU.mult, op1=ALU.add), rd=[lf_r, sm_r], wr=[g_r])
                                if d == 1:
                                    kb.op("dve", lambda e: e.tensor_tensor(out=l3, in0=l3, in1=g3, op=ALU.subtract), rd=[g_r, lf_r], wr=[lf_r])
                                    kb.op("dve", lambda e: e.tensor_tensor(out=g3, in0=l3, in1=g3[:, :, 31:32].to_broadcast([128, 16, 32]),
                                                                           op=ALU.add), rd=[g_r, lf_r], wr=[g_r])
                                kb.op("act", lambda e: e.activation(out=egl[:], in_=g3[:, :, last], func=AF.Exp), rd=[g_r], wr=[g_r])
                                kb.op("pool", lambda e: e.tensor_tensor(out=a1, in0=g3, in1=g3[:, :, 16:17].to_broadcast([128, 16, 32]),
                                                                        op=ALU.subtract), rd=[g_r], wr=[wk_r])
                                kb.op("pool", lambda e: e.tensor_tensor(out=a2, in0=g3, in1=g3[:, :, last:last + 1].to_broadcast([128, 16, 32]),
                                                                        op=ALU.subtract), rd=[g_r], wr=[wk_r])
                                yield
                                kb.op("act", lambda e: e.activation(out=ex[:], in_=T["a1"][:], func=AF.Exp), rd=[wk_r], wr=[wk_r])
                                kb.op("dve", lambda e, sl=sl: e.tensor_tensor(out=qm[:], in0=ex[:], in1=qT[:, sl], op=ALU.mult),
                                      rd=[wk_r, q_r], wr=[blk_r])
                                kb.op("act", lambda e: e.activation(out=ex[:], in_=T["a1"][:], func=AF.Exp, scale=-1.0), rd=[wk_r], wr=[wk_r])
                                kb.op("dve", lambda e: e.tensor_tensor(out=km[:], in0=ex[:], in1=kkf[:], op=ALU.mult),
                                      rd=[wk_r, lf_r], wr=[blk_r])
                                yield
                                kb.op("act", lambda e: e.activation(out=ex[:], in_=gcb[:], func=AF.Exp), rd=[wk_r, g_r], wr=[wk_r])
                                kb.op("dve", lambda e, sl=sl: e.tensor_tensor(out=qd[:], in0=ex[:], in1=qT[:, sl], op=ALU.mult),
                                      rd=[wk_r, q_r], wr=[blk_r])
                                kb.op("act", lambda e: e.activation(out=ex[:], in_=T["a2"][:], func=AF.Exp, scale=-1.0), rd=[wk_r], wr=[wk_r])
                                kb.op("dve", lambda e: e.tensor_tensor(out=kd[:], in0=ex[:], in1=kkf[:], op=ALU.mult),
                                      rd=[wk_r, lf_r], wr=[blk_r])
                                yield
                                for gq in (range(4) if d == 0 else range(3, -1, -1)):
                                    t = b * 4 + gq
                                    gs = slice(gq * 128, (gq + 1) * 128)
                                    pT = pbf(bB)
                                    kb.pe([lambda e, gs=gs: e.matmul(pb[bA][:, 0:128], lhsT=km[:, gs], rhs=qm[:, gs], start=True, stop=True),
                                           lambda e, gs=gs, pT=pT: e.transpose(out=pT[:, 0:128], in_=kd[:, gs], identity=identb[:])],
                                          rd=[blk_r, c_r], wr=[pr[bA], pr[bB]])
                                    yield
                                    kb.op("dve", lambda e: e.tensor_tensor(out=atm[:], in0=pb[bA][:, 0:128], in1=bmask, op=ALU.mult),
                                          rd=[pr[bA], c_r], wr=[atm_r])
                                    for cq_ in range(4):
                                        kb.op("act", lambda e, pT=pT, cq_=cq_: e.activation(out=kdtm[:, cq_, :], in_=pT[:, 0:128], func=AF.Copy,
                                                                                           scale=C["rm4"][:, cq_:cq_ + 1]),
                                              rd=[pr[bB], c_r], wr=[kdtm_r])
                                    kb.pe([lambda e, t=t: e.matmul(pb[bC][:, 0:128], lhsT=vtm[:, t, :], rhs=atm[:], start=True, stop=False)] +
                                          [(lambda e, cq=cq, t=t: e.matmul(pb[bD][:, cq * 128:(cq + 1) * 128], lhsT=kdtm[:, cq, :], rhs=vtm[:, t, :],
                                                                           start=True, stop=True)) for cq in range(4)],
                                          rd=[v_r, atm_r, kdtm_r], wr=[pr[bC], pr[bD]])
                                    yield
                                    for cq in (range(4) if d == 0 else range(3, -1, -1)):
                                        cidx = gq * 4 + cq
                                        islast = (cq == (3 if d == 0 else 0))
                                        cs_ = slice(gq * 128 + cq * 32, gq * 128 + (cq + 1) * 32)
                                        ps_ = slice(cq * 32, (cq + 1) * 32)
                                        snx = (scur + 1) % 4
                                        kb.pe([lambda e, cs_=cs_, ps_=ps_, scur=scur, islast=islast: e.matmul(
                                            pb[bC][:, ps_], lhsT=Sb[:, scur, :], rhs=qd[:, cs_], start=False, stop=islast)],
                                            rd=[S_r[scur], blk_r], wr=[pr[bC]])
                                        kb.op("dve", lambda e, scur=scur, snx=snx, cidx=cidx, cq=cq: e.scalar_tensor_tensor(
                                            out=Sb[:, snx, :], in0=Sb[:, scur, :], scalar=egl[:, cidx:cidx + 1], in1=pb[bD][:, cq * 128:(cq + 1) * 128],
                                            op0=ALU.mult, op1=ALU.add), rd=[S_r[scur], pr[bD], g_r], wr=[S_r[snx]])
                                        scur = snx
                                    yield
                                    osl = slice(t * 128, (t + 1) * 128)
                                    kb.op("pool" if False else "dve", lambda e, osl=osl: e.tensor_tensor(out=oT[:, osl], in0=oT[:, osl], in1=pb[bC][:, 0:128],
                                                                                                         op=ALU.add), rd=[pr[bC], o_r], wr=[o_r])

                        gens = [bchain(0), bchain(1)]
                        while gens:
                            for gen in list(gens):
                                try:
                                    next(gen)
                                except StopIteration:
                                    gens.remove(gen)
                        for b in range(NB):
                            epilogue(oT[:, b * 512:(b + 1) * 512], o_r, 1, OFF["b_z"] + h * 128, l,
                                     yT_d[1, h * 128:(h + 1) * 128, :], b, tmp, tmp_r, wzt[:], wz_r)
                kb.barrier()

                with phase() as ph:
                  if 'A' not in skip:
                    SC = sb("a_SC", [128, NT, 8, 8], F32, ph)
                    sc_r = Res()
                    phg = ExitStack()
                    gpre = sb("a_gpre", [128, NT, 16], F32, phg)
                    gtmp = sb("a_gtmp", [128, NT, 8], F32, phg)
                    gg = sb("a_gg", [128, NT, 8], F32, phg)
                    wt, wt_r = load_win(l, OFF["a_b"], 16)
                    for t in range(NT):
                        projT(pb[0][:, t * 16:(t + 1) * 16], pr[0], wt, wt_r, 0, 16, t * 128)
                    kb.op("act", lambda e: e.activation(out=gpre[:], in_=pb[0][:, 0:NT * 16].rearrange("p (t n) -> p t n", n=16),
                                                        func=AF.Copy), rd=[pr[0]], wr=[sc_r])
                    kb.op("act", lambda e: e.activation(out=SC[:, :, :, 5], in_=gpre[:, :, 0:8], func=AF.Sigmoid), rd=[sc_r], wr=[sc_r])
                    kb.op("dve", lambda e: e.tensor_tensor(out=gtmp[:], in0=gpre[:, :, 8:16],
                                                           in1=gatec[:, 8:16].unsqueeze(1).to_broadcast([128, NT, 8]), op=ALU.add),
                          rd=[sc_r, par_r], wr=[sc_r])
                    kb.op("act", lambda e: e.activation(out=gtmp[:], in_=gtmp[:], func=AF.Exp), rd=[sc_r], wr=[sc_r])
                    kb.op("act", lambda e: e.activation(out=gtmp[:], in_=gtmp[:], func=AF.Ln, bias=1.0), rd=[sc_r], wr=[sc_r])
                    kb.op("dve", lambda e: e.tensor_tensor(out=gg[:], in0=gtmp[:],
                                                           in1=gatec[:, 0:8].unsqueeze(1).to_broadcast([128, NT, 8]), op=ALU.mult),
                          rd=[sc_r, par_r], wr=[sc_r])
                    g2 = gg[:].rearrange("p t n -> p (t n)")
                    kb.pe([lambda e: e.matmul(pb[1][:, 0:NT * 8], lhsT=C["uincl"], rhs=g2, start=True, stop=True)], rd=[sc_r, c_r], wr=[pr[1]])
                    kb.pe([lambda e: e.matmul(pb[2][:, 0:NT * 8], lhsT=C["uinclT"], rhs=g2, start=True, stop=True)], rd=[sc_r, c_r], wr=[pr[2]])
                    kb.pe([lambda e: e.matmul(pb[3][:, 0:NT * 8], lhsT=C["onesf"], rhs=g2, start=True, stop=True)], rd=[sc_r, c_r], wr=[pr[3]])
                    p1 = pb[1][:, 0:NT * 8].rearrange("p (t n) -> p t n", n=8)
                    p2 = pb[2][:, 0:NT * 8].rearrange("p (t n) -> p t n", n=8)
                    p3_ = pb[3][:, 0:NT * 8].rearrange("p (t n) -> p t n", n=8)
                    kb.op("act", lambda e: e.activation(out=SC[:, :, 0:4, 1], in_=p1[:, :, 0:4], func=AF.Copy), rd=[pr[1]], wr=[sc_r])
                    kb.op("act", lambda e: e.activation(out=SC[:, :, 4:8, 1], in_=p2[:, :, 4:8], func=AF.Copy), rd=[pr[2]], wr=[sc_r])
                    kb.op("dve", lambda e: e.tensor_scalar(out=SC[:, :, :, 2], in0=SC[:, :, :, 1], scalar1=-1.0, scalar2=None, op0=ALU.mult),
                          rd=[sc_r], wr=[sc_r])
                    kb.op("act", lambda e: e.activation(out=SC[:, :, :, 7], in_=p3_, func=AF.Exp), rd=[pr[3]], wr=[sc_r])
                    kb.op("dve", lambda e: e.tensor_tensor(out=gtmp[:], in0=p3_, in1=SC[:, :, :, 1], op=ALU.subtract), rd=[pr[3], sc_r], wr=[sc_r])
                    kb.op("act", lambda e: e.activation(out=SC[:, :, :, 4], in_=gtmp[:], func=AF.Exp), rd=[sc_r], wr=[sc_r])
                    kb.op("act", lambda e: e.activation(out=gtmp[:], in_=SC[:, :, :, 1], func=AF.Exp), rd=[sc_r], wr=[sc_r])
                    kb.op("dve", lambda e: e.tensor_tensor(out=SC[:, :, :, 3], in0=gtmp[:], in1=SC[:, :, :, 5], op=ALU.mult), rd=[sc_r], wr=[sc_r])
                    kb.op("dve", lambda e: e.tensor_scalar(out=SC[:, :, :, 6], in0=gtmp[:], scalar1=128.0 ** -0.5, scalar2=None, op0=ALU.mult),
                          rd=[sc_r], wr=[sc_r])
                    kb.op("act", lambda e: e.activation(out=gtmp[:], in_=SC[:, :, :, 5], func=AF.Ln), rd=[sc_r], wr=[sc_r])
                    kb.op("dve", lambda e: e.tensor_tensor(out=SC[:, :, :, 0], in0=gtmp[:], in1=SC[:, :, :, 1], op=ALU.add), rd=[sc_r], wr=[sc_r])

                    kb.barrier()
                    phg.close()
                    dump("SC", SC[:].rearrange("p t n k -> p (t n k)"), sc_r, F32)
                    qT = sb("a_qT", [128, S], BF16, ph)
                    kT = sb("a_kT", [128, S], BF16, ph)
                    qkvtm = sb("a_qkvtm", [128, NT, 3, 128], BF16, ph)
                    qkv_r = Res()
                    oT = sb("a_oT", [128, S], F32, ph)
                    o_r = Res()
                    for h in range(4):
                        with phase() as ph2:
                            xpad = sb("a_xpad", [128, S + 4], F32, ph2)
                            xp_r = Res()
                            acc = sb("a_acc", [128, min(1024, S)], F32, ph2)
                            acc_r = Res()
                            vT = sb("a_vT", [128, S], BF16, ph2)
                            sqb = sb("a_sqb", [128, 512], BF16, ph2)
                            rsb = sb("a_rsb", [128, 512], F32, ph2)
                            tmp_r = Res()
                            kb.op("pool", lambda e: e.memset(xpad[:, 0:2], 0.0), wr=[xp_r])
                            kb.op("pool", lambda e: e.memset(xpad[:, S + 2:S + 4], 0.0), wr=[xp_r])
                            kb.op("pool", lambda e: e.memset(oT[:], 0.0), wr=[o_r])
                            for xi, (nm, dst) in enumerate((("a_q", qT), ("a_k", kT), ("a_v", vT))):
                                wt, wt_r = load_win(l, OFF[nm] + h * 128, 128)
                                for b in range(NB):
                                    k = b % 2
                                    projF(pb[k][:], pr[k], wt, wt_r, 0, 128, b * 512, 512)
                                    kb.op("act", lambda e, b=b, k=k: e.activation(out=xpad[:, 2 + b * 512:2 + (b + 1) * 512], in_=pb[k][:],
                                                                                  func=AF.Copy), rd=[pr[k]], wr=[xp_r])
                                grp = xi * 4 + h
                                QW = min(1024, S)
                                for q0 in range(0, S, QW):
                                    kb.op("dve", lambda e, grp=grp, q0=q0: e.tensor_scalar(out=acc[:], in0=xpad[:, q0:q0 + QW], scalar1=convw[:, grp, 0:1],
                                                                                           scalar2=None, op0=ALU.mult), rd=[xp_r, par_r], wr=[acc_r])
                                    for j in range(1, 5):
                                        kb.op("dve", lambda e, grp=grp, j=j, q0=q0: e.scalar_tensor_tensor(
                                            out=acc[:], in0=xpad[:, q0 + j:q0 + j + QW], scalar=convw[:, grp, j:j + 1], in1=acc[:],
                                            op0=ALU.mult, op1=ALU.add), rd=[xp_r, par_r, acc_r], wr=[acc_r])
                                    kb.op("act", lambda e: e.activation(out=acc[:], in_=acc[:], func=AF.Silu), rd=[acc_r], wr=[acc_r])
                                    if xi < 2:
                                        for b_ in range(QW // 512):
                                            sl = slice(b_ * 512, (b_ + 1) * 512)
                                            osl_ = slice(q0 + b_ * 512, q0 + (b_ + 1) * 512)
                                            kb.op("act", lambda e, sl=sl: e.activation(out=sqb[:], in_=acc[:, sl], func=AF.Square), rd=[acc_r], wr=[tmp_r])
                                            kb.pe([lambda e: e.matmul(pb[2][:], lhsT=onesb[:], rhs=sqb[:], start=True, stop=True)],
                                                  rd=[tmp_r, c_r], wr=[pr[2]])
                                            kb.op("act", lambda e: e.activation(out=rsb[:], in_=pb[2][:], func=AF.Ln, bias=epsc[:, 0:1]),
                                                  rd=[pr[2], c_r], wr=[tmp_r])
                                            kb.op("act", lambda e: e.activation(out=rsb[:], in_=rsb[:], func=AF.Exp, scale=-0.5), rd=[tmp_r], wr=[tmp_r])
                                            kb.op("dve", lambda e, sl=sl, osl_=osl_, dst=dst: e.tensor_tensor(out=dst[:, osl_], in0=acc[:, sl], in1=rsb[:], op=ALU.mult),
                                                  rd=[acc_r, tmp_r], wr=[qkv_r])
                                    else:
                                        kb.op("dve", lambda e, dst=dst, q0=q0: e.tensor_copy(out=dst[:, q0:q0 + QW], in_=acc[:]), rd=[acc_r], wr=[qkv_r])
                            for t in range(NT):
                                k = t % 2
                                pT = pbf(3 + k)
                                ts_ = slice(t * 128, (t + 1) * 128)
                                kb.pe([(lambda e, xi=xi, src=src, pT=pT, ts_=ts_: e.transpose(out=pT[:, xi * 128:(xi + 1) * 128], in_=src[:, ts_],
                                                                                              identity=identb[:]))
                                       for xi, src in enumerate((qT, kT, vT))], rd=[qkv_r, c_r], wr=[pr[3 + k]])
                                kb.op("act", lambda e, t=t, pT=pT: e.activation(out=qkvtm[:, t, :, :],
                                                                                in_=pT[:, 0:384].rearrange("p (x n) -> p x n", x=3),
                                                                                func=AF.Copy), rd=[pr[3 + k]], wr=[qkv_r])
                        dump("qT", qT[:], qkv_r, BF16)
                        with phase() as ph2:
                            G = 2
                            GW = G * 128

                            def chain(d, BK):
                                bA, bB, bC, bD = BK
                                sfx = "_%d" % d
                                dgF = sb("a_dgF" + sfx, [128, G, 3, 128], F32, ph2)
                                dgB = sb("a_dgB" + sfx, [128, G, 4, 128], BF16, ph2)
                                dg_r = Res()
                                DJI = sb("a_DJI" + sfx, [128, G, 128], F32, ph2)
                                DIJ = sb("a_DIJ" + sfx, [128, G, 128], F32, ph2)
                                dd_r = Res()
                                aqk = sb("a_aqk" + sfx, [128, G, 128], BF16, ph2)
                                aqk_r = Res()
                                YP = [sb("a_YP%d" % k + sfx, [128, G, 256], F32, ph2) for k in range(2)]
                                ZZ = [sb("a_ZZ%d" % k + sfx, [128, G, 128], F32, ph2) for k in range(2)]
                                yz_r = [Res(), Res()]
                                TTb = sb("a_TTb" + sfx, [128, G, 128], BF16, ph2)
                                tt_r = Res()
                                scl = sb("a_scl" + sfx, [128, G, 4, 128], BF16, ph2)
                                scl_r = Res()
                                wTt = sb("a_wT" + sfx, [128, G, 128], BF16, ph2)
                                ut = sb("a_u" + sfx, [128, G, 128], F32, ph2)
                                wu_r = Res()
                                vnew = sb("a_vnew" + sfx, [128, 128], BF16, ph2)
                                vn_r = Res()
                                Sf = sb("a_Sf" + sfx, [128, 128], F32, ph2)
                                Sbf = sb("a_Sbf" + sfx, [128, 128], BF16, ph2)
                                S_r = Res()
                                Sf_r = Res()
                                n = d * 4 + h
                                negJI = C["negJI_f"] if d == 0 else C["negJI_b"]
                                negIJ = C["negIJ_f"] if d == 0 else C["negIJ_b"]
                                kb.op("pool", lambda e: e.memset(Sf[:], 0.0), wr=[Sf_r])
                                kb.op("pool", lambda e: e.memset(Sbf[:], 0.0), wr=[S_r])
                                batches = list(range(0, NT, G))
                                if d == 1:
                                    batches = batches[::-1]
                                for t0 in batches:
                                    kb.op("pool", lambda e, t0=t0: e.tensor_tensor(
                                        out=dgF[:], in0=C["identf"].unsqueeze(1).unsqueeze(1).to_broadcast([128, G, 3, 128]),
                                        in1=SC[:, t0:t0 + G, n, 0:3].unsqueeze(3).to_broadcast([128, G, 3, 128]), op=ALU.mult),
                                        rd=[sc_r, c_r], wr=[dg_r])
                                    kb.op("pool", lambda e, t0=t0: e.tensor_tensor(
                                        out=dgB[:], in0=C["identf"].unsqueeze(1).unsqueeze(1).to_broadcast([128, G, 4, 128]),
                                        in1=SC[:, t0:t0 + G, n, 3:7].unsqueeze(3).to_broadcast([128, G, 4, 128]), op=ALU.mult),
                                        rd=[sc_r, c_r], wr=[dg_r])
                                    fns = []
                                    for g in range(G):
                                        o0 = pb[bA][:, g * 128:(g + 1) * 128]
                                        o1 = pb[bB][:, g * 128:(g + 1) * 128]
                                        fns += [lambda e, g=g, o0=o0: e.matmul(o0, lhsT=C["onesf"], rhs=dgF[:, g, 1, :], start=True, stop=False),
                                                lambda e, g=g, o0=o0: e.matmul(o0, lhsT=dgF[:, g, 2, :], rhs=C["onesf"], start=False, stop=False),
                                                lambda e, g=g, o0=o0: e.matmul(o0, lhsT=C["identf"], rhs=negJI, start=False, stop=True),
                                                lambda e, g=g, o1=o1: e.matmul(o1, lhsT=dgF[:, g, 0, :], rhs=C["onesf"], start=True, stop=False),
                                                lambda e, g=g, o1=o1: e.matmul(o1, lhsT=C["onesf"], rhs=dgF[:, g, 2, :], start=False, stop=False),
                                                lambda e, g=g, o1=o1: e.matmul(o1, lhsT=C["identf"], rhs=negIJ, start=False, stop=True)]
                                    kb.pe(fns, rd=[dg_r, c_r], wr=[pr[bA], pr[bB]])
                                    fns = []
                                    for g in range(G):
                                        ts_ = slice((t0 + g) * 128, (t0 + g + 1) * 128)
                                        fns += [lambda e, g=g, ts_=ts_: e.matmul(pb[bC][:, g * 128:(g + 1) * 128], lhsT=kT[:, ts_], rhs=kT[:, ts_], start=True, stop=True),
                                                lambda e, g=g, ts_=ts_: e.matmul(pb[bD][:, g * 128:(g + 1) * 128], lhsT=kT[:, ts_], rhs=qT[:, ts_], start=True, stop=True)]
                                    kb.pe(fns, rd=[qkv_r], wr=[pr[bC], pr[bD]])
                                    kb.op("act", lambda e: e.activation(out=DJI[:].rearrange("p g n -> p (g n)"), in_=pb[bA][:, 0:GW], func=AF.Exp),
                                          rd=[pr[bA]], wr=[dd_r])
                                    kb.op("act", lambda e: e.activation(out=DIJ[:].rearrange("p g n -> p (g n)"), in_=pb[bB][:, 0:GW], func=AF.Exp),
                                          rd=[pr[bB]], wr=[dd_r])
                                    yield
                                    kb.op("dve", lambda e: e.scalar_tensor_tensor(out=ZZ[0][:].rearrange("p g n -> p (g n)"), in0=pb[bC][:, 0:GW], scalar=-1.0,
                                                                                  in1=DIJ[:].rearrange("p g n -> p (g n)"), op0=ALU.mult, op1=ALU.mult),
                                          rd=[pr[bC], dd_r], wr=[yz_r[0]])
                                    kb.op("dve", lambda e: e.scalar_tensor_tensor(out=aqk[:].rearrange("p g n -> p (g n)"), in0=pb[bD][:, 0:GW], scalar=128.0 ** -0.5,
                                                                                  in1=DJI[:].rearrange("p g n -> p (g n)"), op0=ALU.mult, op1=ALU.mult),
                                          rd=[pr[bD], dd_r], wr=[aqk_r])
                                    for g in range(G):
                                        bank = (bA, bB)[g % 2]
                                        t = t0 + g
                                        kb.pe([lambda e, g=g, bank=bank, t=t: e.matmul(pb[bank][:, 0:128], lhsT=dgB[:, g, 0, :], rhs=qkvtm[:, t, 1, :], start=True, stop=True),
                                               lambda e, g=g, bank=bank, t=t: e.matmul(pb[bank][:, 128:256], lhsT=dgB[:, g, 1, :], rhs=qkvtm[:, t, 1, :], start=True, stop=True),
                                               lambda e, g=g, bank=bank, t=t: e.matmul(pb[bank][:, 256:384], lhsT=dgB[:, g, 2, :], rhs=qkvtm[:, t, 2, :], start=True, stop=True),
                                               lambda e, g=g, bank=bank, t=t: e.matmul(pb[bank][:, 384:512], lhsT=qkvtm[:, t, 0, :], rhs=dgB[:, g, 3, :], start=True, stop=True)],
                                              rd=[dg_r, qkv_r], wr=[pr[bank]])
                                        kb.op("act", lambda e, g=g, bank=bank: e.activation(
                                            out=scl[:, g, :, :].rearrange("p x n -> p (x n)"), in_=pb[bank][:], func=AF.Copy),
                                            rd=[pr[bank]], wr=[scl_r])
                                    kb.pe([(lambda e, g=g: e.transpose(out=pb[bC][:, g * 128:(g + 1) * 128], in_=ZZ[0][:, g, :], identity=C["identf"]))
                                           for g in range(G)], rd=[yz_r[0], c_r], wr=[pr[bC]])
                                    yield
                                    pT3 = pb[bC][:, 0:GW].rearrange("p (g n) -> p g n", g=G)
                                    kb.op("act", lambda e, pT3=pT3: e.activation(out=YP[0][:, :, 0:128], in_=pT3, func=AF.Copy), rd=[pr[bC]], wr=[yz_r[0]])
                                    kb.op("dve", lambda e, pT3=pT3: e.tensor_tensor(out=YP[0][:, :, 128:256], in0=pT3,
                                                                                    in1=C["identf"].unsqueeze(1).to_broadcast([128, G, 128]), op=ALU.add),
                                          rd=[pr[bC], c_r], wr=[yz_r[0]])
                                    cur = 0
                                    for lev in range(7):
                                        nxt = 1 - cur
                                        fns = []
                                        for g in range(G):
                                            o_ = g * 256
                                            if lev == 0:
                                                fns.append(lambda e, g=g, o_=o_, cur=cur: e.matmul(
                                                    pb[bC][:, o_:o_ + 128], lhsT=ZZ[cur][:, g, :], rhs=YP[cur][:, g, 0:128], start=True, stop=True))
                                            elif lev < 6:
                                                fns.append(lambda e, g=g, o_=o_, cur=cur: e.matmul(
                                                    pb[bC][:, o_:o_ + 256], lhsT=ZZ[cur][:, g, :], rhs=YP[cur][:, g, :], start=True, stop=True))
                                            else:
                                                fns.append(lambda e, g=g, o_=o_, cur=cur: e.matmul(
                                                    pb[bC][:, o_ + 128:o_ + 256], lhsT=ZZ[cur][:, g, :], rhs=YP[cur][:, g, 128:256], start=True, stop=True))
                                            if lev < 6:
                                                fns.append(lambda e, g=g, cur=cur: e.matmul(
                                                    pb[bD][:, g * 128:(g + 1) * 128], lhsT=YP[cur][:, g, 0:128], rhs=ZZ[cur][:, g, :], start=True, stop=True))
                                        kb.pe(fns, rd=[yz_r[cur]], wr=[pr[bC], pr[bD]])
                                        yield
                                        src = pb[bC][:, 0:G * 256].rearrange("p (g n) -> p g n", g=G)
                                        if lev < 6:
                                            kb.op("act", lambda e, src=src, nxt=nxt: e.activation(out=YP[nxt][:, :, 0:128], in_=src[:, :, 0:128], func=AF.Copy),
                                                  rd=[pr[bC]], wr=[yz_r[nxt]])
                                        if lev == 0:
                                            kb.op("dve", lambda e, nxt=nxt, cur=cur: e.tensor_copy(out=YP[nxt][:, :, 128:256], in_=YP[cur][:, :, 128:256]),
                                                  rd=[yz_r[cur]], wr=[yz_r[nxt]])
                                        else:
                                            kb.op("dve", lambda e, src=src, nxt=nxt, cur=cur: e.tensor_tensor(
                                                out=YP[nxt][:, :, 128:256], in0=src[:, :, 128:256], in1=YP[cur][:, :, 128:256], op=ALU.add),
                                                rd=[pr[bC], yz_r[cur]], wr=[yz_r[nxt]])
                                        if lev < 6:
                                            kb.op("act", lambda e, nxt=nxt: e.activation(out=ZZ[nxt][:].rearrange("p g n -> p (g n)"), in_=pb[bD][:, 0:GW], func=AF.Copy),
                                                  rd=[pr[bD]], wr=[yz_r[nxt]])
                                        cur = nxt
                                    kb.op("act", lambda e, cur=cur: e.activation(out=TTb[:], in_=YP[cur][:, :, 128:256], func=AF.Copy), rd=[yz_r[cur]], wr=[tt_r])
                                    fns = []
                                    for g in range(G):
                                        fns += [lambda e, g=g: e.matmul(pb[bA][:, g * 128:(g + 1) * 128], lhsT=scl[:, g, 0, :], rhs=TTb[:, g, :], start=True, stop=True),
                                                lambda e, g=g: e.matmul(pb[bA][:, GW + g * 128:GW + (g + 1) * 128], lhsT=TTb[:, g, :], rhs=scl[:, g, 2, :], start=True, stop=True)]
                                    kb.pe(fns, rd=[scl_r, tt_r], wr=[pr[bA]])
                                    yield
                                    kb.op("act", lambda e: e.activation(out=wTt[:].rearrange("p g n -> p (g n)"), in_=pb[bA][:, 0:GW], func=AF.Copy), rd=[pr[bA]], wr=[wu_r])
                                    kb.op("dve", lambda e: e.tensor_copy(out=ut[:].rearrange("p g n -> p (g n)"), in_=pb[bA][:, GW:2 * GW]), rd=[pr[bA]], wr=[wu_r])
                                    for g in (range(G) if d == 0 else range(G - 1, -1, -1)):
                                        t = t0 + g
                                        kb.pe([lambda e, g=g: e.matmul(pb[bB][:, 0:128], lhsT=wTt[:, g, :], rhs=Sbf[:], start=True, stop=True)],
                                              rd=[wu_r, S_r], wr=[pr[bB]])
                                        yield
                                        kb.op("dve", lambda e, g=g: e.tensor_tensor(out=vnew[:], in0=ut[:, g, :], in1=pb[bB][:, 0:128], op=ALU.subtract),
                                              rd=[wu_r, pr[bB]], wr=[vn_r])
                                        kb.pe([lambda e, g=g: e.matmul(pb[bB][:, 128:256], lhsT=Sbf[:], rhs=scl[:, g, 3, :], start=True, stop=False),
                                               lambda e, g=g: e.matmul(pb[bB][:, 128:256], lhsT=vnew[:], rhs=aqk[:, g, :], start=False, stop=True),
                                               lambda e, g=g: e.matmul(pb[bB][:, 256:384], lhsT=scl[:, g, 1, :], rhs=vnew[:], start=True, stop=True)],
                                              rd=[S_r, scl_r, vn_r, aqk_r], wr=[pr[bB]])
                                        yield
                                        kb.op("dve", lambda e, t=t: e.scalar_tensor_tensor(out=Sbf[:], in0=Sf[:], scalar=SC[:, t, n, 7:8], in1=pb[bB][:, 256:384],
                                                                                           op0=ALU.mult, op1=ALU.add), rd=[Sf_r, sc_r, pr[bB], S_r], wr=[S_r])
                                        kb.op("dve", lambda e, t=t: e.scalar_tensor_tensor(out=Sf[:], in0=Sf[:], scalar=SC[:, t, n, 7:8], in1=pb[bB][:, 256:384],
                                                                                           op0=ALU.mult, op1=ALU.add), rd=[sc_r, pr[bB], Sf_r], wr=[Sf_r])
                                        osl = slice(t * 128, (t + 1) * 128)
                                        kb.op("dve", lambda e, osl=osl: e.tensor_tensor(out=oT[:, osl], in0=oT[:, osl], in1=pb[bB][:, 128:256], op=ALU.add),
                                              rd=[pr[bB], o_r], wr=[o_r])

                            gens = [chain(0, (0, 1, 2, 3)), chain(1, (4, 5, 6, 7))]
                            while gens:
                                for gen in list(gens):
                                    try:
                                        next(gen)
                                    except StopIteration:
                                        gens.remove(gen)
                        dump("oT", oT[:], o_r, F32)
                        with phase() as ph2:
                            tmp = dict(sq=sb("a_esq", [128, 512], BF16, ph2), rs=sb("a_ers", [128, 512], F32, ph2),
                                       zg=sb("a_ezg", [128, 512], F32, ph2), yb=sb("a_eyb", [128, 512], BF16, ph2))
                            tmp_r = Res()
                            wzt = sb("a_wz", [128, KC, 128], BF16, ph2)
                            wz_r = Res()
                            wt, wt_r = load_win(l, OFF["a_z"] + h * 128, 128)
                            kb.op("pool", lambda e, wt=wt: e.tensor_copy(out=wzt[:], in_=wt), rd=[wt_r], wr=[wz_r])
                            for b in range(NB):
                                epilogue(oT[:, b * 512:(b + 1) * 512], o_r, 0, OFF["a_z"] + h * 128, l,
                                         yT_d[0, h * 128:(h + 1) * 128, :], b, tmp, tmp_r, wzt[:], wz_r)
                kb.barrier()
                if dbg and s == 0 and l == dcur["dl"]:
                    kb.dma(dbg_d[:, :, :], yT_d[:, :, :], st_ds, rd=[yT_r], wr=[Res()])
                    kb.barrier()

                with phase() as ph:
                  if 'M' not in skip:
                    wbr = [sb("m_wbr%d" % k, [128, 4, D], BF16, ph) for k in range(3)]
                    wo = sb("m_wo", [128, KC, D], BF16, ph)
                    mw_r = Res()
                    ci = 0
                    for x in range(3):
                        for ch in range(2):
                            kb.dma(stg[:, 0:4, 0:512], w_br[x][l, :, ch * 512:(ch + 1) * 512].rearrange("(c p) n -> p c n", p=128), ld_ds, wr=[stg_r])
                            kb.op(("pool", "dve", "act")[ci % 3], lambda e, x=x, ch=ch, ci=ci: (
                                e.activation(out=wbr[x][:, :, ch * 512:(ch + 1) * 512], in_=stg[:, 0:4, 0:512], func=AF.Copy) if ci % 3 == 2
                                else e.tensor_copy(out=wbr[x][:, :, ch * 512:(ch + 1) * 512], in_=stg[:, 0:4, 0:512])), rd=[stg_r], wr=[mw_r])
                            ci += 1
                    for ch in range(2):
                        kb.dma(stg[:, 0:KC, 0:512], w_out[l, :, ch * 512:(ch + 1) * 512].rearrange("(c p) n -> p c n", p=128), ld_ds, wr=[stg_r])
                        kb.op(("pool", "dve", "act")[ci % 3], lambda e, ch=ch, ci=ci: (
                            e.activation(out=wo[:, :, ch * 512:(ch + 1) * 512], in_=stg[:, 0:KC, 0:512], func=AF.Copy) if ci % 3 == 2
                            else e.tensor_copy(out=wo[:, :, ch * 512:(ch + 1) * 512], in_=stg[:, 0:KC, 0:512])), rd=[stg_r], wr=[mw_r])
                        ci += 1
                    yt = [sb("m_yt%d" % k, [128, 3, 4, 128], BF16, ph) for k in range(2)]
                    gt = [sb("m_gt%d" % k, [128, 3 * D], BF16, ph) for k in range(2)]
                    xt = [sb("m_xt%d" % k, [128, D], F32, ph) for k in range(2)]
                    in_r = [Res(), Res()]
                    iny_r = [Res(), Res()]
                    ing_r = [Res(), Res()]
                    mg = [sb("m_mg%d" % k, [128, D], F32, ph) for k in range(2)]
                    mgb = [sb("m_mgb%d" % k, [128, D], BF16, ph) for k in range(2)]
                    mtmp = [sb("m_tmp%d" % k, [128, 2, 512], F32, ph) for k in range(2)]
                    mg_r = [Res(), Res()]
                    mt_r = [[Res(), Res()], [Res(), Res()]]
                    mT = [sb("m_mT%d" % k, [128, KC, 128], BF16, ph) for k in range(2)]
                    mT_r = [Res(), Res()]
                    ot = [sb("m_ot%d" % k, [128, D], F32, ph) for k in range(2)]
                    ot_r = [Res(), Res()]
                    x_new = Res()

                    def m_s1(t):
                        k = t % 2
                        ts_ = slice(t * 128, (t + 1) * 128)
                        kb.dma(yt[k][:].rearrange("p x c n -> p (x c) n"),
                               yT_d[:, :, ts_].rearrange("x (c p) n -> p (x c) n", p=128), ml_ds[k][0], rd=[yT_r], wr=[iny_r[k]])
                        kb.dma(gt[k][:], gate_d[ts_, :], ml_ds[k][1], rd=[gate_r], wr=[ing_r[k]])
                        kb.dma(xt[k][:], xsrc[ts_, :], ml_ds[k][2], rd=[xr[0]], wr=[in_r[k]])
                        for ch in range(2):
                            cs_ = slice(ch * 512, (ch + 1) * 512)
                            for x in range(3):
                                bank = (x + ch) % 3
                                kb.pe([(lambda e, c=c, x=x, bank=bank, cs_=cs_: e.matmul(pb[bank][:], lhsT=yt[k][:, x, c, :], rhs=wbr[x][:, c, cs_],
                                                                                        start=(c == 0), stop=(c == 3))) for c in range(4)],
                                      rd=[iny_r[k], mw_r], wr=[pr[bank]])
                                gsl = gt[k][:, x * D + ch * 512:x * D + (ch + 1) * 512]
                                if x == 0:
                                    kb.op("dve", lambda e, bank=bank, gsl=gsl, cs_=cs_: e.tensor_tensor(out=mg[k][:, cs_], in0=pb[bank][:], in1=gsl, op=ALU.mult),
                                          rd=[pr[bank], ing_r[k]], wr=[mg_r[k]])
                                else:
                                    kb.op("dve", lambda e, bank=bank, gsl=gsl, x=x: e.tensor_tensor(out=mtmp[k][:, x - 1, :], in0=pb[bank][:], in1=gsl, op=ALU.mult),
                                          rd=[pr[bank], ing_r[k]], wr=[mt_r[k][x - 1]])
                                    kb.op("pool", lambda e, cs_=cs_, x=x: e.tensor_tensor(out=mg[k][:, cs_], in0=mg[k][:, cs_], in1=mtmp[k][:, x - 1, :], op=ALU.add),
                                          rd=[mg_r[k], mt_r[k][x - 1]], wr=[mg_r[k]])
                        kb.op("act", lambda e: e.activation(out=mgb[k][:], in_=mg[k][:], func=AF.Copy), rd=[mg_r[k]], wr=[mg_r[k]])

                    def m_s2(t):
                        k = t % 2
                        ts_ = slice(t * 128, (t + 1) * 128)
                        pT = pbf(3)
                        kb.pe([(lambda e, c=c: e.transpose(out=pT[:, c * 128:(c + 1) * 128], in_=mgb[k][:, c * 128:(c + 1) * 128], identity=identb[:]))
                               for c in range(KC)], rd=[mg_r[k], c_r], wr=[pr[3]])
                        kb.op("act", lambda e: e.activation(out=mT[k][:].rearrange("p c n -> p (c n)"), in_=pT, func=AF.Copy), rd=[pr[3]], wr=[mT_r[k]])
                        for ch in range(2):
                            cs_ = slice(ch * 512, (ch + 1) * 512)
                            bank = 4 + ch
                            kb.pe([(lambda e, c=c, bank=bank, cs_=cs_: e.matmul(pb[bank][:], lhsT=mT[k][:, c, :], rhs=wo[:, c, cs_], start=(c == 0), stop=(c == KC - 1)))
                                   for c in range(KC)], rd=[mT_r[k], mw_r], wr=[pr[bank]])
                            kb.op("dve", lambda e, bank=bank, cs_=cs_: e.tensor_tensor(out=ot[k][:, cs_], in0=pb[bank][:], in1=xt[k][:, cs_], op=ALU.add),
                                  rd=[pr[bank], in_r[k]], wr=[ot_r[k]])
                        kb.dma(xdst[ts_, :], ot[k][:], st2_ds[k], rd=[ot_r[k]], wr=[x_new])

                    m_s1(0)
                    for t in range(NT):
                        if t + 1 < NT:
                            m_s1(t + 1)
                        m_s2(t)
                    xr[0] = x_new
                kb.barrier()
        kb.barrier()
    print("instructions:", kb.nins, flush=True)
    return nc


_CACHE = {}


def kernel(**inputs):
    S = 4096
    NSEQ = 2
    DEPTH = 4
    NCORE = 8
    key = (S, NSEQ, DEPTH)
    if key not in _CACHE:
        _CACHE[key] = build(S, NSEQ, DEPTH)
    nc = _CACHE[key]
    xp = np.asarray(inputs["x_prompt"], dtype=np.float32)
    xs = np.asarray(inputs["x_sample"], dtype=np.float32)
    cf, rope, sm = host_consts(S)
    shared = {
        "norm_g": inputs["norm_g"], "w_in": inputs["w_in"], "conv_w": inputs["conv_w"],
        "a_log": np.asarray(inputs["a_log"]).reshape(DEPTH, 8), "dt_bias": np.asarray(inputs["dt_bias"]).reshape(DEPTH, 8),
        "gdn_norm_g": inputs["gdn_norm_g"], "hgrn_lb_logits": inputs["hgrn_lb_logits"], "hgrn_norm_g": inputs["hgrn_norm_g"],
        "q_norm_g": np.asarray(inputs["q_norm_g"]).reshape(DEPTH, 128), "k_norm_g": np.asarray(inputs["k_norm_g"]).reshape(DEPTH, 128),
        "diff_lambda": np.asarray(inputs["diff_lambda"]).reshape(DEPTH, 256), "subln_g": inputs["subln_g"],
        "w_br_a": inputs["w_br_a"], "w_br_b": inputs["w_br_b"], "w_br_c": inputs["w_br_c"], "w_out": inputs["w_out"],
        "cst_f": cf, "cst_rope": rope, "cst_smask": sm,
    }
    shared = {k: np.ascontiguousarray(np.asarray(v, dtype=np.float32)) for k, v in shared.items()}
    in_maps = []
    for c in range(NCORE):
        xin = np.ascontiguousarray(np.stack([xp[c], xs[c % 4]], axis=0))
        m = dict(shared)
        m["xin"] = xin
        in_maps.append(m)
    res = run_bass_kernel_spmd(nc, in_maps, core_ids=list(range(NCORE)))
    y_prompt = np.stack([np.asarray(res.results[c]["yout"][0]) for c in range(NCORE)], axis=0).astype(np.float32)
    y_sample = np.stack([np.asarray(res.results[c]["yout"][1]) for c in range(4)], axis=0).astype(np.float32)
    return (y_prompt, y_sample)
```
